# Optimizing a Trainium2 kernel written in Bass

```python
import math
import jax
import jax.numpy as jnp
from jax import lax
import numpy as np

D_MODEL = 1024
BATCH = 16
SEQ = 2048
DEPTH = 2

GRID_W = 64
CTX_LEN = 256
HEAD_DIM = 64
Q_BLOCK = 128
D_FF = 2816
N_MOD = 9
ROPE_BASE = 10000.0
EPS = 1e-6
NEG = -1e30
A_HEADS = 8
A_KV_HEADS = 2
A_WINDOW = 128
B_HEADS = 4
B_V_DIM = 2 * HEAD_DIM
C_HEADS = 8
C_Q_RANK = 768
C_KV_RANK = 256
C_NOPE = 64
C_ROPE = 32
C_V = 64
D_HEADS = 8
D_WIN_ROWS = 8
D_WIN_COLS = 16
N_EVEN = (DEPTH + 1) // 2
N_ODD = DEPTH // 2
AB_SPLITS = (A_HEADS * HEAD_DIM, B_HEADS * 2 * HEAD_DIM, A_KV_HEADS * HEAD_DIM, A_KV_HEADS * HEAD_DIM, B_HEADS * 2 * HEAD_DIM, B_HEADS * B_V_DIM)
AB_Q_WIDTH = AB_SPLITS[0] + AB_SPLITS[1]
AB_IN = sum(AB_SPLITS)
AB_OUT = A_HEADS * HEAD_DIM + B_HEADS * B_V_DIM
CD_SPLITS = (C_Q_RANK, D_HEADS * HEAD_DIM, C_KV_RANK + C_ROPE, D_HEADS * HEAD_DIM, D_HEADS * HEAD_DIM)
CD_Q_WIDTH = CD_SPLITS[0] + CD_SPLITS[1]
CD_IN = sum(CD_SPLITS)
CD_OUT = C_HEADS * C_V + D_HEADS * HEAD_DIM

kernel_name = "hybrid_dit_window_diff_mla_natten_macaron"


def split_cols(t, sizes):
    cuts = [int(v) for v in np.cumsum(sizes)[:-1]]
    return jnp.split(t, cuts, axis=-1)


def rms_norm(x, gain=None):
    xf = x.astype(jnp.float32)
    y = xf * lax.rsqrt(jnp.mean(xf * xf, axis=-1, keepdims=True) + EPS)
    if gain is not None:
        y = y * gain.astype(jnp.float32)
    return y.astype(x.dtype)


def modulate(h, shift, scale):
    return h * (1.0 + scale) + shift


def swiglu(h, w_gate, w_up, w_down):
    return (jax.nn.silu(h @ w_gate) * (h @ w_up)) @ w_down


def axial_rope_tables(n, rot_dim):
    nf = rot_dim // 4
    inv = ROPE_BASE ** (-jnp.arange(nf, dtype=jnp.float32) / nf)
    t = jnp.arange(n)
    row = (t // GRID_W).astype(jnp.float32)
    col = (t % GRID_W).astype(jnp.float32)
    ang = jnp.concatenate([row[:, None] * inv, col[:, None] * inv], axis=-1)
    return jnp.cos(ang), jnp.sin(ang)


def apply_rope2d(x, cos, sin):
    nf = x.shape[-1] // 4
    xf = x.astype(jnp.float32).reshape(x.shape[:-1] + (2, 2, nf))
    x1, x2 = xf[..., 0, :], xf[..., 1, :]
    c = cos.reshape(cos.shape[0], 1, 2, nf)
    s = sin.reshape(sin.shape[0], 1, 2, nf)
    out = jnp.stack([x1 * c - x2 * s, x2 * c + x1 * s], axis=-2)
    return out.reshape(x.shape).astype(x.dtype)


def joint_softmax(scores, sink=None):
    m = scores[0].max(axis=-1, keepdims=True)
    for s in scores[1:]:
        m = jnp.maximum(m, s.max(axis=-1, keepdims=True))
    if sink is not None:
        m = jnp.maximum(m, sink)
    es = [jnp.exp(s - m) for s in scores]
    denom = es[0].sum(axis=-1, keepdims=True)
    for e in es[1:]:
        denom = denom + e.sum(axis=-1, keepdims=True)
    if sink is not None:
        denom = denom + jnp.exp(sink - m)
    return [e / denom for e in es]


def sweep_blocks(fn, q, block):
    B, S = q.shape[0], q.shape[1]
    nb = S // block
    qb = jnp.moveaxis(q.reshape((B, nb, block) + q.shape[2:]), 1, 0)
    out = lax.map(lambda a: fn(a[0], a[1]), (qb, jnp.arange(nb)))
    return jnp.moveaxis(out, 0, 1).reshape((B, S) + out.shape[3:])


def gqa_attend(q, keyvals, sink=None):
    B, Q, H, d = q.shape
    kvh = keyvals[0][0].shape[2]
    g = H // kvh
    qg = q.reshape(B, Q, kvh, g, d)
    scores = []
    for k, _, mask in keyvals:
        s = jnp.einsum('bqhgd,bkhd->bhgqk', qg, k).astype(jnp.float32) * (d ** -0.5)
        if mask is not None:
            s = jnp.where(mask, s, NEG)
        scores.append(s)
    sink_b = None if sink is None else sink.astype(jnp.float32).reshape(kvh, g, 1, 1)
    probs = joint_softmax(scores, sink_b)
    out = None
    for p, (_, v, _) in zip(probs, keyvals):
        o = jnp.einsum('bhgqk,bkhd->bqhgd', p.astype(v.dtype), v)
        out = o if out is None else out + o
    return out.reshape(B, Q, H, d)


def windowed_sink_attention(q, k, v, k_ctx, v_ctx, sink):
    S = q.shape[1]
    span = Q_BLOCK + 2 * A_WINDOW
    pad = ((0, 0), (A_WINDOW, A_WINDOW), (0, 0), (0, 0))
    k_pad = jnp.pad(k, pad)
    v_pad = jnp.pad(v, pad)

    def block(q_blk, n):
        start = n * Q_BLOCK
        k_blk = lax.dynamic_slice_in_dim(k_pad, start, span, axis=1)
        v_blk = lax.dynamic_slice_in_dim(v_pad, start, span, axis=1)
        q_pos = start + jnp.arange(Q_BLOCK)
        k_pos = start - A_WINDOW + jnp.arange(span)
        mask = ((jnp.abs(q_pos[:, None] - k_pos[None, :]) <= A_WINDOW)
                & (k_pos >= 0)[None, :] & (k_pos < S)[None, :])
        return gqa_attend(q_blk, [(k_blk, v_blk, mask), (k_ctx, v_ctx, None)], sink)

    return sweep_blocks(block, q, Q_BLOCK)


def diff_lambda(lq1, lk1, lq2, lk2, lam_init):
    f = jnp.float32
    return (jnp.exp(jnp.sum(lq1.astype(f) * lk1.astype(f)))
            - jnp.exp(jnp.sum(lq2.astype(f) * lk2.astype(f))) + lam_init)


def diff_qk(t, gain, rope):
    B, n, _ = t.shape
    t = rms_norm(t.reshape(B, n, 2 * B_HEADS, HEAD_DIM), gain)
    if rope is not None:
        t = apply_rope2d(t, *rope)
    return t.reshape(B, n, B_HEADS, 2, HEAD_DIM)


def diff_attend(q, k, v, lam):
    s = jnp.einsum('bqhmd,bkhmd->bhmqk', q, k).astype(jnp.float32) * (HEAD_DIM ** -0.5)
    p = jax.nn.softmax(s, axis=-1)
    w = p[:, :, 0] - lam * p[:, :, 1]
    return jnp.einsum('bhqk,bkhe->bqhe', w.astype(v.dtype), v)


def mla_queries(dq, q_a_norm, w_uq, qn_nope, qn_rope, rope):
    B, n, _ = dq.shape
    q = (rms_norm(dq, q_a_norm) @ w_uq).reshape(B, n, C_HEADS, C_NOPE + C_ROPE)
    q_nope = rms_norm(q[..., :C_NOPE], qn_nope)
    q_rope = rms_norm(q[..., C_NOPE:], qn_rope)
    if rope is not None:
        q_rope = apply_rope2d(q_rope, *rope)
    return jnp.concatenate([q_nope, q_rope], axis=-1)


def mla_keys_values(dkv, kv_a_norm, w_ukv, kn_nope, kn_rope, rope):
    B, n, _ = dkv.shape
    c_kv, k_rope = dkv[..., :C_KV_RANK], dkv[..., C_KV_RANK:]
    kv = (rms_norm(c_kv, kv_a_norm) @ w_ukv).reshape(B, n, C_HEADS, C_NOPE + C_V)
    k_nope = rms_norm(kv[..., :C_NOPE], kn_nope)
    k_rope = rms_norm(k_rope, kn_rope)[:, :, None, :]
    if rope is not None:
        k_rope = apply_rope2d(k_rope, *rope)
    return k_nope, k_rope[:, :, 0, :], kv[..., C_NOPE:]


def mla_attend(q, k_nope, k_rope, v):
    q_nope, q_rope = q[..., :C_NOPE], q[..., C_NOPE:]
    s = (jnp.einsum('bqhd,bkhd->bhqk', q_nope, k_nope)
         + jnp.einsum('bqhr,bkr->bhqk', q_rope, k_rope)).astype(jnp.float32) * ((C_NOPE + C_ROPE) ** -0.5)
    p = jax.nn.softmax(s, axis=-1)
    return jnp.einsum('bhqk,bkhd->bqhd', p.astype(v.dtype), v)


def neighbourhood_attention(q, k, v, k_ctx, v_ctx, rpb):
    B, S, H, d = q.shape
    rows = S // GRID_W
    kh = min(D_WIN_ROWS, rows)
    kw = D_WIN_COLS
    k_grid = k.reshape(B, rows, GRID_W, H, d)
    v_grid = v.reshape(B, rows, GRID_W, H, d)
    cols = jnp.arange(GRID_W)
    col_start = jnp.clip(cols - kw // 2, 0, GRID_W - kw)
    col_mask = (cols[None, :] >= col_start[:, None]) & (cols[None, :] < col_start[:, None] + kw)
    dc_idx = jnp.clip(cols[None, :] - cols[:, None], -(kw - 1), kw - 1) + (D_WIN_COLS - 1)
    scale = d ** -0.5
    rpb_f = rpb.astype(jnp.float32)

    def row_block(q_row, r):
        row_start = jnp.clip(r - kh // 2, 0, rows - kh)
        k_band = lax.dynamic_slice_in_dim(k_grid, row_start, kh, axis=1)
        v_band = lax.dynamic_slice_in_dim(v_grid, row_start, kh, axis=1)
        dr_idx = row_start + jnp.arange(kh) - r + (D_WIN_ROWS - 1)
        bias = rpb_f[:, dr_idx[None, :, None], dc_idx[:, None, :]]
        s = jnp.einsum('bqhd,bjkhd->bhqjk', q_row, k_band).astype(jnp.float32) * scale + bias
        s = jnp.where(col_mask[:, None, :], s, NEG).reshape(B, H, GRID_W, kh * GRID_W)
        s_ctx = jnp.einsum('bqhd,bchd->bhqc', q_row, k_ctx).astype(jnp.float32) * scale
        p, p_ctx = joint_softmax([s, s_ctx])
        p = p.reshape(B, H, GRID_W, kh, GRID_W).astype(v.dtype)
        return (jnp.einsum('bhqjk,bjkhd->bqhd', p, v_band)
                + jnp.einsum('bhqc,bchd->bqhd', p_ctx.astype(v.dtype), v_ctx))

    return sweep_blocks(row_block, q, GRID_W)


def mix_window_diff(h, hc, w_in, w_out, a_q_norm, a_k_norm, a_sink, b_q_norm, b_k_norm,
                    b_lq1, b_lk1, b_lq2, b_lk2, b_sub_norm, lam_init, rope, need_ctx):
    B, S, _ = h.shape
    L = hc.shape[1]
    aq, bq, ak, av, bk, bv = split_cols(h @ w_in, AB_SPLITS)
    ak_c, av_c, bk_c, bv_c = split_cols(hc @ w_in[:, AB_Q_WIDTH:], AB_SPLITS[2:])
    aq = apply_rope2d(rms_norm(aq.reshape(B, S, A_HEADS, HEAD_DIM), a_q_norm), *rope)
    ak = apply_rope2d(rms_norm(ak.reshape(B, S, A_KV_HEADS, HEAD_DIM), a_k_norm), *rope)
    av = av.reshape(B, S, A_KV_HEADS, HEAD_DIM)
    ak_c = rms_norm(ak_c.reshape(B, L, A_KV_HEADS, HEAD_DIM), a_k_norm)
    av_c = av_c.reshape(B, L, A_KV_HEADS, HEAD_DIM)
    y_a = windowed_sink_attention(aq, ak, av, ak_c, av_c, a_sink)
    lam = diff_lambda(b_lq1, b_lk1, b_lq2, b_lk2, lam_init)
    bq = diff_qk(bq, b_q_norm, rope)
    bk = diff_qk(bk, b_k_norm, rope)
    bv = bv.reshape(B, S, B_HEADS, B_V_DIM)
    bk_c = diff_qk(bk_c, b_k_norm, None)
    bv_c = bv_c.reshape(B, L, B_HEADS, B_V_DIM)
    k_all = jnp.concatenate([bk_c, bk], axis=1)
    v_all = jnp.concatenate([bv_c, bv], axis=1)
    y_b = sweep_blocks(lambda q_blk, n: diff_attend(q_blk, k_all, v_all, lam), bq, Q_BLOCK)
    y_b = rms_norm(y_b, b_sub_norm) * (1.0 - lam_init)
    y = jnp.concatenate([y_a.reshape(B, S, -1), y_b.reshape(B, S, -1)], axis=-1) @ w_out
    if not need_ctx:
        return y, None
    aq_c, bq_c = split_cols(hc @ w_in[:, :AB_Q_WIDTH], AB_SPLITS[:2])
    aq_c = rms_norm(aq_c.reshape(B, L, A_HEADS, HEAD_DIM), a_q_norm)
    y_a_c = gqa_attend(aq_c, [(ak_c, av_c, None)], a_sink)
    y_b_c = rms_norm(diff_attend(diff_qk(bq_c, b_q_norm, None), bk_c, bv_c, lam), b_sub_norm) * (1.0 - lam_init)
    y_c = jnp.concatenate([y_a_c.reshape(B, L, -1), y_b_c.reshape(B, L, -1)], axis=-1) @ w_out
    return y, y_c


def mix_mla_neighbourhood(h, hc, w_in, w_out, c_q_a_norm, c_kv_a_norm, c_w_uq, c_w_ukv,
                          c_q_nope_norm, c_q_rope_norm, c_k_nope_norm, c_k_rope_norm,
                          d_q_norm, d_k_norm, d_rpb, rope, need_ctx):
    B, S, _ = h.shape
    L = hc.shape[1]
    cq, dq, ckv, dk, dv = split_cols(h @ w_in, CD_SPLITS)
    ckv_c, dk_c, dv_c = split_cols(hc @ w_in[:, CD_Q_WIDTH:], CD_SPLITS[2:])
    q = mla_queries(cq, c_q_a_norm, c_w_uq, c_q_nope_norm, c_q_rope_norm, rope)
    kn, kr, v = mla_keys_values(ckv, c_kv_a_norm, c_w_ukv, c_k_nope_norm, c_k_rope_norm, rope)
    kn_c, kr_c, v_c = mla_keys_values(ckv_c, c_kv_a_norm, c_w_ukv, c_k_nope_norm, c_k_rope_norm, None)
    kn_all = jnp.concatenate([kn_c, kn], axis=1)
    kr_all = jnp.concatenate([kr_c, kr], axis=1)
    v_all = jnp.concatenate([v_c, v], axis=1)
    y_mla = sweep_blocks(lambda q_blk, n: mla_attend(q_blk, kn_all, kr_all, v_all), q, Q_BLOCK)
    dq = rms_norm(dq.reshape(B, S, D_HEADS, HEAD_DIM), d_q_norm)
    dk = rms_norm(dk.reshape(B, S, D_HEADS, HEAD_DIM), d_k_norm)
    dv = dv.reshape(B, S, D_HEADS, HEAD_DIM)
    dk_c = rms_norm(dk_c.reshape(B, L, D_HEADS, HEAD_DIM), d_k_norm)
    dv_c = dv_c.reshape(B, L, D_HEADS, HEAD_DIM)
    y_nat = neighbourhood_attention(dq, dk, dv, dk_c, dv_c, d_rpb)
    y = jnp.concatenate([y_mla.reshape(B, S, -1), y_nat.reshape(B, S, -1)], axis=-1) @ w_out
    if not need_ctx:
        return y, None
    cq_c, dq_c = split_cols(hc @ w_in[:, :CD_Q_WIDTH], CD_SPLITS[:2])
    y_mla_c = mla_attend(mla_queries(cq_c, c_q_a_norm, c_w_uq, c_q_nope_norm, c_q_rope_norm, None), kn_c, kr_c, v_c)
    y_nat_c = gqa_attend(rms_norm(dq_c.reshape(B, L, D_HEADS, HEAD_DIM), d_q_norm), [(dk_c, dv_c, None)])
    y_c = jnp.concatenate([y_mla_c.reshape(B, L, -1), y_nat_c.reshape(B, L, -1)], axis=-1) @ w_out
    return y, y_c


def setup_inputs(seed: int = 0) -> dict:
    key = jax.random.key(seed)
    ks = jax.random.split(key, 37)
    f32 = jnp.float32

    def nrm(i, shape, scale):
        return scale * jax.random.normal(ks[i], shape, f32)

    def gain(i, shape):
        return 1.0 + 0.05 * jax.random.normal(ks[i], shape, f32)

    D, F, NE, NO = D_MODEL, D_FF, N_EVEN, N_ODD
    return {
        "x": nrm(0, (BATCH, SEQ, D), 1.0),
        "c": nrm(1, (BATCH, D), 1.0),
        "ctx": nrm(2, (BATCH, CTX_LEN, D), 1.0),
        "c_ctx": nrm(3, (D,), 1.0),
        "w_mod": nrm(4, (DEPTH, D, N_MOD * D), 0.5 * D ** -0.5),
        "b_mod": nrm(5, (DEPTH, N_MOD * D), 0.02),
        "ffn1_w_gate": nrm(6, (DEPTH, D, F), D ** -0.5),
        "ffn1_w_up": nrm(7, (DEPTH, D, F), D ** -0.5),
        "ffn1_w_down": nrm(8, (DEPTH, F, D), F ** -0.5),
        "ffn2_w_gate": nrm(9, (DEPTH, D, F), D ** -0.5),
        "ffn2_w_up": nrm(10, (DEPTH, D, F), D ** -0.5),
        "ffn2_w_down": nrm(11, (DEPTH, F, D), F ** -0.5),
        "ab_w_in": nrm(12, (NE, D, AB_IN), D ** -0.5),
        "ab_w_out": nrm(13, (NE, AB_OUT, D), AB_OUT ** -0.5),
        "a_q_norm": gain(14, (NE, HEAD_DIM)),
        "a_k_norm": gain(15, (NE, HEAD_DIM)),
        "a_sink": nrm(16, (NE, A_HEADS), 0.5),
        "b_q_norm": gain(17, (NE, HEAD_DIM)),
        "b_k_norm": gain(18, (NE, HEAD_DIM)),
        "b_lambda_q1": nrm(19, (NE, HEAD_DIM), 0.1),
        "b_lambda_k1": nrm(20, (NE, HEAD_DIM), 0.1),
        "b_lambda_q2": nrm(21, (NE, HEAD_DIM), 0.1),
        "b_lambda_k2": nrm(22, (NE, HEAD_DIM), 0.1),
        "b_sub_norm": gain(23, (NE, B_V_DIM)),
        "cd_w_in": nrm(24, (NO, D, CD_IN), D ** -0.5),
        "cd_w_out": nrm(25, (NO, CD_OUT, D), CD_OUT ** -0.5),
        "c_q_a_norm": gain(26, (NO, C_Q_RANK)),
        "c_kv_a_norm": gain(27, (NO, C_KV_RANK)),
        "c_w_uq": nrm(28, (NO, C_Q_RANK, C_HEADS * (C_NOPE + C_ROPE)), C_Q_RANK ** -0.5),
        "c_w_ukv": nrm(29, (NO, C_KV_RANK, C_HEADS * (C_NOPE + C_V)), C_KV_RANK ** -0.5),
        "c_q_nope_norm": gain(30, (NO, C_NOPE)),
        "c_q_rope_norm": gain(31, (NO, C_ROPE)),
        "c_k_nope_norm": gain(32, (NO, C_NOPE)),
        "c_k_rope_norm": gain(33, (NO, C_ROPE)),
        "d_q_norm": gain(34, (NO, HEAD_DIM)),
        "d_k_norm": gain(35, (NO, HEAD_DIM)),
        "d_rpb": nrm(36, (NO, D_HEADS, 2 * D_WIN_ROWS - 1, 2 * D_WIN_COLS - 1), 0.5),
    }


def reference(x, c, ctx, c_ctx, w_mod, b_mod,
              ffn1_w_gate, ffn1_w_up, ffn1_w_down, ffn2_w_gate, ffn2_w_up, ffn2_w_down,
              ab_w_in, ab_w_out, a_q_norm, a_k_norm, a_sink, b_q_norm, b_k_norm,
              b_lambda_q1, b_lambda_k1, b_lambda_q2, b_lambda_k2, b_sub_norm,
              cd_w_in, cd_w_out, c_q_a_norm, c_kv_a_norm, c_w_uq, c_w_ukv,
              c_q_nope_norm, c_q_rope_norm, c_k_nope_norm, c_k_rope_norm,
              d_q_norm, d_k_norm, d_rpb):
    S = x.shape[1]
    rope_head = axial_rope_tables(S, HEAD_DIM)
    rope_mla = axial_rope_tables(S, C_ROPE)
    c_act = jax.nn.silu(c)
    c_ctx_act = jax.nn.silu(c_ctx)
    xc = ctx
    for l in range(DEPTH):
        need_ctx = l < DEPTH - 1
        mx = jnp.split((c_act @ w_mod[l] + b_mod[l])[:, None, :], N_MOD, axis=-1)
        mc = jnp.split((c_ctx_act @ w_mod[l] + b_mod[l])[None, None, :], N_MOD, axis=-1)
        ffn1 = (ffn1_w_gate[l], ffn1_w_up[l], ffn1_w_down[l])
        ffn2 = (ffn2_w_gate[l], ffn2_w_up[l], ffn2_w_down[l])
        x = x + 0.5 * mx[2] * swiglu(modulate(rms_norm(x), mx[0], mx[1]), *ffn1)
        xc = xc + 0.5 * mc[2] * swiglu(modulate(rms_norm(xc), mc[0], mc[1]), *ffn1)
        h = modulate(rms_norm(x), mx[3], mx[4])
        hc = modulate(rms_norm(xc), mc[3], mc[4])
        i = l // 2
        if l % 2 == 0:
            lam_init = 0.8 - 0.6 * math.exp(-0.3 * l)
            y, y_c = mix_window_diff(h, hc, ab_w_in[i], ab_w_out[i], a_q_norm[i], a_k_norm[i], a_sink[i],
                                     b_q_norm[i], b_k_norm[i], b_lambda_q1[i], b_lambda_k1[i],
                                     b_lambda_q2[i], b_lambda_k2[i], b_sub_norm[i], lam_init,
                                     rope_head, need_ctx)
        else:
            y, y_c = mix_mla_neighbourhood(h, hc, cd_w_in[i], cd_w_out[i], c_q_a_norm[i], c_kv_a_norm[i],
                                           c_w_uq[i], c_w_ukv[i], c_q_nope_norm[i], c_q_rope_norm[i],
                                           c_k_nope_norm[i], c_k_rope_norm[i], d_q_norm[i], d_k_norm[i],
                                           d_rpb[i], rope_mla, need_ctx)
        x = x + mx[5] * y
        x = x + 0.5 * mx[8] * swiglu(modulate(rms_norm(x), mx[6], mx[7]), *ffn2)
        if need_ctx:
            xc = xc + mc[5] * y_c
            xc = xc + 0.5 * mc[8] * swiglu(modulate(rms_norm(xc), mc[6], mc[7]), *ffn2)
    return x
```

```python
import contextlib
import math
import os
import numpy as np
import concourse.bass as bass
import concourse.mybir as mybir
from concourse.bass_utils import run_bass_kernel_spmd

F32 = mybir.dt.float32
BF16 = mybir.dt.bfloat16
AF = mybir.ActivationFunctionType
ALU = mybir.AluOpType
EPS = 1e-6


class Op:
    __slots__ = ("eng", "fn", "waits", "dwaits", "inc", "semval", "dma")

    def __init__(self, eng, fn):
        self.eng = eng
        self.fn = fn
        self.waits = []
        self.dwaits = []
        self.inc = False
        self.semval = None
        self.dma = None


class Prog:
    ENGS = ("pe", "act", "dve", "pool", "sp")

    def __init__(self, nc):
        self.nc = nc
        self.ops = {e: [] for e in self.ENGS}
        self.last_write = {}
        self.readers = {}
        self.dma_sem_of = {}
        self.dma_sem_cnt = []
        self.pending = {}

    def dma_sem(self, name):
        if name not in self.dma_sem_of:
            self.dma_sem_of[name] = len(self.dma_sem_cnt)
            self.dma_sem_cnt.append(0)
        return self.dma_sem_of[name]

    def _dep(self, op, prod):
        if prod is None or prod is op:
            return
        if prod.dma is not None:
            op.dwaits.append(prod.dma)
        elif prod.eng != op.eng or prod.eng != "pe":
            op.waits.append(prod)
            prod.inc = True

    def barrier(self):
        snap = []
        for e in self.ENGS:
            if self.ops[e]:
                o = self.ops[e][-1]
                if o.dma is None:
                    o.inc = True
                    snap.append(o)
        dsnap = [(i, c) for i, c in enumerate(self.dma_sem_cnt) if c > 0]
        self.pending = {e: (snap, dsnap) for e in self.ENGS}

    def add(self, eng, fn, reads=(), writes=(), dma_sem=None):
        op = Op(eng, fn)
        pend = self.pending.pop(eng, None)
        if pend is not None:
            for o in pend[0]:
                if o.eng != eng:
                    op.waits.append(o)
            op.dwaits.extend(pend[1])
        for k in reads:
            self._dep(op, self.last_write.get(k))
        for k in writes:
            self._dep(op, self.last_write.get(k))
            for r in self.readers.get(k, {}).values():
                self._dep(op, r)
        if dma_sem is not None:
            si = self.dma_sem(dma_sem)
            self.dma_sem_cnt[si] += 16
            op.dma = (si, self.dma_sem_cnt[si])
        for k in reads:
            rk = (eng, op.dma[0]) if op.dma is not None else eng
            self.readers.setdefault(k, {})[rk] = op
        for k in writes:
            self.last_write[k] = op
            self.readers[k] = {}
        self.ops[eng].append(op)
        return op

    def dma(self, q, out, in_, reads=(), writes=(), sem=None):
        return self.add(q, lambda e: e.dma_start(out=out, in_=in_), reads, writes, dma_sem=sem)

    def emit(self, final_ops=()):
        nc = self.nc
        for e in self.ENGS:
            c = 0
            for op in self.ops[e]:
                if op.inc and op.dma is None:
                    c += 1
                    op.semval = c
        with contextlib.ExitStack() as st:
            esem = {e: st.enter_context(nc.semaphore("s_" + e)) for e in self.ENGS}
            dsem = [st.enter_context(nc.semaphore("d_%d" % i)) for i in range(len(self.dma_sem_cnt))]
            block = st.enter_context(nc.Block())
            prog = self

            def run(e, eng):
                waited = {f: 0 for f in prog.ENGS}
                dwaited = {}
                for op in prog.ops[e]:
                    need = {}
                    for w in op.waits:
                        if w.semval > waited[w.eng] and w.semval > need.get(w.eng, 0):
                            need[w.eng] = w.semval
                    for f, v in need.items():
                        eng.wait_ge(esem[f], v)
                        waited[f] = v
                    dneed = {}
                    for si, v in op.dwaits:
                        if v > dwaited.get(si, 0) and v > dneed.get(si, 0):
                            dneed[si] = v
                    for si, v in dneed.items():
                        eng.wait_ge(dsem[si], v)
                        dwaited[si] = v
                    ins = op.fn(eng)
                    if op.dma is not None:
                        ins.then_inc(dsem[op.dma[0]], 16)
                    elif op.inc:
                        ins.then_inc(esem[e], 1)
                if e == "sp":
                    for op in final_ops:
                        si, v = op.dma
                        if v > dwaited.get(si, 0):
                            eng.wait_ge(dsem[si], v)
                            dwaited[si] = v

            block.tensor(lambda eng: run("pe", eng))
            block.scalar(lambda eng: run("act", eng))
            block.vector(lambda eng: run("dve", eng))
            block.gpsimd(lambda eng: run("pool", eng))
            block.sync(lambda eng: run("sp", eng))


class Rot:
    def __init__(self, name, tensors):
        self.name = name
        self.t = tensors
        self.i = 0

    def next(self):
        s = self.i % len(self.t)
        self.i += 1
        return self.t[s], (self.name, s)


NX = 2048
NCTX = 256
TOK = NX + NCTX
NFF = 22
TILES = [(0, 768), (768, 768), (1536, 768)]
SUBS5 = [(0, 512), (512, 512), (1024, 512), (1536, 512), (2048, 256)]
NCOL = 32
KSUB = int(os.environ.get('KSUB', '0'))
LAM_INIT0 = 0.8 - 0.6 * math.exp(-0.3 * 0)

C_AQ, C_AK, C_BQ, C_BK, C_BSUB, C_SINK, C_LQ1, C_LK1, C_LQ2, C_LK2 = 0, 1, 2, 3, 4, 5, 13, 14, 15, 16
C_QA, C_KVA, C_Q96, C_INV96, C_KN, C_KR, C_DQ, C_DK = 17, 23, 25, 26, 27, 28, 29, 30
M_ONES, M_BD64, M_PERM64, M_BD96, M_PERM96, M_PERM32 = 0, 1, 2, 3, 4, 5


def build(stage=99):
    nc = bass.Bass("TRN2", target_bir_lowering=False)

    def din(name, shape):
        return nc.dram_tensor(name, list(shape), F32, kind="ExternalInput").ap()

    xT = din("xT", [2, 8, 128, NX])
    ctxT = din("ctxT", [2, 8, 128, NCTX])
    cT = din("cT", [128, 8, 3])
    w_mod = din("w_mod", [2, 1024, 9216])
    b_modT = din("b_modT", [128, 2, 72])
    fgw = [din("f1g", [2, 1024, 2816]), din("f2g", [2, 1024, 2816])]
    fuw = [din("f1u", [2, 1024, 2816]), din("f2u", [2, 1024, 2816])]
    fdw = [din("f1d", [2, 2816, 1024]), din("f2d", [2, 2816, 1024])]
    ab_w_in = din("ab_w_in", [1024, 2304])
    ab_w_out = din("ab_w_out", [1024, 1024])
    cd_w_in = din("cd_w_in", [1024, 2592])
    cd_w_out = din("cd_w_out", [1024, 1024])
    c_w_uq = din("c_w_uq", [768, 768])
    c_w_ukv = din("c_w_ukv", [256, 1024])
    cols_d = din("cols", [128, NCOL])
    cmat_d = din("cmat", [128, 6, 128])
    rope64_d = din("rope64", [2, 128, NX])
    ropeM_d = din("ropeM", [2, 128, NX])
    maskA_d = din("maskA", [128, 6, 512])
    rpbG_d = din("rpbG", [128, 14, 512])
    outT = nc.dram_tensor("outT", [2, 8, 128, NX], F32, kind="ExternalOutput").ap()
    xs = nc.dram_tensor("xs_scratch", [8, 128, TOK], F32).ap()
    wc_g = [[nc.dram_tensor("wcg%d%d" % (w, l), [11, 128, 8, 256], BF16).ap() for l in range(2)] for w in range(2)]
    wc_u = [[nc.dram_tensor("wcu%d%d" % (w, l), [11, 128, 8, 256], BF16).ap() for l in range(2)] for w in range(2)]
    wc_d = [[nc.dram_tensor("wcd%d%d" % (w, l), [2, 4, 128, 6, 512], BF16).ap() for l in range(2)] for w in range(2)]

    P = Prog(nc)
    final_ops = []

    def ACT(out, in_, func, r, w, **kw):
        P.add("act", lambda e: e.activation(out=out, in_=in_, func=func, **kw), r, w)

    def MM(out, lhsT, rhs, start, stop, r, w, **kw):
        P.add("pe", lambda e: e.matmul(out, lhsT, rhs, start=start, stop=stop, **kw), r, w)

    def TT(eng, out, in0, in1, op, r, w):
        P.add(eng, lambda e: e.tensor_tensor(out=out, in0=in0, in1=in1, op=op), r, w)

    def TS(eng, out, in0, s1, op0, r, w):
        P.add(eng, lambda e: e.tensor_scalar(out=out, in0=in0, scalar1=s1, scalar2=None, op0=op0), r, w)

    def STT(eng, out, in0, scalar, in1, op0, op1, r, w):
        P.add(eng, lambda e: e.scalar_tensor_tensor(out=out, in0=in0, scalar=scalar, in1=in1, op0=op0, op1=op1), r, w)

    def RECIP(out, in_, r, w):
        P.add("dve", lambda e: e.reciprocal(out=out, in_=in_), r, w)

    def COPY(eng, out, in_, r, w):
        P.add(eng, lambda e: e.tensor_copy(out=out, in_=in_), r, w)

    with contextlib.ExitStack() as top:
        uid = [0]

        def sbt(st, name, shape, dt):
            uid[0] += 1
            return st.enter_context(nc.sbuf_tensor("sb_%s_%d" % (name, uid[0]), list(shape), dt))

        psb = [top.enter_context(nc.psum_tensor("ps%d" % i, [128, 512], F32)) for i in range(8)]

        def bank(i):
            return psb[i], ("ps", i)

        c_sb = sbt(top, "c_sb", [128, 8, 3], F32)
        c_act = sbt(top, "c_act", [128, 8, 3], BF16)
        bm = sbt(top, "bm", [128, 2, 72], F32)
        cols = sbt(top, "cols", [128, NCOL], F32)
        cm = sbt(top, "cm", [128, 6, 128], BF16)
        ones_f = sbt(top, "ones_f", [128, 128], F32)
        modT = sbt(top, "modT", [128, 2, 72, 3], F32)
        esink = sbt(top, "esink", [128, 8], F32)
        lamt = sbt(top, "lamt", [128, 8], F32)

        P.dma("sp", c_sb[:], cT, writes=["c_sb"], sem="c")
        P.dma("sp", bm[:], b_modT, writes=["bm"], sem="bm")
        P.dma("sp", cols[:], cols_d, writes=["cols"], sem="cols")
        P.dma("pool", cm[:], cmat_d, writes=["cm"], sem="cm")
        P.add("pool", lambda e: e.memset(ones_f[:], 1.0), (), ["ones_f"])
        ACT(c_act[:], c_sb[:], AF.Silu, ["c_sb"], ["c_act"])

        def mcol(l, i, c, j):
            return modT[:, l, 8 * i + c, j:j + 1]

        with contextlib.ExitStack() as st:
            wm = [sbt(st, "wm%d" % i, [128, 8, 512], BF16) for i in range(2)]
            it = 0
            for l in range(2):
                for sl in range(18):
                    s = it % 2
                    P.dma("pool", wm[s][:], w_mod[l][:, sl * 512:(sl + 1) * 512].rearrange("(k p) f -> p k f", p=128),
                          writes=[("wm", s)], sem="wm%d" % s)
                    pb, pk = bank(it % 2)
                    for cc in range(4):
                        for k in range(8):
                            MM(pb[:, cc * 4:cc * 4 + 3], wm[s][:, k, cc * 128:(cc + 1) * 128], c_act[:, k, :],
                               k == 0, k == 7, [("wm", s), "c_act"], [pk])
                    for cc in range(4):
                        col = sl * 4 + cc
                        TS("dve", modT[:, l, col, :], pb[:, cc * 4:cc * 4 + 3], bm[:, l, col:col + 1], ALU.add,
                           [pk, "bm"], ["modT"])
                    it += 1
            for l in range(2):
                for i in (1, 4, 7):
                    TS("dve", modT[:, l, 8 * i:8 * i + 8, :], modT[:, l, 8 * i:8 * i + 8, :], 1.0, ALU.add, ["modT"], ["modT"])
                for i in (2, 8):
                    TS("dve", modT[:, l, 8 * i:8 * i + 8, :], modT[:, l, 8 * i:8 * i + 8, :], 0.5, ALU.mult, ["modT"], ["modT"])
            ACT(esink[:], cols[:, C_SINK:C_SINK + 8], AF.Exp, ["cols"], ["esink"])
            TT("dve", lamt[:, 0:1], cols[:, C_LQ1:C_LQ1 + 1], cols[:, C_LK1:C_LK1 + 1], ALU.mult, ["cols"], ["lamt"])
            TT("dve", lamt[:, 1:2], cols[:, C_LQ2:C_LQ2 + 1], cols[:, C_LK2:C_LK2 + 1], ALU.mult, ["cols"], ["lamt"])
            pb, pk = bank(2)
            MM(pb[:, 0:2], ones_f[:], lamt[:, 0:2], True, True, ["ones_f", "lamt"], [pk])
            ACT(lamt[:, 2:4], pb[:, 0:2], AF.Exp, [pk], ["lamt"])
            TT("dve", lamt[:, 4:5], lamt[:, 3:4], lamt[:, 2:3], ALU.subtract, ["lamt"], ["lamt"])
            TS("dve", lamt[:, 4:5], lamt[:, 4:5], -LAM_INIT0, ALU.add, ["lamt"], ["lamt"])
            TS("dve", lamt[:, 5:6], cols[:, C_BSUB:C_BSUB + 1], 1.0 - LAM_INIT0, ALU.mult, ["cols", "lamt"], ["lamt"])
            P.barrier()

        def load_x(xrot, bi, ti, src):
            t, key = xrot.next()
            off = TILES[ti][0]
            sem = "x%d" % key[1]
            if src == "in":
                if ti < 2:
                    P.dma("sp", t[:, :, :], xT[bi][:, :, off:off + 768].rearrange("c p t -> p c t"), (), [key], sem)
                else:
                    P.dma("sp", t[:, :, 0:512], xT[bi][:, :, 1536:2048].rearrange("c p t -> p c t"), (), [key], sem)
                    P.dma("sp", t[:, :, 512:768], ctxT[bi].rearrange("c p t -> p c t"), (), [key], sem)
            else:
                P.dma("sp", t[:, :, :], xs[:, :, off:off + 768].rearrange("c p t -> p c t"), [("xs", ti)], [key], sem)
            return t, key

        def store_x(t, key, bi, ti, dst):
            off = TILES[ti][0]
            if dst == "out":
                if ti < 2:
                    op = P.dma("sp", outT[bi][:, :, off:off + 768].rearrange("c p t -> p c t"), t[:, :, :], [key], [("out", bi, ti)], "o%d" % key[1])
                else:
                    op = P.dma("sp", outT[bi][:, :, 1536:2048].rearrange("c p t -> p c t"), t[:, :, 0:512], [key], [("out", bi, ti)], "o%d" % key[1])
                final_ops.append(op)
            else:
                P.dma("sp", xs[:, :, off:off + 768].rearrange("c p t -> p c t"), t[:, :, :], [key], [("xs", ti)], "o%d" % key[1])

        def tile_subs(bi, ti, with_ctx=True):
            off = TILES[ti][0]
            if ti < 2:
                return [(0, 512, bi, off), (512, 256, bi, off + 512)]
            s = [(0, 512, bi, off)]
            if with_ctx:
                s.append((512, 256, 2, off + 512))
            return s

        def norm_mod(T, xt, xkey, loff, n, l, j, i_shift, i_scale, dst_fn, dst_key):
            ssb, ssk = T["ss"].next()
            for c in range(8):
                sq, sqk = T["sq"].next()
                ACT(sq[:, :n], xt[:, c, loff:loff + n], AF.Square, [xkey], [sqk])
                MM(ssb[:, :n], cm[:, M_ONES, :], sq[:, :n], c == 0, c == 7, [sqk, "cm"], [ssk])
            rs, rsk = T["rstd"].next()
            ACT(rs[:, :n], ssb[:, :n], AF.Sqrt, [ssk], [rsk], bias=EPS, scale=1.0 / 1024)
            RECIP(rs[:, :n], rs[:, :n], [rsk], [rsk])
            for c in range(8):
                tm, tmk = T["tmp"].next()
                STT("dve", tm[:, :n], xt[:, c, loff:loff + n], mcol(l, i_scale, c, j), rs[:, :n], ALU.mult, ALU.mult,
                    [xkey, rsk, "modT"], [tmk])
                ACT(dst_fn(c), tm[:, :n], AF.Identity, [tmk, "modT"], [dst_key], bias=mcol(l, i_shift, c, j), scale=1.0)

        FD_PIECES = [(0, 6), (6, 12), (12, 17), (17, 22)]

        def ffn(l, which, bi, src, dst, with_ctx):
            i_shift, i_scale, i_gate = (0, 1, 2) if which == 0 else (6, 7, 8)
            with contextlib.ExitStack() as st:
                xrot = Rot("xt", [sbt(st, "xt%d" % i, [128, 8, 768], F32) for i in range(2)])
                hT = sbt(st, "hT", [128, 8, 768], BF16)
                gT = sbt(st, "gT", [128, NFF, 768], BF16)
                T = make_rots(st, "f")
                T["ss"] = PsRot([6, 7])
                srot = Rot("s", [sbt(st, "s%d" % i, [128, 512], F32) for i in range(3)])
                wgu = Rot("wgu", [sbt(st, "wgu%d" % i, [128, 8, 256], BF16) for i in range(6)])
                wdr = Rot("wd", [sbt(st, "wd%d" % i, [128, 6, 512], BF16) for i in range(3)])
                P.barrier()
                pair = 0
                for ti in range(3):
                    subs = tile_subs(bi, ti, with_ctx)
                    xt, xkey = load_x(xrot, bi, ti, src)
                    for si, (loff, n, j, tok) in enumerate(subs):
                        norm_mod(T, xt, xkey, loff, n, l, j, i_shift, i_scale,
                                 (lambda c, loff=loff, n=n: hT[:, c, loff:loff + n]), ("hT", si))
                    for g in range(11):
                        f0 = g * 256
                        wgt, wgk = wgu.next()
                        wut, wuk = wgu.next()
                        first = (bi == 0 and ti == 0)
                        for (wt_, wk_, src_, cache_, cn_) in ((wgt, wgk, fgw, wc_g, "wcg"), (wut, wuk, fuw, wc_u, "wcu")):
                            ck = (cn_, which, l, g)
                            if first:
                                P.dma("pool", wt_[:], src_[which][l][:, f0:f0 + 256].rearrange("(k p) f -> p k f", p=128), (), [wk_], "wgu%d" % wk_[1])
                                P.dma("sp", cache_[which][l][g], wt_[:], [wk_], [ck], "wcs%d" % wk_[1])
                            else:
                                P.dma("sp", wt_[:], cache_[which][l][g], [ck], [wk_], "wgu%d" % wk_[1])
                        for fc in range(2):
                            f = g * 2 + fc
                            for si, (loff, n, j, tok) in enumerate(subs):
                                Gb, Gk = bank(2 * (pair % 3))
                                Ub, Uk = bank(2 * (pair % 3) + 1)
                                pair += 1
                                for k in range(8):
                                    MM(Gb[:, :n], wgt[:, k, fc * 128:(fc + 1) * 128], hT[:, k, loff:loff + n], k == 0, k == 7,
                                       [wgk, ("hT", si)], [Gk])
                                for k in range(8):
                                    MM(Ub[:, :n], wut[:, k, fc * 128:(fc + 1) * 128], hT[:, k, loff:loff + n], k == 0, k == 7,
                                       [wuk, ("hT", si)], [Uk])
                                sg, sk = srot.next()
                                ACT(sg[:, :n], Gb[:, :n], AF.Silu, [Gk], [sk])
                                TT("dve", gT[:, f, loff:loff + n], Ub[:, :n], sg[:, :n], ALU.mult, [Uk, sk], [("gT", si)])
                    for half in range(2):
                        for (fc0, fc1) in FD_PIECES:
                            wdt, wdk = wdr.next()
                            pi = FD_PIECES.index((fc0, fc1))
                            ck = ("wcd", which, l, half, pi)
                            if bi == 0 and ti == 0:
                                P.dma("pool", wdt[:, 0:fc1 - fc0, :],
                                      fdw[which][l][fc0 * 128:fc1 * 128, half * 512:(half + 1) * 512].rearrange("(j p) d -> p j d", p=128),
                                      (), [wdk], "wd%d" % wdk[1])
                                P.dma("sp", wc_d[which][l][half, pi, :, 0:fc1 - fc0, :], wdt[:, 0:fc1 - fc0, :], [wdk], [ck], "wds%d" % wdk[1])
                            else:
                                P.dma("sp", wdt[:, 0:fc1 - fc0, :], wc_d[which][l][half, pi, :, 0:fc1 - fc0, :], [ck], [wdk], "wd%d" % wdk[1])
                            for jf in range(fc1 - fc0):
                                fc = fc0 + jf
                                for dc in range(4):
                                    for si, (loff, n, j, tok) in enumerate(subs):
                                        Ob, Ok = bank(dc * 2 + si)
                                        MM(Ob[:, :n], wdt[:, jf, dc * 128:(dc + 1) * 128], gT[:, fc, loff:loff + n], fc == 0, fc == NFF - 1,
                                           [wdk, ("gT", si)], [Ok])
                        for dc in range(4):
                            for si, (loff, n, j, tok) in enumerate(subs):
                                Ob, Ok = bank(dc * 2 + si)
                                ch = half * 4 + dc
                                STT("dve", xt[:, ch, loff:loff + n], Ob[:, :n], mcol(l, i_gate, ch, j), xt[:, ch, loff:loff + n],
                                    ALU.mult, ALU.add, [Ok, xkey, "modT"], [xkey])
                    store_x(xt, xkey, bi, ti, dst)
                P.barrier()

        def make_rots(st, pfx):
            R = {
                "sq": Rot(pfx + "sq", [sbt(st, pfx + "sq%d" % i, [128, 512], BF16) for i in range(3)]),
                "tmp": Rot(pfx + "tmp", [sbt(st, pfx + "tmp%d" % i, [128, 512], F32) for i in range(3)]),
                "rstd": Rot(pfx + "rstd", [sbt(st, pfx + "rstd%d" % i, [128, 512], F32) for i in range(2)]),
            }
            return R

        def compute_hall(l, bi, hy):
            with contextlib.ExitStack() as st:
                xrot = Rot("xt", [sbt(st, "hxt%d" % i, [128, 8, 768], F32) for i in range(2)])
                T = make_rots(st, "h")
                T["ss"] = PsRot([6, 7])
                P.barrier()
                for ti in range(3):
                    xt, xkey = load_x(xrot, bi, ti, "xs")
                    for si, (loff, n, j, tok) in enumerate(tile_subs(bi, ti, True)):
                        norm_mod(T, xt, xkey, loff, n, l, j, 3, 4,
                                 (lambda c, tok=tok, n=n: hy[:, c, tok:tok + n]), "hy")
                P.barrier()

        class PsRot:
            def __init__(self, idxs):
                self.idxs = idxs
                self.i = 0

            def next(self):
                b = self.idxs[self.i % len(self.idxs)]
                self.i += 1
                return bank(b)

        def proj_fm(pr, wt, wk, col0, M, hy, tok, n):
            pb, pk = pr.next()
            for k in range(8):
                MM(pb[:M, :n], wt[:, k, col0:col0 + M], hy[:, k, tok:tok + n], k == 0, k == 7, [wk, "hy"], [pk])
            return pb, pk

        def headnorm(R, pr, src, srck, M, n, gain_col, inv_n, bd, dst, dstk, rope=None, tok=0):
            sq, sqk = R["sq"].next()
            ACT(sq[:M, :n], src, AF.Square, [srck], [sqk])
            sb_, sk_ = pr.next()
            MM(sb_[:M, :n], bd, sq[:M, :n], True, True, [sqk, "cm"], [sk_])
            rs, rsk = R["rstd"].next()
            ACT(rs[:M, :n], sb_[:M, :n], AF.Sqrt, [sk_, "cols"], [rsk], bias=EPS, scale=inv_n)
            RECIP(rs[:M, :n], rs[:M, :n], [rsk], [rsk])
            STT("dve", dst, src, gain_col, rs[:M, :n], ALU.mult, ALU.mult, [srck, rsk, "cols"], [dstk])
            if rope is not None:
                perm, rC, rS, p0, p1 = rope
                swb, swk = pr.next()
                MM(swb[:M, :n], perm, dst, True, True, [dstk, "cm"], [swk])
                t1, t1k = R["tmp"].next()
                TT("pool", t1[p0:p1, :n], dst[p0:p1, :], rC[p0:p1, tok:tok + n], ALU.mult, [dstk, "rope"], [t1k])
                t2, t2k = R["tmp"].next()
                TT("dve", t2[p0:p1, :n], swb[p0:p1, :n], rS[p0:p1, tok:tok + n], ALU.mult, [swk, "rope"], [t2k])
                TT("pool", dst[p0:p1, :], t1[p0:p1, :n], t2[p0:p1, :n], ALU.add, [t1k, t2k, swk], [dstk])

        def out_proj(l, bi, w_out_d, ychunk, with_ctx):
            with contextlib.ExitStack() as st:
                xrot = Rot("xt", [sbt(st, "oxt%d" % i, [128, 8, 768], F32) for i in range(2)])
                wo = sbt(st, "wo", [128, 8, 1024], BF16)
                P.barrier()
                for hh in range(2):
                    P.dma("pool", wo[:, :, hh * 512:(hh + 1) * 512],
                          w_out_d[:, hh * 512:(hh + 1) * 512].rearrange("(k p) f -> p k f", p=128), (), [("wo", hh)], "wo%d" % hh)
                pr = PsRot([0, 1, 2, 3, 4, 5, 6, 7])
                for ti in range(3):
                    xt, xkey = load_x(xrot, bi, ti, "xs")
                    for si, (loff, n, j, tok) in enumerate(tile_subs(bi, ti, with_ctx)):
                        for dc in range(8):
                            pb, pk = pr.next()
                            for fc in range(8):
                                MM(pb[:, :n], wo[:, fc, dc * 128:(dc + 1) * 128], ychunk(fc)[:, tok:tok + n], fc == 0, fc == 7,
                                   [("wo", dc // 4), "hy", "yD"], [pk])
                            STT("dve", xt[:, dc, loff:loff + n], pb[:, :n], mcol(l, 5, dc, j), xt[:, dc, loff:loff + n],
                                ALU.mult, ALU.add, [pk, xkey, "modT"], [xkey])
                    store_x(xt, xkey, bi, ti, "xs")
                P.barrier()

        def mixer_ab(bi):
            l = 0
            with contextlib.ExitStack() as st0:
                hy = sbt(st0, "hy", [128, 8, TOK], BF16)
                compute_hall(l, bi, hy)
                with contextlib.ExitStack() as st:
                    aq = sbt(st, "aq", [128, 4, TOK], BF16)
                    bq = sbt(st, "bq", [128, 4, TOK], BF16)
                    bk = sbt(st, "bk", [128, 4, TOK], BF16)
                    bv = sbt(st, "bv", [128, 18, 512], BF16)
                    akd = sbt(st, "akd", [128, 2, TOK], BF16)
                    av = sbt(st, "av", [128, 18, 128], BF16)
                    rC = sbt(st, "rC", [128, NX], F32)
                    rS = sbt(st, "rS", [128, NX], F32)
                    maskA = sbt(st, "maskA", [128, 6, 512], BF16)
                    wr = Rot("w", [sbt(st, "wr%d" % i, [128, 8, 512], BF16) for i in range(2)])
                    R = make_rots(st, "m")
                    prot = Rot("P", [sbt(st, "P%d" % i, [128, 512], BF16) for i in range(3)])
                    rdrot = Rot("rd", [sbt(st, "rd%d" % i, [128, 512], F32) for i in range(2)])
                    ta = sbt(st, "ta", [128, 512], F32)
                    tb = sbt(st, "tb", [128, 512], F32)
                    yv = sbt(st, "yv", [128, 512], F32)
                    P.barrier()
                    P.dma("sp", rC[:], rope64_d[0], (), ["rope"], "rC")
                    P.dma("sp", rS[:], rope64_d[1], (), ["rope"], "rS")
                    P.dma("pool", maskA[:], maskA_d, (), ["maskA"], "maskA")
                    pr = PsRot([0, 1, 2, 3, 4, 5, 6, 7])
                    rope64 = (cm[:, M_PERM64, :], rC, rS, 0, 128)
                    bd64 = cm[:, M_BD64, :]

                    def load_w(src_ap, ncols, dst_off=0, slot=None):
                        if slot is None:
                            slot = wr.next()
                        wt, wk = slot
                        P.dma("pool", wt[:, :, dst_off:dst_off + ncols], src_ap.rearrange("(k p) f -> p k f", p=128), (), [wk], "w%d" % wk[1])
                        return slot

                    def qk_group(col_base, gain_c, dst, dname, subs_list):
                        wt, wk = load_w(ab_w_in[:, col_base:col_base + 512], 512)
                        for (tok, n) in subs_list:
                            for c in range(4):
                                pb, pk = proj_fm(pr, wt, wk, c * 128, 128, hy, tok, n)
                                headnorm(R, pr, pb[:, :n], pk, 128, n, cols[:, gain_c:gain_c + 1], 1.0 / 64, bd64,
                                         dst[:, c, tok:tok + n], (dname, c), rope=(rope64 if tok < NX else None), tok=tok)

                    qk_group(0, C_AQ, aq, "aq", SUBS5)
                    qk_group(512, C_BQ, bq, "bq", SUBS5)
                    slot = wr.next()
                    for g in range(2):
                        for d in range(2):
                            load_w(ab_w_in[:, 1024 + g * 64:1024 + (g + 1) * 64], 64, dst_off=g * 128 + d * 64, slot=slot)
                    load_w(ab_w_in[:, 1152:1280], 128, dst_off=256, slot=slot)
                    wt, wk = slot
                    for (tok, n) in SUBS5:
                        for g in range(2):
                            pb, pk = proj_fm(pr, wt, wk, g * 128, 128, hy, tok, n)
                            headnorm(R, pr, pb[:, :n], pk, 128, n, cols[:, C_AK:C_AK + 1], 1.0 / 64, bd64,
                                     akd[:, g, tok:tok + n], ("akd", g), rope=(rope64 if tok < NX else None), tok=tok)
                    for tt in range(18):
                        pb, pk = pr.next()
                        for k in range(8):
                            MM(pb[:, 0:128], hy[:, k, tt * 128:(tt + 1) * 128], wt[:, k, 256:384], k == 0, k == 7, [wk, "hy"], [pk])
                        COPY("dve", av[:, tt, :], pb[:, 0:128], [pk], ["av"])
                    qk_group(1280, C_BK, bk, "bk", SUBS5)
                    wt, wk = load_w(ab_w_in[:, 1792:2304], 512)
                    for tt in range(18):
                        pb, pk = pr.next()
                        for k in range(8):
                            MM(pb[:, :], hy[:, k, tt * 128:(tt + 1) * 128], wt[:, k, 0:512], k == 0, k == 7, [wk, "hy"], [pk])
                        ACT(bv[:, tt, :], pb[:, :], AF.Copy, [pk], ["bv"])

                    srot = PsRot([0, 1])
                    orot = PsRot([2, 4])
                    drot = PsRot([3, 5])
                    groupsA = []
                    for qg in range(4):
                        n0 = qg * 4
                        kts = [(kt, kt - n0 + 1) for kt in range(max(0, n0 - 1), min(15, n0 + 4) + 1)] + [(16, None), (17, None)]
                        groupsA.append((qg * 512, 512, kts))
                    groupsA.append((NX, NCTX, [(16, None), (17, None)]))
                    for h in range(8):
                        g, c, p0 = h // 4, h // 2, (h % 2) * 64
                        for (qoff, nq, kts) in groupsA:
                            Ob, Ok = orot.next()
                            Db, Dk = drot.next()
                            for i, (kt, mi) in enumerate(kts):
                                Sb, Sk = srot.next()
                                MM(Sb[:, :nq], akd[p0:p0 + 64, g, kt * 128:(kt + 1) * 128], aq[p0:p0 + 64, c, qoff:qoff + nq], True, True,
                                   [("akd", g), ("aq", c)], [Sk])
                                Pt, Pk = prot.next()
                                ACT(Pt[:, :nq], Sb[:, :nq], AF.Exp, [Sk], [Pk], scale=0.125)
                                if mi is not None:
                                    TT("dve", Pt[:, :nq], Pt[:, :nq], maskA[:, mi, :nq], ALU.mult, [Pk, "maskA"], [Pk])
                                MM(Ob[0:64, :nq], av[:, kt, g * 64:(g + 1) * 64], Pt[:, :nq], i == 0, i == len(kts) - 1, ["av", Pk], [Ok])
                                MM(Db[0:64, :nq], cm[:, M_ONES, 0:64], Pt[:, :nq], i == 0, i == len(kts) - 1, ["cm", Pk], [Dk])
                            rd, rdk = rdrot.next()
                            TS("dve", rd[0:64, :nq], Db[0:64, :nq], esink[0:64, h:h + 1], ALU.add, [Dk, "esink"], [rdk])
                            RECIP(rd[0:64, :nq], rd[0:64, :nq], [rdk], [rdk])
                            TT("dve", hy[p0:p0 + 64, c, qoff:qoff + nq], Ob[0:64, :nq], rd[0:64, :nq], ALU.mult, [Ok, rdk], ["hy"])

                    allk = [16, 17] + list(range(16))
                    groupsB = [(qg * 512, 512, allk) for qg in range(4)] + [(NX, NCTX, [16, 17])]
                    for h in range(4):
                        for (qoff, nq, kts) in groupsB:
                            for m in range(2):
                                Ob, Ok = bank(2 + 2 * m)
                                Db, Dk = bank(3 + 2 * m)
                                for i, kt in enumerate(kts):
                                    Sb, Sk = srot.next()
                                    MM(Sb[:, :nq], bk[m * 64:(m + 1) * 64, h, kt * 128:(kt + 1) * 128], bq[m * 64:(m + 1) * 64, h, qoff:qoff + nq],
                                       True, True, [("bk", h), ("bq", h)], [Sk])
                                    Pt, Pk = prot.next()
                                    ACT(Pt[:, :nq], Sb[:, :nq], AF.Exp, [Sk], [Pk], scale=0.125)
                                    MM(Ob[:, :nq], bv[:, kt, h * 128:(h + 1) * 128], Pt[:, :nq], i == 0, i == len(kts) - 1, ["bv", Pk], [Ok])
                                    MM(Db[:, :nq], cm[:, M_ONES, :], Pt[:, :nq], i == 0, i == len(kts) - 1, ["cm", Pk], [Dk])
                            r1, r1k = rdrot.next()
                            RECIP(r1[:, :nq], psb[3][:, :nq], [("ps", 3)], [r1k])
                            r2, r2k = rdrot.next()
                            RECIP(r2[:, :nq], psb[5][:, :nq], [("ps", 5)], [r2k])
                            TT("dve", ta[:, :nq], psb[2][:, :nq], r1[:, :nq], ALU.mult, [("ps", 2), r1k], ["ta"])
                            TT("dve", tb[:, :nq], psb[4][:, :nq], r2[:, :nq], ALU.mult, [("ps", 4), r2k], ["tb"])
                            STT("dve", yv[:, :nq], tb[:, :nq], lamt[:, 4:5], ta[:, :nq], ALU.mult, ALU.add, ["ta", "tb", "lamt"], ["yv"])
                            headnorm(R, PsRot([6, 7]), yv[:, :nq], "yv", 128, nq, lamt[:, 5:6], 1.0 / 128, cm[:, M_ONES, :],
                                     hy[:, 4 + h, qoff:qoff + nq], "hy")
                    P.barrier()
                out_proj(l, bi, ab_w_out, (lambda fc: hy[:, fc, :]), True)

        def mixer_cd(bi):
            l = 1
            with contextlib.ExitStack() as st0:
                hy = sbt(st0, "hy1", [128, 8, TOK], BF16)
                yD = sbt(st0, "yD", [128, 4, NX], BF16)
                compute_hall(l, bi, hy)
                srot = PsRot([0, 1])
                with contextlib.ExitStack() as st:
                    dq = sbt(st, "dq", [128, 4, NX], BF16)
                    dk = sbt(st, "dk", [128, 4, TOK], BF16)
                    vD = sbt(st, "vD", [128, 33, 512], BF16)
                    E2 = sbt(st, "E2", [128, 14, 512], BF16)
                    eg = sbt(st, "eg", [128, 2, 512], F32)
                    wr = Rot("w", [sbt(st, "dwr%d" % i, [128, 8, 512], BF16) for i in range(2)])
                    R = make_rots(st, "d")
                    prot = Rot("P", [sbt(st, "dP%d" % i, [128, 512], BF16) for i in range(3)])
                    rdrot = Rot("rd", [sbt(st, "drd%d" % i, [128, 512], F32) for i in range(2)])
                    P.barrier()
                    pr = PsRot([0, 1, 2, 3, 4, 5, 6, 7])
                    bd64 = cm[:, M_BD64, :]
                    for i in range(14):
                        s = i % 2
                        P.dma("sp", eg[:, s, :], rpbG_d[:, i, :], (), [("eg", s)], "eg%d" % s)
                        ACT(E2[:, i, :], eg[:, s, :], AF.Exp, [("eg", s)], ["E2"])

                    def load_w(src_ap, ncols):
                        wt, wk = wr.next()
                        P.dma("pool", wt[:, :, 0:ncols], src_ap.rearrange("(k p) f -> p k f", p=128), (), [wk], "w%d" % wk[1])
                        return wt, wk

                    wt, wk = load_w(cd_w_in[:, 768:1280], 512)
                    for (tok, n) in SUBS5[:4]:
                        for c in range(4):
                            pb, pk = proj_fm(pr, wt, wk, c * 128, 128, hy, tok, n)
                            headnorm(R, pr, pb[:, :n], pk, 128, n, cols[:, C_DQ:C_DQ + 1], 1.0 / 64, bd64, dq[:, c, tok:tok + n], ("dq", c))
                    wt, wk = load_w(cd_w_in[:, 1568:2080], 512)
                    for (tok, n) in SUBS5:
                        for c in range(4):
                            pb, pk = proj_fm(pr, wt, wk, c * 128, 128, hy, tok, n)
                            headnorm(R, pr, pb[:, :n], pk, 128, n, cols[:, C_DK:C_DK + 1], 1.0 / 64, bd64, dk[:, c, tok:tok + n], ("dk", c))
                    wt, wk = load_w(cd_w_in[:, 2080:2592], 512)

                    def wtok(w):
                        return 64 * w if w < 31 else NX + 128 * (w - 31)
                    for w in range(33):
                        pb, pk = pr.next()
                        t0 = wtok(w)
                        for k in range(8):
                            MM(pb[:, :], hy[:, k, t0:t0 + 128], wt[:, k, 0:512], k == 0, k == 7, [wk, "hy"], [pk])
                        if w % 2 == 0:
                            ACT(vD[:, w, :], pb[:, :], AF.Copy, [pk], ["vD"])
                        else:
                            COPY("dve", vD[:, w, :], pb[:, :], [pk], ["vD"])
                    orot = PsRot([2, 4])
                    drot = PsRot([3, 5])
                    srotE = PsRot([0, 6])
                    srotO = PsRot([1, 7])
                    for r in range(0 if (KSUB & 1) else 32):
                        rs_ = min(max(r - 4, 0), 24)
                        units = [(rs_ + 2 * i, rs_ + 2 * i - r + 7) for i in range(4)] + [(31, None), (32, None)]
                        Ob, Ok = orot.next()
                        Db, Dk = drot.next()
                        for ui, (w, ei) in enumerate(units):
                            t0 = wtok(w)
                            SbE, SkE = srotE.next()
                            SbO, SkO = srotO.next()
                            for h in range(8):
                                c, p0 = h // 2, (h % 2) * 64
                                Sb, Sk = (SbE, SkE) if h % 2 == 0 else (SbO, SkO)
                                MM(Sb[:, c * 64:(c + 1) * 64], dk[p0:p0 + 64, c, t0:t0 + 128], dq[p0:p0 + 64, c, r * 64:(r + 1) * 64],
                                   c == 0, c == 3, [("dk", c), ("dq", c)], [Sk], skip_group_check=True)
                            Pt, Pk = prot.next()
                            PtV = Pt[:, :].rearrange("p (c two q) -> p c two q", two=2, q=64)
                            ACT(PtV[:, :, 0, :], SbE[:, 0:256].rearrange("p (c q) -> p c q", q=64), AF.Exp, [SkE], [Pk], scale=0.125)
                            ACT(PtV[:, :, 1, :], SbO[:, 0:256].rearrange("p (c q) -> p c q", q=64), AF.Exp, [SkO, Pk], [Pk], scale=0.125)
                            if ei is not None:
                                TT("dve", Pt[:, :], Pt[:, :], E2[:, ei, :], ALU.mult, [Pk, "E2"], [Pk])
                            for h in range(8):
                                MM(Ob[0:64, h * 64:(h + 1) * 64], vD[:, w, h * 64:(h + 1) * 64], Pt[:, h * 64:(h + 1) * 64],
                                   ui == 0 and h == 0, ui == len(units) - 1 and h == 7, ["vD", Pk], [Ok], skip_group_check=True)
                            MM(Db[0:64, :], cm[:, M_ONES, 0:64], Pt[:, :], ui == 0, ui == len(units) - 1, ["cm", Pk], [Dk])
                        rd, rdk = rdrot.next()
                        RECIP(rd[0:64, :], Db[0:64, :], [Dk], [rdk])
                        for h in range(8):
                            c, p0 = h // 2, (h % 2) * 64
                            TT("dve", yD[p0:p0 + 64, c, r * 64:(r + 1) * 64], Ob[0:64, h * 64:(h + 1) * 64], rd[0:64, h * 64:(h + 1) * 64],
                               ALU.mult, [Ok, rdk], ["yD"])
                    P.barrier()
                with contextlib.ExitStack() as st:
                    cqg = sbt(st, "cqg", [128, 6, NX], BF16)
                    rsa = sbt(st, "rsa", [128, NX], F32)
                    ckvn = sbt(st, "ckvn", [128, 2, TOK], BF16)
                    krope = sbt(st, "krope", [32, TOK], BF16)
                    rC = sbt(st, "rCm", [128, NX], F32)
                    rS = sbt(st, "rSm", [128, NX], F32)
                    vC = sbt(st, "vC", [128, 18, 512], BF16)
                    qcat = sbt(st, "qcat", [96, NX], BF16)
                    kcat = sbt(st, "kcat", [96, TOK], BF16)
                    wr = Rot("w", [sbt(st, "cwr%d" % i, [128, 8, 512], BF16) for i in range(2)])
                    wqr = Rot("wq", [sbt(st, "wq%d" % i, [128, 6, 96], BF16) for i in range(2)])
                    wkr = Rot("wk", [sbt(st, "wk%d" % i, [128, 2, 64], BF16) for i in range(2)])
                    R = make_rots(st, "c")
                    prot = Rot("P", [sbt(st, "cP%d" % i, [128, 512], BF16) for i in range(3)])
                    rdrot = Rot("rd", [sbt(st, "crd%d" % i, [128, 512], F32) for i in range(2)])
                    P.barrier()
                    P.dma("sp", rC[:], ropeM_d[0], (), ["rope"], "rC")
                    P.dma("sp", rS[:], ropeM_d[1], (), ["rope"], "rS")
                    pr = PsRot([2, 3, 4, 5, 6, 7])

                    def load_w(src_ap, ncols):
                        wt, wk = wr.next()
                        P.dma("pool", wt[:, :, 0:ncols], src_ap.rearrange("(k p) f -> p k f", p=128), (), [wk], "w%d" % wk[1])
                        return wt, wk

                    wa, wak = load_w(cd_w_in[:, 0:512], 512)
                    wb, wbk = load_w(cd_w_in[:, 512:768], 256)
                    for (tok, n) in SUBS5[:4]:
                        ssb, ssk = bank(0 + (tok // 512) % 2)
                        for c in range(6):
                            wt, wk, cc = (wa, wak, c) if c < 4 else (wb, wbk, c - 4)
                            pb, pk = proj_fm(pr, wt, wk, cc * 128, 128, hy, tok, n)
                            sq, sqk = R["sq"].next()
                            ACT(sq[:, :n], pb[:, :n], AF.Square, [pk], [sqk])
                            MM(ssb[:, :n], cm[:, M_ONES, :], sq[:, :n], c == 0, c == 5, [sqk, "cm"], [ssk])
                            TS("dve", cqg[:, c, tok:tok + n], pb[:, :n], cols[:, C_QA + c:C_QA + c + 1], ALU.mult, [pk, "cols", sqk], ["cqg"])
                        ACT(rsa[:, tok:tok + n], ssb[:, :n], AF.Sqrt, [ssk], ["rsa"], bias=EPS, scale=1.0 / 768)
                        RECIP(rsa[:, tok:tok + n], rsa[:, tok:tok + n], ["rsa"], ["rsa"])
                    wt, wk = load_w(cd_w_in[:, 1280:1568], 288)
                    rope32 = (cm[0:32, M_PERM32, 0:32], rC, rS, 0, 32)
                    for (tok, n) in SUBS5:
                        ssb, ssk = bank(0 + (tok // 512) % 2)
                        pbs = []
                        for c in range(2):
                            pb, pk = proj_fm(pr, wt, wk, c * 128, 128, hy, tok, n)
                            pbs.append((pb, pk))
                            sq, sqk = R["sq"].next()
                            ACT(sq[:, :n], pb[:, :n], AF.Square, [pk], [sqk])
                            MM(ssb[:, :n], cm[:, M_ONES, :], sq[:, :n], c == 0, c == 1, [sqk, "cm"], [ssk])
                        rs, rsk = R["rstd"].next()
                        ACT(rs[:, :n], ssb[:, :n], AF.Sqrt, [ssk], [rsk], bias=EPS, scale=1.0 / 256)
                        RECIP(rs[:, :n], rs[:, :n], [rsk], [rsk])
                        for c in range(2):
                            pb, pk = pbs[c]
                            STT("dve", ckvn[:, c, tok:tok + n], pb[:, :n], cols[:, C_KVA + c:C_KVA + c + 1], rs[:, :n], ALU.mult, ALU.mult,
                                [pk, rsk, "cols"], ["ckvn"])
                        pb, pk = proj_fm(pr, wt, wk, 256, 32, hy, tok, n)
                        headnorm(R, pr, pb[0:32, :n], pk, 32, n, cols[0:32, C_KR:C_KR + 1], 1.0 / 32, cm[0:32, M_ONES, 0:32],
                                 krope[0:32, tok:tok + n], "krope", rope=(rope32 if tok < NX else None), tok=tok)
                    wt, wk = wr.next()
                    for kk in range(2):
                        P.dma("pool", wt[:, kk, 0:512].rearrange("p (h e) -> p h e", e=64),
                              c_w_ukv[kk * 128:(kk + 1) * 128, :].rearrange("p (h e) -> p h e", e=128)[:, :, 64:128], (), [wk], "w%d" % wk[1])
                    for tt in range(18):
                        pb, pk = pr.next()
                        for c in range(2):
                            MM(pb[:, :], ckvn[:, c, tt * 128:(tt + 1) * 128], wt[:, c, 0:512], c == 0, c == 1, [wk, "ckvn"], [pk])
                        ACT(vC[:, tt, :], pb[:, :], AF.Copy, [pk], ["vC"])
                    orot = PsRot([2, 4])
                    drot = PsRot([3, 5])
                    pr2 = PsRot([6, 7])
                    allk = [16, 17] + list(range(16))
                    sc = 96.0 ** -0.5
                    for h in range(0 if (KSUB & 2) else 8):
                        c_, p0 = h // 2, (h % 2) * 64
                        wq, wqk = wqr.next()
                        P.dma("pool", wq[:], c_w_uq[:, h * 96:(h + 1) * 96].rearrange("(k p) f -> p k f", p=128), (), [wqk], "wq%d" % wqk[1])
                        wkk, wkkk = wkr.next()
                        P.dma("pool", wkk[:], c_w_ukv[:, h * 128:h * 128 + 64].rearrange("(k p) f -> p k f", p=128), (), [wkkk], "wk%d" % wkkk[1])
                        for (tok, n) in SUBS5[:4]:
                            pb, pk = pr2.next()
                            for c in range(6):
                                MM(pb[0:96, :n], wq[:, c, :], cqg[:, c, tok:tok + n], c == 0, c == 5, [wqk, "cqg"], [pk])
                            qs, qsk = R["tmp"].next()
                            TT("dve", qs[0:96, :n], pb[0:96, :n], rsa[0:96, tok:tok + n], ALU.mult, [pk, "rsa"], [qsk])
                            headnorm(R, pr2, qs[0:96, :n], qsk, 96, n, cols[0:96, C_Q96:C_Q96 + 1], cols[0:96, C_INV96:C_INV96 + 1],
                                     cm[0:96, M_BD96, 0:96], qcat[0:96, tok:tok + n], "qcat",
                                     rope=(cm[0:96, M_PERM96, 0:96], rC, rS, 64, 96), tok=tok)
                        for (tok, n) in SUBS5:
                            pb, pk = pr2.next()
                            for c in range(2):
                                MM(pb[0:64, :n], wkk[:, c, :], ckvn[:, c, tok:tok + n], c == 0, c == 1, [wkkk, "ckvn"], [pk])
                            headnorm(R, pr2, pb[0:64, :n], pk, 64, n, cols[0:64, C_KN:C_KN + 1], 1.0 / 64, cm[0:64, M_ONES, 0:64],
                                     kcat[0:64, tok:tok + n], "kcat")
                        P.dma("sp", kcat[64:96, :], krope[0:32, :], ["krope"], ["kcat"], "kcr")
                        for qg in range(4):
                            qoff = qg * 512
                            Ob, Ok = orot.next()
                            Db, Dk = drot.next()
                            for i, kt in enumerate(allk):
                                Sb, Sk = srot.next()
                                MM(Sb[:, :], kcat[0:96, kt * 128:(kt + 1) * 128], qcat[0:96, qoff:qoff + 512], True, True, ["kcat", "qcat"], [Sk])
                                Pt, Pk = prot.next()
                                ACT(Pt[:, :], Sb[:, :], AF.Exp, [Sk], [Pk], scale=sc)
                                MM(Ob[0:64, :], vC[:, kt, h * 64:(h + 1) * 64], Pt[:, :], i == 0, i == 17, ["vC", Pk], [Ok])
                                MM(Db[0:64, :], cm[:, M_ONES, 0:64], Pt[:, :], i == 0, i == 17, ["cm", Pk], [Dk])
                            rd, rdk = rdrot.next()
                            RECIP(rd[0:64, :], Db[0:64, :], [Dk], [rdk])
                            TT("dve", hy[p0:p0 + 64, c_, qoff:qoff + 512], Ob[0:64, :], rd[0:64, :], ALU.mult, [Ok, rdk], ["hy"])
                    P.barrier()
                out_proj(l, bi, cd_w_out, (lambda fc: hy[:, fc, :] if fc < 4 else yD[:, fc - 4, :]), False)

        for bi in range(2):
            ffn(0, 0, bi, "in", "xs" if stage > 1 else "out", True)
            if stage <= 1:
                continue
            mixer_ab(bi)
            if stage <= 2:
                _dump(P, nc, top, xs, outT, bi, final_ops)
                continue
            ffn(0, 1, bi, "xs", "xs" if stage > 3 else "out", True)
            if stage <= 3:
                continue
            ffn(1, 0, bi, "xs", "xs" if stage > 4 else "out", True)
            if stage <= 4:
                continue
            mixer_cd(bi)
            if stage <= 5:
                _dump(P, nc, top, xs, outT, bi, final_ops)
                continue
            ffn(1, 1, bi, "xs", "out", False)
        P.emit(final_ops)
    return nc


def _dump(P, nc, top, xs, outT, bi, final_ops):
    P.barrier()
    op = P.dma("sp", outT[bi], xs[:, :, 0:NX], [("xs", 0), ("xs", 1), ("xs", 2)], [("out", bi, "d")], "dump")
    final_ops.append(op)
    P.barrier()


def _rope_tables(rot_dim):
    nf = rot_dim // 4
    inv = (10000.0 ** (-np.arange(nf, dtype=np.float64) / nf))
    t = np.arange(NX)
    row = (t // 64).astype(np.float64)
    col = (t % 64).astype(np.float64)
    C = np.zeros((rot_dim, NX), np.float64)
    S = np.zeros((rot_dim, NX), np.float64)
    for d in range(rot_dim):
        axis = d // (2 * nf)
        half = (d % (2 * nf)) // nf
        f = d % nf
        ang = (row if axis == 0 else col) * inv[f]
        C[d] = np.cos(ang)
        S[d] = np.sin(ang) * (-1.0 if half == 0 else 1.0)
    return C.astype(np.float32), S.astype(np.float32)


def _perm(rot_dim):
    nf = rot_dim // 4
    Pm = np.zeros((rot_dim, rot_dim), np.float32)
    for d in range(rot_dim):
        half = (d % (2 * nf)) // nf
        partner = d + nf if half == 0 else d - nf
        Pm[partner, d] = 1.0
    return Pm


def _constants():
    cmat = np.zeros((128, 6, 128), np.float32)
    cmat[:, M_ONES, :] = 1.0
    cmat[0:64, M_BD64, 0:64] = 1.0
    cmat[64:128, M_BD64, 64:128] = 1.0
    p64 = _perm(64)
    cmat[0:64, M_PERM64, 0:64] = p64
    cmat[64:128, M_PERM64, 64:128] = p64
    cmat[0:64, M_BD96, 0:64] = 1.0
    cmat[64:96, M_BD96, 64:96] = 1.0
    p32 = _perm(32)
    cmat[0:64, M_PERM96, 0:64] = np.eye(64, dtype=np.float32)
    cmat[64:96, M_PERM96, 64:96] = p32
    cmat[0:32, M_PERM32, 0:32] = p32
    C64, S64 = _rope_tables(64)
    rope64 = np.stack([np.concatenate([C64, C64]), np.concatenate([S64, S64])]).astype(np.float32)
    C32, S32 = _rope_tables(32)
    ropeM = np.zeros((2, 128, NX), np.float32)
    ropeM[0] = 1.0
    ropeM[0, 0:32] = C32
    ropeM[1, 0:32] = S32
    ropeM[0, 64:96] = C32
    ropeM[1, 64:96] = S32
    maskA = np.zeros((128, 6, 512), np.float32)
    jj = np.arange(128)[:, None]
    qq = np.arange(512)[None, :]
    b = qq // 128
    ii = qq % 128
    for mi in range(6):
        d = mi - 1 - b
        maskA[:, mi, :] = (np.abs(ii - jj - 128 * d) <= 128).astype(np.float32)
    return cmat, rope64, ropeM, maskA


def _rpb_gather(rpb):
    kc = np.arange(64)[:, None]
    qc = np.arange(64)[None, :]
    dc = np.clip(kc - qc, -15, 15) + 15
    cs = np.clip(qc - 8, 0, 48)
    valid = (kc >= cs) & (kc < cs + 16)
    G = np.empty((2, 64, 14, 8, 64), np.float32)
    for rr in range(2):
        for i in range(14):
            g = rpb[:, i + rr, :][:, dc]
            g = np.where(valid[None], g, np.float32(-100.0))
            G[rr, :, i, :, :] = np.transpose(g, (1, 0, 2))
    return np.ascontiguousarray(G.reshape(128, 14, 512))


def _prep(inp):
    f = lambda a: np.ascontiguousarray(np.asarray(a, dtype=np.float32))
    cmat, rope64, ropeM, maskA = _constants()
    shared = {
        "w_mod": f(inp["w_mod"]),
        "b_modT": f(np.transpose(np.asarray(inp["b_mod"]).reshape(2, 72, 128), (2, 0, 1))),
        "f1g": f(inp["ffn1_w_gate"]), "f1u": f(inp["ffn1_w_up"]), "f1d": f(inp["ffn1_w_down"]),
        "f2g": f(inp["ffn2_w_gate"]), "f2u": f(inp["ffn2_w_up"]), "f2d": f(inp["ffn2_w_down"]),
        "ab_w_in": f(inp["ab_w_in"][0]), "ab_w_out": f(inp["ab_w_out"][0]),
        "cd_w_in": f(inp["cd_w_in"][0]), "cd_w_out": f(inp["cd_w_out"][0]),
        "c_w_uq": f(inp["c_w_uq"][0]), "c_w_ukv": f(inp["c_w_ukv"][0]),
        "cmat": cmat, "rope64": rope64, "ropeM": ropeM, "maskA": maskA,
        "rpbG": _rpb_gather(np.asarray(inp["d_rpb"][0], dtype=np.float32)),
    }
    cols = np.zeros((128, NCOL), np.float32)
    g = lambda k: np.asarray(inp[k][0], dtype=np.float32)
    cols[:, C_AQ] = np.tile(g("a_q_norm"), 2)
    cols[:, C_AK] = np.tile(g("a_k_norm"), 2)
    cols[:, C_BQ] = np.tile(g("b_q_norm"), 2)
    cols[:, C_BK] = np.tile(g("b_k_norm"), 2)
    cols[:, C_BSUB] = g("b_sub_norm")
    cols[:, C_SINK:C_SINK + 8] = g("a_sink")[None, :]
    cols[0:64, C_LQ1] = g("b_lambda_q1")
    cols[0:64, C_LK1] = g("b_lambda_k1")
    cols[0:64, C_LQ2] = g("b_lambda_q2")
    cols[0:64, C_LK2] = g("b_lambda_k2")
    cols[:, C_QA:C_QA + 6] = g("c_q_a_norm").reshape(6, 128).T
    cols[:, C_KVA:C_KVA + 2] = g("c_kv_a_norm").reshape(2, 128).T
    cols[:, C_Q96] = 1.0
    cols[0:64, C_Q96] = g("c_q_nope_norm")
    cols[64:96, C_Q96] = g("c_q_rope_norm")
    cols[:, C_INV96] = 1.0 / 64
    cols[64:96, C_INV96] = 1.0 / 32
    cols[:, C_KN] = np.tile(g("c_k_nope_norm"), 2)
    cols[:, C_KR] = np.tile(g("c_k_rope_norm"), 4)
    cols[:, C_DQ] = np.tile(g("d_q_norm"), 2)
    cols[:, C_DK] = np.tile(g("d_k_norm"), 2)
    shared["cols"] = cols
    x = np.asarray(inp["x"], dtype=np.float32)
    ctx = np.asarray(inp["ctx"], dtype=np.float32)
    c = np.asarray(inp["c"], dtype=np.float32)
    c_ctx = np.asarray(inp["c_ctx"], dtype=np.float32)
    in_maps = []
    for core in range(8):
        b0 = 2 * core
        m = dict(shared)
        m["xT"] = np.ascontiguousarray(np.transpose(x[b0:b0 + 2], (0, 2, 1)).reshape(2, 8, 128, NX))
        m["ctxT"] = np.ascontiguousarray(np.transpose(ctx[b0:b0 + 2], (0, 2, 1)).reshape(2, 8, 128, NCTX))
        cc = np.stack([c[b0], c[b0 + 1], c_ctx], axis=-1)
        m["cT"] = np.ascontiguousarray(np.transpose(cc.reshape(8, 128, 3), (1, 0, 2)))
        in_maps.append(m)
    return in_maps


def kernel(**inputs):
    stage = int(os.environ.get("KSTAGE", "99"))
    ncores = int(os.environ.get("KCORES", "8"))
    in_maps = _prep(inputs)
    nc = build(stage)
    res = run_bass_kernel_spmd(nc, in_maps[:ncores], core_ids=list(range(ncores)))
    out = np.zeros((16, NX, 1024), np.float32)
    for core in range(ncores):
        o = np.asarray(res.results[core]["outT"]).reshape(2, 1024, NX)
        out[2 * core:2 * core + 2] = np.transpose(o, (0, 2, 1))
    return out
```

```python
import contextlib
import math
import os
import numpy as np
import concourse.bass as bass
import concourse.mybir as mybir
from concourse.bass_utils import run_bass_kernel_spmd

F32 = mybir.dt.float32
BF16 = mybir.dt.bfloat16
AF = mybir.ActivationFunctionType
ALU = mybir.AluOpType
EPS = 1e-6


class Op:
    __slots__ = ("eng", "fn", "waits", "dwaits", "inc", "semval", "dma")

    def __init__(self, eng, fn):
        self.eng = eng
        self.fn = fn
        self.waits = []
        self.dwaits = []
        self.inc = False
        self.semval = None
        self.dma = None


class Prog:
    ENGS = ("pe", "act", "dve", "pool", "sp")

    def __init__(self, nc):
        self.nc = nc
        self.ops = {e: [] for e in self.ENGS}
        self.last_write = {}
        self.readers = {}
        self.dma_sem_of = {}
        self.dma_sem_cnt = []
        self.pending = {}

    def dma_sem(self, name):
        if name not in self.dma_sem_of:
            self.dma_sem_of[name] = len(self.dma_sem_cnt)
            self.dma_sem_cnt.append(0)
        return self.dma_sem_of[name]

    def _dep(self, op, prod):
        if prod is None or prod is op:
            return
        if prod.dma is not None:
            op.dwaits.append(prod.dma)
        elif prod.eng != op.eng or prod.eng != "pe":
            op.waits.append(prod)
            prod.inc = True

    def barrier(self):
        snap = []
        for e in self.ENGS:
            if self.ops[e]:
                o = self.ops[e][-1]
                if o.dma is None:
                    o.inc = True
                    snap.append(o)
        dsnap = [(i, c) for i, c in enumerate(self.dma_sem_cnt) if c > 0]
        self.pending = {e: (snap, dsnap) for e in self.ENGS}

    def add(self, eng, fn, reads=(), writes=(), dma_sem=None):
        op = Op(eng, fn)
        pend = self.pending.pop(eng, None)
        if pend is not None:
            for o in pend[0]:
                if o.eng != eng:
                    op.waits.append(o)
            op.dwaits.extend(pend[1])
        for k in reads:
            self._dep(op, self.last_write.get(k))
        for k in writes:
            self._dep(op, self.last_write.get(k))
            for r in self.readers.get(k, {}).values():
                self._dep(op, r)
        if dma_sem is not None:
            si = self.dma_sem(dma_sem)
            self.dma_sem_cnt[si] += 16
            op.dma = (si, self.dma_sem_cnt[si])
        for k in reads:
            rk = (eng, op.dma[0]) if op.dma is not None else eng
            self.readers.setdefault(k, {})[rk] = op
        for k in writes:
            self.last_write[k] = op
            self.readers[k] = {}
        self.ops[eng].append(op)
        return op

    def dma(self, q, out, in_, reads=(), writes=(), sem=None):
        return self.add(q, lambda e: e.dma_start(out=out, in_=in_), reads, writes, dma_sem=sem)

    def emit(self, final_ops=()):
        nc = self.nc
        for e in self.ENGS:
            c = 0
            for op in self.ops[e]:
                if op.inc and op.dma is None:
                    c += 1
                    op.semval = c
        with contextlib.ExitStack() as st:
            esem = {e: st.enter_context(nc.semaphore("s_" + e)) for e in self.ENGS}
            dsem = [st.enter_context(nc.semaphore("d_%d" % i)) for i in range(len(self.dma_sem_cnt))]
            block = st.enter_context(nc.Block())
            prog = self

            def run(e, eng):
                waited = {f: 0 for f in prog.ENGS}
                dwaited = {}
                for op in prog.ops[e]:
                    need = {}
                    for w in op.waits:
                        if w.semval > waited[w.eng] and w.semval > need.get(w.eng, 0):
                            need[w.eng] = w.semval
                    for f, v in need.items():
                        eng.wait_ge(esem[f], v)
                        waited[f] = v
                    dneed = {}
                    for si, v in op.dwaits:
                        if v > dwaited.get(si, 0) and v > dneed.get(si, 0):
                            dneed[si] = v
                    for si, v in dneed.items():
                        eng.wait_ge(dsem[si], v)
                        dwaited[si] = v
                    ins = op.fn(eng)
                    if op.dma is not None:
                        ins.then_inc(dsem[op.dma[0]], 16)
                    elif op.inc:
                        ins.then_inc(esem[e], 1)
                if e == "sp":
                    for op in final_ops:
                        si, v = op.dma
                        if v > dwaited.get(si, 0):
                            eng.wait_ge(dsem[si], v)
                            dwaited[si] = v

            block.tensor(lambda eng: run("pe", eng))
            block.scalar(lambda eng: run("act", eng))
            block.vector(lambda eng: run("dve", eng))
            block.gpsimd(lambda eng: run("pool", eng))
            block.sync(lambda eng: run("sp", eng))


class Rot:
    def __init__(self, name, tensors):
        self.name = name
        self.t = tensors
        self.i = 0

    def next(self):
        s = self.i % len(self.t)
        self.i += 1
        return self.t[s], (self.name, s)


NX = 2048
NCTX = 256
TOK = NX + NCTX
NFF = 22
TILES = [(0, 768), (768, 768), (1536, 768)]
SUBS5 = [(0, 512), (512, 512), (1024, 512), (1536, 512), (2048, 256)]
NCOL = 32
KSUB = int(os.environ.get('KSUB', '0'))
LAM_INIT0 = 0.8 - 0.6 * math.exp(-0.3 * 0)

C_AQ, C_AK, C_BQ, C_BK, C_BSUB, C_SINK, C_LQ1, C_LK1, C_LQ2, C_LK2 = 0, 1, 2, 3, 4, 5, 13, 14, 15, 16
C_QA, C_KVA, C_Q96, C_INV96, C_KN, C_KR, C_DQ, C_DK = 17, 23, 25, 26, 27, 28, 29, 30
M_ONES, M_BD64, M_PERM64, M_BD96, M_PERM96, M_PERM32 = 0, 1, 2, 3, 4, 5


def build(stage=99):
    nc = bass.Bass("TRN2", target_bir_lowering=False)

    def din(name, shape):
        return nc.dram_tensor(name, list(shape), F32, kind="ExternalInput").ap()

    xT = din("xT", [2, 8, 128, NX])
    ctxT = din("ctxT", [2, 8, 128, NCTX])
    cT = din("cT", [128, 8, 3])
    w_mod = din("w_mod", [2, 1024, 9216])
    b_modT = din("b_modT", [128, 2, 72])
    fgw = [din("f1g", [2, 1024, 2816]), din("f2g", [2, 1024, 2816])]
    fuw = [din("f1u", [2, 1024, 2816]), din("f2u", [2, 1024, 2816])]
    fdw = [din("f1d", [2, 2816, 1024]), din("f2d", [2, 2816, 1024])]
    ab_w_in = din("ab_w_in", [1024, 2304])
    ab_w_out = din("ab_w_out", [1024, 1024])
    cd_w_in = din("cd_w_in", [1024, 2592])
    cd_w_out = din("cd_w_out", [1024, 1024])
    c_w_uq = din("c_w_uq", [768, 768])
    c_w_ukv = din("c_w_ukv", [256, 1024])
    cols_d = din("cols", [128, NCOL])
    cmat_d = din("cmat", [128, 6, 128])
    rope64_d = din("rope64", [2, 128, NX])
    ropeM_d = din("ropeM", [2, 128, NX])
    maskA_d = din("maskA", [128, 6, 512])
    rpbG_d = din("rpbG", [128, 14, 512])
    outT = nc.dram_tensor("outT", [2, 8, 128, NX], F32, kind="ExternalOutput").ap()
    xs = nc.dram_tensor("xs_scratch", [8, 128, TOK], F32).ap()
    wc_g = [[nc.dram_tensor("wcg%d%d" % (w, l), [11, 128, 8, 256], BF16).ap() for l in range(2)] for w in range(2)]
    wc_u = [[nc.dram_tensor("wcu%d%d" % (w, l), [11, 128, 8, 256], BF16).ap() for l in range(2)] for w in range(2)]
    wc_d = [[nc.dram_tensor("wcd%d%d" % (w, l), [2, 4, 128, 6, 512], BF16).ap() for l in range(2)] for w in range(2)]

    P = Prog(nc)
    final_ops = []

    def ACT(out, in_, func, r, w, **kw):
        P.add("act", lambda e: e.activation(out=out, in_=in_, func=func, **kw), r, w)

    def MM(out, lhsT, rhs, start, stop, r, w, **kw):
        P.add("pe", lambda e: e.matmul(out, lhsT, rhs, start=start, stop=stop, **kw), r, w)

    def TT(eng, out, in0, in1, op, r, w):
        P.add(eng, lambda e: e.tensor_tensor(out=out, in0=in0, in1=in1, op=op), r, w)

    def TS(eng, out, in0, s1, op0, r, w):
        P.add(eng, lambda e: e.tensor_scalar(out=out, in0=in0, scalar1=s1, scalar2=None, op0=op0), r, w)

    def STT(eng, out, in0, scalar, in1, op0, op1, r, w):
        P.add(eng, lambda e: e.scalar_tensor_tensor(out=out, in0=in0, scalar=scalar, in1=in1, op0=op0, op1=op1), r, w)

    def RECIP(out, in_, r, w):
        P.add("dve", lambda e: e.reciprocal(out=out, in_=in_), r, w)

    def COPY(eng, out, in_, r, w):
        P.add(eng, lambda e: e.tensor_copy(out=out, in_=in_), r, w)

    with contextlib.ExitStack() as top:
        uid = [0]

        def sbt(st, name, shape, dt):
            uid[0] += 1
            return st.enter_context(nc.sbuf_tensor("sb_%s_%d" % (name, uid[0]), list(shape), dt))

        psb = [top.enter_context(nc.psum_tensor("ps%d" % i, [128, 512], F32)) for i in range(8)]

        def bank(i):
            return psb[i], ("ps", i)

        c_sb = sbt(top, "c_sb", [128, 8, 3], F32)
        c_act = sbt(top, "c_act", [128, 8, 3], BF16)
        bm = sbt(top, "bm", [128, 2, 72], F32)
        cols = sbt(top, "cols", [128, NCOL], F32)
        cm = sbt(top, "cm", [128, 6, 128], BF16)
        ones_f = sbt(top, "ones_f", [128, 128], F32)
        modT = sbt(top, "modT", [128, 2, 72, 3], F32)
        esink = sbt(top, "esink", [128, 8], F32)
        lamt = sbt(top, "lamt", [128, 8], F32)

        P.dma("sp", c_sb[:], cT, writes=["c_sb"], sem="c")
        P.dma("sp", bm[:], b_modT, writes=["bm"], sem="bm")
        P.dma("sp", cols[:], cols_d, writes=["cols"], sem="cols")
        P.dma("pool", cm[:], cmat_d, writes=["cm"], sem="cm")
        P.add("pool", lambda e: e.memset(ones_f[:], 1.0), (), ["ones_f"])
        ACT(c_act[:], c_sb[:], AF.Silu, ["c_sb"], ["c_act"])

        def mcol(l, i, c, j):
            return modT[:, l, 8 * i + c, j:j + 1]

        with contextlib.ExitStack() as st:
            wm = [sbt(st, "wm%d" % i, [128, 8, 512], BF16) for i in range(2)]
            it = 0
            for l in range(2):
                for sl in range(18):
                    s = it % 2
                    P.dma("pool", wm[s][:], w_mod[l][:, sl * 512:(sl + 1) * 512].rearrange("(k p) f -> p k f", p=128),
                          writes=[("wm", s)], sem="wm%d" % s)
                    pb, pk = bank(it % 2)
                    for cc in range(4):
                        for k in range(8):
                            MM(pb[:, cc * 4:cc * 4 + 3], wm[s][:, k, cc * 128:(cc + 1) * 128], c_act[:, k, :],
                               k == 0, k == 7, [("wm", s), "c_act"], [pk])
                    for cc in range(4):
                        col = sl * 4 + cc
                        TS("dve", modT[:, l, col, :], pb[:, cc * 4:cc * 4 + 3], bm[:, l, col:col + 1], ALU.add,
                           [pk, "bm"], ["modT"])
                    it += 1
            for l in range(2):
                for i in (1, 4, 7):
                    TS("dve", modT[:, l, 8 * i:8 * i + 8, :], modT[:, l, 8 * i:8 * i + 8, :], 1.0, ALU.add, ["modT"], ["modT"])
                for i in (2, 8):
                    TS("dve", modT[:, l, 8 * i:8 * i + 8, :], modT[:, l, 8 * i:8 * i + 8, :], 0.5, ALU.mult, ["modT"], ["modT"])
            ACT(esink[:], cols[:, C_SINK:C_SINK + 8], AF.Exp, ["cols"], ["esink"])
            TT("dve", lamt[:, 0:1], cols[:, C_LQ1:C_LQ1 + 1], cols[:, C_LK1:C_LK1 + 1], ALU.mult, ["cols"], ["lamt"])
            TT("dve", lamt[:, 1:2], cols[:, C_LQ2:C_LQ2 + 1], cols[:, C_LK2:C_LK2 + 1], ALU.mult, ["cols"], ["lamt"])
            pb, pk = bank(2)
            MM(pb[:, 0:2], ones_f[:], lamt[:, 0:2], True, True, ["ones_f", "lamt"], [pk])
            ACT(lamt[:, 2:4], pb[:, 0:2], AF.Exp, [pk], ["lamt"])
            TT("dve", lamt[:, 4:5], lamt[:, 3:4], lamt[:, 2:3], ALU.subtract, ["lamt"], ["lamt"])
            TS("dve", lamt[:, 4:5], lamt[:, 4:5], -LAM_INIT0, ALU.add, ["lamt"], ["lamt"])
            TS("dve", lamt[:, 5:6], cols[:, C_BSUB:C_BSUB + 1], 1.0 - LAM_INIT0, ALU.mult, ["cols", "lamt"], ["lamt"])
            P.barrier()

        def load_x(xrot, bi, ti, src):
            t, key = xrot.next()
            off = TILES[ti][0]
            sem = "x%d" % key[1]
            if src == "in":
                if ti < 2:
                    P.dma("sp", t[:, :, :], xT[bi][:, :, off:off + 768].rearrange("c p t -> p c t"), (), [key], sem)
                else:
                    P.dma("sp", t[:, :, 0:512], xT[bi][:, :, 1536:2048].rearrange("c p t -> p c t"), (), [key], sem)
                    P.dma("sp", t[:, :, 512:768], ctxT[bi].rearrange("c p t -> p c t"), (), [key], sem)
            else:
                P.dma("sp", t[:, :, :], xs[:, :, off:off + 768].rearrange("c p t -> p c t"), [("xs", ti)], [key], sem)
            return t, key

        def store_x(t, key, bi, ti, dst):
            off = TILES[ti][0]
            if dst == "out":
                if ti < 2:
                    op = P.dma("sp", outT[bi][:, :, off:off + 768].rearrange("c p t -> p c t"), t[:, :, :], [key], [("out", bi, ti)], "o%d" % key[1])
                else:
                    op = P.dma("sp", outT[bi][:, :, 1536:2048].rearrange("c p t -> p c t"), t[:, :, 0:512], [key], [("out", bi, ti)], "o%d" % key[1])
                final_ops.append(op)
            else:
                P.dma("sp", xs[:, :, off:off + 768].rearrange("c p t -> p c t"), t[:, :, :], [key], [("xs", ti)], "o%d" % key[1])

        def tile_subs(bi, ti, with_ctx=True):
            off = TILES[ti][0]
            if ti < 2:
                return [(0, 512, bi, off), (512, 256, bi, off + 512)]
            s = [(0, 512, bi, off)]
            if with_ctx:
                s.append((512, 256, 2, off + 512))
            return s

        def norm_mod(T, xt, xkey, loff, n, l, j, i_shift, i_scale, dst_fn, dst_key):
            ssb, ssk = T["ss"].next()
            for c in range(8):
                sq, sqk = T["sq"].next()
                ACT(sq[:, :n], xt[:, c, loff:loff + n], AF.Square, [xkey], [sqk])
                MM(ssb[:, :n], cm[:, M_ONES, :], sq[:, :n], c == 0, c == 7, [sqk, "cm"], [ssk])
            rs, rsk = T["rstd"].next()
            ACT(rs[:, :n], ssb[:, :n], AF.Sqrt, [ssk], [rsk], bias=EPS, scale=1.0 / 1024)
            RECIP(rs[:, :n], rs[:, :n], [rsk], [rsk])
            for c in range(8):
                tm, tmk = T["tmp"].next()
                STT("dve", tm[:, :n], xt[:, c, loff:loff + n], mcol(l, i_scale, c, j), rs[:, :n], ALU.mult, ALU.mult,
                    [xkey, rsk, "modT"], [tmk])
                ACT(dst_fn(c), tm[:, :n], AF.Identity, [tmk, "modT"], [dst_key], bias=mcol(l, i_shift, c, j), scale=1.0)

        FD_PIECES = [(0, 6), (6, 12), (12, 17), (17, 22)]

        def ffn(l, which, bi, src, dst, with_ctx):
            i_shift, i_scale, i_gate = (0, 1, 2) if which == 0 else (6, 7, 8)
            with contextlib.ExitStack() as st:
                xrot = Rot("xt", [sbt(st, "xt%d" % i, [128, 8, 768], F32) for i in range(2)])
                hT = sbt(st, "hT", [128, 8, 768], BF16)
                gT = sbt(st, "gT", [128, NFF, 768], BF16)
                T = make_rots(st, "f")
                T["ss"] = PsRot([6, 7])
                srot = Rot("s", [sbt(st, "s%d" % i, [128, 512], F32) for i in range(3)])
                wgu = Rot("wgu", [sbt(st, "wgu%d" % i, [128, 8, 256], BF16) for i in range(6)])
                wdr = Rot("wd", [sbt(st, "wd%d" % i, [128, 6, 512], BF16) for i in range(3)])
                P.barrier()
                pair = 0
                for ti in range(3):
                    subs = tile_subs(bi, ti, with_ctx)
                    xt, xkey = load_x(xrot, bi, ti, src)
                    for si, (loff, n, j, tok) in enumerate(subs):
                        norm_mod(T, xt, xkey, loff, n, l, j, i_shift, i_scale,
                                 (lambda c, loff=loff, n=n: hT[:, c, loff:loff + n]), ("hT", si))
                    for g in range(11):
                        f0 = g * 256
                        wgt, wgk = wgu.next()
                        wut, wuk = wgu.next()
                        first = (bi == 0 and ti == 0)
                        for (wt_, wk_, src_, cache_, cn_) in ((wgt, wgk, fgw, wc_g, "wcg"), (wut, wuk, fuw, wc_u, "wcu")):
                            ck = (cn_, which, l, g)
                            if first:
                                P.dma("pool", wt_[:], src_[which][l][:, f0:f0 + 256].rearrange("(k p) f -> p k f", p=128), (), [wk_], "wgu%d" % wk_[1])
                                P.dma("sp", cache_[which][l][g], wt_[:], [wk_], [ck], "wcs%d" % wk_[1])
                            else:
                                P.dma("sp", wt_[:], cache_[which][l][g], [ck], [wk_], "wgu%d" % wk_[1])
                        for fc in range(2):
                            f = g * 2 + fc
                            for si, (loff, n, j, tok) in enumerate(subs):
                                Gb, Gk = bank(2 * (pair % 3))
                                Ub, Uk = bank(2 * (pair % 3) + 1)
                                pair += 1
                                for k in range(8):
                                    MM(Gb[:, :n], wgt[:, k, fc * 128:(fc + 1) * 128], hT[:, k, loff:loff + n], k == 0, k == 7,
                                       [wgk, ("hT", si)], [Gk])
                                for k in range(8):
                                    MM(Ub[:, :n], wut[:, k, fc * 128:(fc + 1) * 128], hT[:, k, loff:loff + n], k == 0, k == 7,
                                       [wuk, ("hT", si)], [Uk])
                                sg, sk = srot.next()
                                ACT(sg[:, :n], Gb[:, :n], AF.Silu, [Gk], [sk])
                                TT("dve", gT[:, f, loff:loff + n], Ub[:, :n], sg[:, :n], ALU.mult, [Uk, sk], [("gT", si)])
                    for half in range(2):
                        for (fc0, fc1) in FD_PIECES:
                            wdt, wdk = wdr.next()
                            pi = FD_PIECES.index((fc0, fc1))
                            ck = ("wcd", which, l, half, pi)
                            if bi == 0 and ti == 0:
                                P.dma("pool", wdt[:, 0:fc1 - fc0, :],
                                      fdw[which][l][fc0 * 128:fc1 * 128, half * 512:(half + 1) * 512].rearrange("(j p) d -> p j d", p=128),
                                      (), [wdk], "wd%d" % wdk[1])
                                P.dma("sp", wc_d[which][l][half, pi, :, 0:fc1 - fc0, :], wdt[:, 0:fc1 - fc0, :], [wdk], [ck], "wds%d" % wdk[1])
                            else:
                                P.dma("sp", wdt[:, 0:fc1 - fc0, :], wc_d[which][l][half, pi, :, 0:fc1 - fc0, :], [ck], [wdk], "wd%d" % wdk[1])
                            for jf in range(fc1 - fc0):
                                fc = fc0 + jf
                                for dc in range(4):
                                    for si, (loff, n, j, tok) in enumerate(subs):
                                        Ob, Ok = bank(dc * 2 + si)
                                        MM(Ob[:, :n], wdt[:, jf, dc * 128:(dc + 1) * 128], gT[:, fc, loff:loff + n], fc == 0, fc == NFF - 1,
                                           [wdk, ("gT", si)], [Ok])
                        for dc in range(4):
                            for si, (loff, n, j, tok) in enumerate(subs):
                                Ob, Ok = bank(dc * 2 + si)
                                ch = half * 4 + dc
                                STT("dve", xt[:, ch, loff:loff + n], Ob[:, :n], mcol(l, i_gate, ch, j), xt[:, ch, loff:loff + n],
                                    ALU.mult, ALU.add, [Ok, xkey, "modT"], [xkey])
                    store_x(xt, xkey, bi, ti, dst)
                P.barrier()

        def make_rots(st, pfx):
            R = {
                "sq": Rot(pfx + "sq", [sbt(st, pfx + "sq%d" % i, [128, 512], BF16) for i in range(3)]),
                "tmp": Rot(pfx + "tmp", [sbt(st, pfx + "tmp%d" % i, [128, 512], F32) for i in range(3)]),
                "rstd": Rot(pfx + "rstd", [sbt(st, pfx + "rstd%d" % i, [128, 512], F32) for i in range(2)]),
            }
            return R

        def compute_hall(l, bi, hy):
            with contextlib.ExitStack() as st:
                xrot = Rot("xt", [sbt(st, "hxt%d" % i, [128, 8, 768], F32) for i in range(2)])
                T = make_rots(st, "h")
                T["ss"] = PsRot([6, 7])
                P.barrier()
                for ti in range(3):
                    xt, xkey = load_x(xrot, bi, ti, "xs")
                    for si, (loff, n, j, tok) in enumerate(tile_subs(bi, ti, True)):
                        norm_mod(T, xt, xkey, loff, n, l, j, 3, 4,
                                 (lambda c, tok=tok, n=n: hy[:, c, tok:tok + n]), "hy")
                P.barrier()

        def pipelined(n_units, s_fn, rest_fn):
            ctx = {}
            if n_units:
                ctx[0] = s_fn(0)
            for i in range(n_units):
                if i + 1 < n_units:
                    ctx[i + 1] = s_fn(i + 1)
                rest_fn(i, ctx.pop(i))

        class PsRot:
            def __init__(self, idxs):
                self.idxs = idxs
                self.i = 0

            def next(self):
                b = self.idxs[self.i % len(self.idxs)]
                self.i += 1
                return bank(b)

        def proj_fm(pr, wt, wk, col0, M, hy, tok, n):
            pb, pk = pr.next()
            for k in range(8):
                MM(pb[:M, :n], wt[:, k, col0:col0 + M], hy[:, k, tok:tok + n], k == 0, k == 7, [wk, "hy"], [pk])
            return pb, pk

        def headnorm(R, pr, src, srck, M, n, gain_col, inv_n, bd, dst, dstk, rope=None, tok=0):
            sq, sqk = R["sq"].next()
            ACT(sq[:M, :n], src, AF.Square, [srck], [sqk])
            sb_, sk_ = pr.next()
            MM(sb_[:M, :n], bd, sq[:M, :n], True, True, [sqk, "cm"], [sk_])
            rs, rsk = R["rstd"].next()
            ACT(rs[:M, :n], sb_[:M, :n], AF.Sqrt, [sk_, "cols"], [rsk], bias=EPS, scale=inv_n)
            RECIP(rs[:M, :n], rs[:M, :n], [rsk], [rsk])
            STT("dve", dst, src, gain_col, rs[:M, :n], ALU.mult, ALU.mult, [srck, rsk, "cols"], [dstk])
            if rope is not None:
                perm, rC, rS, p0, p1 = rope
                swb, swk = pr.next()
                MM(swb[:M, :n], perm, dst, True, True, [dstk, "cm"], [swk])
                t1, t1k = R["tmp"].next()
                TT("pool", t1[p0:p1, :n], dst[p0:p1, :], rC[p0:p1, tok:tok + n], ALU.mult, [dstk, "rope"], [t1k])
                t2, t2k = R["tmp"].next()
                TT("dve", t2[p0:p1, :n], swb[p0:p1, :n], rS[p0:p1, tok:tok + n], ALU.mult, [swk, "rope"], [t2k])
                TT("pool", dst[p0:p1, :], t1[p0:p1, :n], t2[p0:p1, :n], ALU.add, [t1k, t2k, swk], [dstk])

        def out_proj(l, bi, w_out_d, ychunk, with_ctx):
            with contextlib.ExitStack() as st:
                xrot = Rot("xt", [sbt(st, "oxt%d" % i, [128, 8, 768], F32) for i in range(2)])
                wo = sbt(st, "wo", [128, 8, 1024], BF16)
                P.barrier()
                for hh in range(2):
                    P.dma("pool", wo[:, :, hh * 512:(hh + 1) * 512],
                          w_out_d[:, hh * 512:(hh + 1) * 512].rearrange("(k p) f -> p k f", p=128), (), [("wo", hh)], "wo%d" % hh)
                pr = PsRot([0, 1, 2, 3, 4, 5, 6, 7])
                for ti in range(3):
                    xt, xkey = load_x(xrot, bi, ti, "xs")
                    for si, (loff, n, j, tok) in enumerate(tile_subs(bi, ti, with_ctx)):
                        for dc in range(8):
                            pb, pk = pr.next()
                            for fc in range(8):
                                MM(pb[:, :n], wo[:, fc, dc * 128:(dc + 1) * 128], ychunk(fc)[:, tok:tok + n], fc == 0, fc == 7,
                                   [("wo", dc // 4), "hy", "yD"], [pk])
                            STT("dve", xt[:, dc, loff:loff + n], pb[:, :n], mcol(l, 5, dc, j), xt[:, dc, loff:loff + n],
                                ALU.mult, ALU.add, [pk, xkey, "modT"], [xkey])
                    store_x(xt, xkey, bi, ti, "xs")
                P.barrier()

        def mixer_ab(bi):
            l = 0
            with contextlib.ExitStack() as st0:
                hy = sbt(st0, "hy", [128, 8, TOK], BF16)
                compute_hall(l, bi, hy)
                with contextlib.ExitStack() as st:
                    aq = sbt(st, "aq", [128, 4, TOK], BF16)
                    bq = sbt(st, "bq", [128, 4, TOK], BF16)
                    bk = sbt(st, "bk", [128, 4, TOK], BF16)
                    bv = sbt(st, "bv", [128, 18, 512], BF16)
                    akd = sbt(st, "akd", [128, 2, TOK], BF16)
                    av = sbt(st, "av", [128, 18, 128], BF16)
                    rC = sbt(st, "rC", [128, NX], F32)
                    rS = sbt(st, "rS", [128, NX], F32)
                    maskA = sbt(st, "maskA", [128, 6, 512], BF16)
                    wr = Rot("w", [sbt(st, "wr%d" % i, [128, 8, 512], BF16) for i in range(2)])
                    R = make_rots(st, "m")
                    prot = Rot("P", [sbt(st, "P%d" % i, [128, 512], BF16) for i in range(3)])
                    rdrot = Rot("rd", [sbt(st, "rd%d" % i, [128, 512], F32) for i in range(2)])
                    ta = sbt(st, "ta", [128, 512], F32)
                    tb = sbt(st, "tb", [128, 512], F32)
                    yv = sbt(st, "yv", [128, 512], F32)
                    P.barrier()
                    P.dma("sp", rC[:], rope64_d[0], (), ["rope"], "rC")
                    P.dma("sp", rS[:], rope64_d[1], (), ["rope"], "rS")
                    P.dma("pool", maskA[:], maskA_d, (), ["maskA"], "maskA")
                    pr = PsRot([0, 1, 2, 3, 4, 5, 6, 7])
                    rope64 = (cm[:, M_PERM64, :], rC, rS, 0, 128)
                    bd64 = cm[:, M_BD64, :]

                    def load_w(src_ap, ncols, dst_off=0, slot=None):
                        if slot is None:
                            slot = wr.next()
                        wt, wk = slot
                        P.dma("pool", wt[:, :, dst_off:dst_off + ncols], src_ap.rearrange("(k p) f -> p k f", p=128), (), [wk], "w%d" % wk[1])
                        return slot

                    def qk_group(col_base, gain_c, dst, dname, subs_list):
                        wt, wk = load_w(ab_w_in[:, col_base:col_base + 512], 512)
                        for (tok, n) in subs_list:
                            for c in range(4):
                                pb, pk = proj_fm(pr, wt, wk, c * 128, 128, hy, tok, n)
                                headnorm(R, pr, pb[:, :n], pk, 128, n, cols[:, gain_c:gain_c + 1], 1.0 / 64, bd64,
                                         dst[:, c, tok:tok + n], (dname, c), rope=(rope64 if tok < NX else None), tok=tok)

                    qk_group(0, C_AQ, aq, "aq", SUBS5)
                    qk_group(512, C_BQ, bq, "bq", SUBS5)
                    slot = wr.next()
                    for g in range(2):
                        for d in range(2):
                            load_w(ab_w_in[:, 1024 + g * 64:1024 + (g + 1) * 64], 64, dst_off=g * 128 + d * 64, slot=slot)
                    load_w(ab_w_in[:, 1152:1280], 128, dst_off=256, slot=slot)
                    wt, wk = slot
                    for (tok, n) in SUBS5:
                        for g in range(2):
                            pb, pk = proj_fm(pr, wt, wk, g * 128, 128, hy, tok, n)
                            headnorm(R, pr, pb[:, :n], pk, 128, n, cols[:, C_AK:C_AK + 1], 1.0 / 64, bd64,
                                     akd[:, g, tok:tok + n], ("akd", g), rope=(rope64 if tok < NX else None), tok=tok)
                    for tt in range(18):
                        pb, pk = pr.next()
                        for k in range(8):
                            MM(pb[:, 0:128], hy[:, k, tt * 128:(tt + 1) * 128], wt[:, k, 256:384], k == 0, k == 7, [wk, "hy"], [pk])
                        COPY("dve", av[:, tt, :], pb[:, 0:128], [pk], ["av"])
                    qk_group(1280, C_BK, bk, "bk", SUBS5)
                    wt, wk = load_w(ab_w_in[:, 1792:2304], 512)
                    for tt in range(18):
                        pb, pk = pr.next()
                        for k in range(8):
                            MM(pb[:, :], hy[:, k, tt * 128:(tt + 1) * 128], wt[:, k, 0:512], k == 0, k == 7, [wk, "hy"], [pk])
                        ACT(bv[:, tt, :], pb[:, :], AF.Copy, [pk], ["bv"])

                    srot = PsRot([0, 1])
                    orot = PsRot([2, 4])
                    drot = PsRot([3, 5])
                    groupsA = []
                    for qg in range(4):
                        n0 = qg * 4
                        kts = [(kt, kt - n0 + 1) for kt in range(max(0, n0 - 1), min(15, n0 + 4) + 1)] + [(16, None), (17, None)]
                        groupsA.append((qg * 512, 512, kts))
                    groupsA.append((NX, NCTX, [(16, None), (17, None)]))
                    for h in range(8):
                        g, c, p0 = h // 4, h // 2, (h % 2) * 64
                        for (qoff, nq, kts) in groupsA:
                            Ob, Ok = orot.next()
                            Db, Dk = drot.next()
                            def s_fn(i):
                                kt, mi = kts[i]
                                Sb, Sk = srot.next()
                                MM(Sb[:, :nq], akd[p0:p0 + 64, g, kt * 128:(kt + 1) * 128], aq[p0:p0 + 64, c, qoff:qoff + nq], True, True,
                                   [("akd", g), ("aq", c)], [Sk])
                                return Sb, Sk

                            def rest_fn(i, sctx):
                                Sb, Sk = sctx
                                kt, mi = kts[i]
                                Pt, Pk = prot.next()
                                ACT(Pt[:, :nq], Sb[:, :nq], AF.Exp, [Sk], [Pk], scale=0.125)
                                if mi is not None:
                                    TT("dve", Pt[:, :nq], Pt[:, :nq], maskA[:, mi, :nq], ALU.mult, [Pk, "maskA"], [Pk])
                                MM(Ob[0:64, :nq], av[:, kt, g * 64:(g + 1) * 64], Pt[:, :nq], i == 0, i == len(kts) - 1, ["av", Pk], [Ok])
                                MM(Db[0:64, :nq], cm[:, M_ONES, 0:64], Pt[:, :nq], i == 0, i == len(kts) - 1, ["cm", Pk], [Dk])
                            pipelined(len(kts), s_fn, rest_fn)
                            rd, rdk = rdrot.next()
                            TS("dve", rd[0:64, :nq], Db[0:64, :nq], esink[0:64, h:h + 1], ALU.add, [Dk, "esink"], [rdk])
                            RECIP(rd[0:64, :nq], rd[0:64, :nq], [rdk], [rdk])
                            TT("dve", hy[p0:p0 + 64, c, qoff:qoff + nq], Ob[0:64, :nq], rd[0:64, :nq], ALU.mult, [Ok, rdk], ["hy"])

                    allk = [16, 17] + list(range(16))
                    groupsB = [(qg * 512, 512, allk) for qg in range(4)] + [(NX, NCTX, [16, 17])]
                    for h in range(4):
                        for (qoff, nq, kts) in groupsB:
                            for m in range(2):
                                Ob, Ok = bank(2 + 2 * m)
                                Db, Dk = bank(3 + 2 * m)
                                def s_fn(i):
                                    kt = kts[i]
                                    Sb, Sk = srot.next()
                                    MM(Sb[:, :nq], bk[m * 64:(m + 1) * 64, h, kt * 128:(kt + 1) * 128], bq[m * 64:(m + 1) * 64, h, qoff:qoff + nq],
                                       True, True, [("bk", h), ("bq", h)], [Sk])
                                    return Sb, Sk

                                def rest_fn(i, sctx):
                                    Sb, Sk = sctx
                                    kt = kts[i]
                                    Pt, Pk = prot.next()
                                    ACT(Pt[:, :nq], Sb[:, :nq], AF.Exp, [Sk], [Pk], scale=0.125)
                                    MM(Ob[:, :nq], bv[:, kt, h * 128:(h + 1) * 128], Pt[:, :nq], i == 0, i == len(kts) - 1, ["bv", Pk], [Ok])
                                    MM(Db[:, :nq], cm[:, M_ONES, :], Pt[:, :nq], i == 0, i == len(kts) - 1, ["cm", Pk], [Dk])
                                pipelined(len(kts), s_fn, rest_fn)
                            r1, r1k = rdrot.next()
                            RECIP(r1[:, :nq], psb[3][:, :nq], [("ps", 3)], [r1k])
                            r2, r2k = rdrot.next()
                            RECIP(r2[:, :nq], psb[5][:, :nq], [("ps", 5)], [r2k])
                            TT("dve", ta[:, :nq], psb[2][:, :nq], r1[:, :nq], ALU.mult, [("ps", 2), r1k], ["ta"])
                            TT("dve", tb[:, :nq], psb[4][:, :nq], r2[:, :nq], ALU.mult, [("ps", 4), r2k], ["tb"])
                            STT("dve", yv[:, :nq], tb[:, :nq], lamt[:, 4:5], ta[:, :nq], ALU.mult, ALU.add, ["ta", "tb", "lamt"], ["yv"])
                            headnorm(R, PsRot([6, 7]), yv[:, :nq], "yv", 128, nq, lamt[:, 5:6], 1.0 / 128, cm[:, M_ONES, :],
                                     hy[:, 4 + h, qoff:qoff + nq], "hy")
                    P.barrier()
                out_proj(l, bi, ab_w_out, (lambda fc: hy[:, fc, :]), True)

        def mixer_cd(bi):
            l = 1
            with contextlib.ExitStack() as st0:
                hy = sbt(st0, "hy1", [128, 8, TOK], BF16)
                yD = sbt(st0, "yD", [128, 4, NX], BF16)
                compute_hall(l, bi, hy)
                srot = PsRot([0, 1])
                with contextlib.ExitStack() as st:
                    dq = sbt(st, "dq", [128, 4, NX], BF16)
                    dk = sbt(st, "dk", [128, 4, TOK], BF16)
                    vD = sbt(st, "vD", [128, 33, 512], BF16)
                    E2 = sbt(st, "E2", [128, 14, 512], BF16)
                    eg = sbt(st, "eg", [128, 2, 512], F32)
                    wr = Rot("w", [sbt(st, "dwr%d" % i, [128, 8, 512], BF16) for i in range(2)])
                    R = make_rots(st, "d")
                    prot = Rot("P", [sbt(st, "dP%d" % i, [128, 512], BF16) for i in range(3)])
                    rdrot = Rot("rd", [sbt(st, "drd%d" % i, [128, 512], F32) for i in range(2)])
                    P.barrier()
                    pr = PsRot([0, 1, 2, 3, 4, 5, 6, 7])
                    bd64 = cm[:, M_BD64, :]
                    for i in range(14):
                        s = i % 2
                        P.dma("sp", eg[:, s, :], rpbG_d[:, i, :], (), [("eg", s)], "eg%d" % s)
                        ACT(E2[:, i, :], eg[:, s, :], AF.Exp, [("eg", s)], ["E2"])

                    def load_w(src_ap, ncols):
                        wt, wk = wr.next()
                        P.dma("pool", wt[:, :, 0:ncols], src_ap.rearrange("(k p) f -> p k f", p=128), (), [wk], "w%d" % wk[1])
                        return wt, wk

                    wt, wk = load_w(cd_w_in[:, 768:1280], 512)
                    for (tok, n) in SUBS5[:4]:
                        for c in range(4):
                            pb, pk = proj_fm(pr, wt, wk, c * 128, 128, hy, tok, n)
                            headnorm(R, pr, pb[:, :n], pk, 128, n, cols[:, C_DQ:C_DQ + 1], 1.0 / 64, bd64, dq[:, c, tok:tok + n], ("dq", c))
                    wt, wk = load_w(cd_w_in[:, 1568:2080], 512)
                    for (tok, n) in SUBS5:
                        for c in range(4):
                            pb, pk = proj_fm(pr, wt, wk, c * 128, 128, hy, tok, n)
                            headnorm(R, pr, pb[:, :n], pk, 128, n, cols[:, C_DK:C_DK + 1], 1.0 / 64, bd64, dk[:, c, tok:tok + n], ("dk", c))
                    wt, wk = load_w(cd_w_in[:, 2080:2592], 512)

                    def wtok(w):
                        return 64 * w if w < 31 else NX + 128 * (w - 31)
                    for w in range(33):
                        pb, pk = pr.next()
                        t0 = wtok(w)
                        for k in range(8):
                            MM(pb[:, :], hy[:, k, t0:t0 + 128], wt[:, k, 0:512], k == 0, k == 7, [wk, "hy"], [pk])
                        if w % 2 == 0:
                            ACT(vD[:, w, :], pb[:, :], AF.Copy, [pk], ["vD"])
                        else:
                            COPY("dve", vD[:, w, :], pb[:, :], [pk], ["vD"])
                    orot = PsRot([2, 4])
                    drot = PsRot([3, 5])
                    srotE = PsRot([0, 6])
                    srotO = PsRot([1, 7])
                    for r in range(0 if (KSUB & 1) else 32):
                        rs_ = min(max(r - 4, 0), 24)
                        units = [(rs_ + 2 * i, rs_ + 2 * i - r + 7) for i in range(4)] + [(31, None), (32, None)]
                        Ob, Ok = orot.next()
                        Db, Dk = drot.next()
                        def s_fn(ui):
                            w, ei = units[ui]
                            t0 = wtok(w)
                            SbE, SkE = srotE.next()
                            SbO, SkO = srotO.next()
                            for h in range(8):
                                c, p0 = h // 2, (h % 2) * 64
                                Sb, Sk = (SbE, SkE) if h % 2 == 0 else (SbO, SkO)
                                MM(Sb[:, c * 64:(c + 1) * 64], dk[p0:p0 + 64, c, t0:t0 + 128], dq[p0:p0 + 64, c, r * 64:(r + 1) * 64],
                                   c == 0, c == 3, [("dk", c), ("dq", c)], [Sk], skip_group_check=True)
                            return SbE, SkE, SbO, SkO

                        def rest_fn(ui, sctx):
                            SbE, SkE, SbO, SkO = sctx
                            w, ei = units[ui]
                            Pt, Pk = prot.next()
                            PtV = Pt[:, :].rearrange("p (c two q) -> p c two q", two=2, q=64)
                            ACT(PtV[:, :, 0, :], SbE[:, 0:256].rearrange("p (c q) -> p c q", q=64), AF.Exp, [SkE], [Pk], scale=0.125)
                            ACT(PtV[:, :, 1, :], SbO[:, 0:256].rearrange("p (c q) -> p c q", q=64), AF.Exp, [SkO, Pk], [Pk], scale=0.125)
                            if ei is not None:
                                TT("dve", Pt[:, :], Pt[:, :], E2[:, ei, :], ALU.mult, [Pk, "E2"], [Pk])
                            for h in range(8):
                                MM(Ob[0:64, h * 64:(h + 1) * 64], vD[:, w, h * 64:(h + 1) * 64], Pt[:, h * 64:(h + 1) * 64],
                                   ui == 0 and h == 0, ui == len(units) - 1 and h == 7, ["vD", Pk], [Ok], skip_group_check=True)
                            MM(Db[0:64, :], cm[:, M_ONES, 0:64], Pt[:, :], ui == 0, ui == len(units) - 1, ["cm", Pk], [Dk])
                        pipelined(len(units), s_fn, rest_fn)
                        rd, rdk = rdrot.next()
                        RECIP(rd[0:64, :], Db[0:64, :], [Dk], [rdk])
                        for h in range(8):
                            c, p0 = h // 2, (h % 2) * 64
                            TT("dve", yD[p0:p0 + 64, c, r * 64:(r + 1) * 64], Ob[0:64, h * 64:(h + 1) * 64], rd[0:64, h * 64:(h + 1) * 64],
                               ALU.mult, [Ok, rdk], ["yD"])
                    P.barrier()
                with contextlib.ExitStack() as st:
                    cqg = sbt(st, "cqg", [128, 6, NX], BF16)
                    rsa = sbt(st, "rsa", [128, NX], F32)
                    ckvn = sbt(st, "ckvn", [128, 2, TOK], BF16)
                    krope = sbt(st, "krope", [32, TOK], BF16)
                    rC = sbt(st, "rCm", [128, NX], F32)
                    rS = sbt(st, "rSm", [128, NX], F32)
                    vC = sbt(st, "vC", [128, 18, 512], BF16)
                    qcat = sbt(st, "qcat", [96, NX], BF16)
                    kcat = sbt(st, "kcat", [96, TOK], BF16)
                    wr = Rot("w", [sbt(st, "cwr%d" % i, [128, 8, 512], BF16) for i in range(2)])
                    wqr = Rot("wq", [sbt(st, "wq%d" % i, [128, 6, 96], BF16) for i in range(2)])
                    wkr = Rot("wk", [sbt(st, "wk%d" % i, [128, 2, 64], BF16) for i in range(2)])
                    R = make_rots(st, "c")
                    prot = Rot("P", [sbt(st, "cP%d" % i, [128, 512], BF16) for i in range(3)])
                    rdrot = Rot("rd", [sbt(st, "crd%d" % i, [128, 512], F32) for i in range(2)])
                    P.barrier()
                    P.dma("sp", rC[:], ropeM_d[0], (), ["rope"], "rC")
                    P.dma("sp", rS[:], ropeM_d[1], (), ["rope"], "rS")
                    pr = PsRot([2, 3, 4, 5, 6, 7])

                    def load_w(src_ap, ncols):
                        wt, wk = wr.next()
                        P.dma("pool", wt[:, :, 0:ncols], src_ap.rearrange("(k p) f -> p k f", p=128), (), [wk], "w%d" % wk[1])
                        return wt, wk

                    wa, wak = load_w(cd_w_in[:, 0:512], 512)
                    wb, wbk = load_w(cd_w_in[:, 512:768], 256)
                    for (tok, n) in SUBS5[:4]:
                        ssb, ssk = bank(0 + (tok // 512) % 2)
                        for c in range(6):
                            wt, wk, cc = (wa, wak, c) if c < 4 else (wb, wbk, c - 4)
                            pb, pk = proj_fm(pr, wt, wk, cc * 128, 128, hy, tok, n)
                            sq, sqk = R["sq"].next()
                            ACT(sq[:, :n], pb[:, :n], AF.Square, [pk], [sqk])
                            MM(ssb[:, :n], cm[:, M_ONES, :], sq[:, :n], c == 0, c == 5, [sqk, "cm"], [ssk])
                            TS("dve", cqg[:, c, tok:tok + n], pb[:, :n], cols[:, C_QA + c:C_QA + c + 1], ALU.mult, [pk, "cols", sqk], ["cqg"])
                        ACT(rsa[:, tok:tok + n], ssb[:, :n], AF.Sqrt, [ssk], ["rsa"], bias=EPS, scale=1.0 / 768)
                        RECIP(rsa[:, tok:tok + n], rsa[:, tok:tok + n], ["rsa"], ["rsa"])
                    wt, wk = load_w(cd_w_in[:, 1280:1568], 288)
                    rope32 = (cm[0:32, M_PERM32, 0:32], rC, rS, 0, 32)
                    for (tok, n) in SUBS5:
                        ssb, ssk = bank(0 + (tok // 512) % 2)
                        pbs = []
                        for c in range(2):
                            pb, pk = proj_fm(pr, wt, wk, c * 128, 128, hy, tok, n)
                            pbs.append((pb, pk))
                            sq, sqk = R["sq"].next()
                            ACT(sq[:, :n], pb[:, :n], AF.Square, [pk], [sqk])
                            MM(ssb[:, :n], cm[:, M_ONES, :], sq[:, :n], c == 0, c == 1, [sqk, "cm"], [ssk])
                        rs, rsk = R["rstd"].next()
                        ACT(rs[:, :n], ssb[:, :n], AF.Sqrt, [ssk], [rsk], bias=EPS, scale=1.0 / 256)
                        RECIP(rs[:, :n], rs[:, :n], [rsk], [rsk])
                        for c in range(2):
                            pb, pk = pbs[c]
                            STT("dve", ckvn[:, c, tok:tok + n], pb[:, :n], cols[:, C_KVA + c:C_KVA + c + 1], rs[:, :n], ALU.mult, ALU.mult,
                                [pk, rsk, "cols"], ["ckvn"])
                        pb, pk = proj_fm(pr, wt, wk, 256, 32, hy, tok, n)
                        headnorm(R, pr, pb[0:32, :n], pk, 32, n, cols[0:32, C_KR:C_KR + 1], 1.0 / 32, cm[0:32, M_ONES, 0:32],
                                 krope[0:32, tok:tok + n], "krope", rope=(rope32 if tok < NX else None), tok=tok)
                    wt, wk = wr.next()
                    for kk in range(2):
                        P.dma("pool", wt[:, kk, 0:512].rearrange("p (h e) -> p h e", e=64),
                              c_w_ukv[kk * 128:(kk + 1) * 128, :].rearrange("p (h e) -> p h e", e=128)[:, :, 64:128], (), [wk], "w%d" % wk[1])
                    for tt in range(18):
                        pb, pk = pr.next()
                        for c in range(2):
                            MM(pb[:, :], ckvn[:, c, tt * 128:(tt + 1) * 128], wt[:, c, 0:512], c == 0, c == 1, [wk, "ckvn"], [pk])
                        ACT(vC[:, tt, :], pb[:, :], AF.Copy, [pk], ["vC"])
                    orot = PsRot([2, 4])
                    drot = PsRot([3, 5])
                    pr2 = PsRot([6, 7])
                    allk = [16, 17] + list(range(16))
                    sc = 96.0 ** -0.5
                    for h in range(0 if (KSUB & 2) else 8):
                        c_, p0 = h // 2, (h % 2) * 64
                        wq, wqk = wqr.next()
                        P.dma("pool", wq[:], c_w_uq[:, h * 96:(h + 1) * 96].rearrange("(k p) f -> p k f", p=128), (), [wqk], "wq%d" % wqk[1])
                        wkk, wkkk = wkr.next()
                        P.dma("pool", wkk[:], c_w_ukv[:, h * 128:h * 128 + 64].rearrange("(k p) f -> p k f", p=128), (), [wkkk], "wk%d" % wkkk[1])
                        for (tok, n) in SUBS5[:4]:
                            pb, pk = pr2.next()
                            for c in range(6):
                                MM(pb[0:96, :n], wq[:, c, :], cqg[:, c, tok:tok + n], c == 0, c == 5, [wqk, "cqg"], [pk])
                            qs, qsk = R["tmp"].next()
                            TT("dve", qs[0:96, :n], pb[0:96, :n], rsa[0:96, tok:tok + n], ALU.mult, [pk, "rsa"], [qsk])
                            headnorm(R, pr2, qs[0:96, :n], qsk, 96, n, cols[0:96, C_Q96:C_Q96 + 1], cols[0:96, C_INV96:C_INV96 + 1],
                                     cm[0:96, M_BD96, 0:96], qcat[0:96, tok:tok + n], "qcat",
                                     rope=(cm[0:96, M_PERM96, 0:96], rC, rS, 64, 96), tok=tok)
                        for (tok, n) in SUBS5:
                            pb, pk = pr2.next()
                            for c in range(2):
                                MM(pb[0:64, :n], wkk[:, c, :], ckvn[:, c, tok:tok + n], c == 0, c == 1, [wkkk, "ckvn"], [pk])
                            headnorm(R, pr2, pb[0:64, :n], pk, 64, n, cols[0:64, C_KN:C_KN + 1], 1.0 / 64, cm[0:64, M_ONES, 0:64],
                                     kcat[0:64, tok:tok + n], "kcat")
                        P.dma("sp", kcat[64:96, :], krope[0:32, :], ["krope"], ["kcat"], "kcr")
                        for qg in range(4):
                            qoff = qg * 512
                            Ob, Ok = orot.next()
                            Db, Dk = drot.next()
                            def s_fn(i):
                                kt = allk[i]
                                Sb, Sk = srot.next()
                                MM(Sb[:, :], kcat[0:96, kt * 128:(kt + 1) * 128], qcat[0:96, qoff:qoff + 512], True, True, ["kcat", "qcat"], [Sk])
                                return Sb, Sk

                            def rest_fn(i, sctx):
                                Sb, Sk = sctx
                                kt = allk[i]
                                Pt, Pk = prot.next()
                                ACT(Pt[:, :], Sb[:, :], AF.Exp, [Sk], [Pk], scale=sc)
                                MM(Ob[0:64, :], vC[:, kt, h * 64:(h + 1) * 64], Pt[:, :], i == 0, i == 17, ["vC", Pk], [Ok])
                                MM(Db[0:64, :], cm[:, M_ONES, 0:64], Pt[:, :], i == 0, i == 17, ["cm", Pk], [Dk])
                            pipelined(18, s_fn, rest_fn)
                            rd, rdk = rdrot.next()
                            RECIP(rd[0:64, :], Db[0:64, :], [Dk], [rdk])
                            TT("dve", hy[p0:p0 + 64, c_, qoff:qoff + 512], Ob[0:64, :], rd[0:64, :], ALU.mult, [Ok, rdk], ["hy"])
                    P.barrier()
                out_proj(l, bi, cd_w_out, (lambda fc: hy[:, fc, :] if fc < 4 else yD[:, fc - 4, :]), False)

        for bi in range(2):
            ffn(0, 0, bi, "in", "xs" if stage > 1 else "out", True)
            if stage <= 1:
                continue
            mixer_ab(bi)
            if stage <= 2:
                _dump(P, nc, top, xs, outT, bi, final_ops)
                continue
            ffn(0, 1, bi, "xs", "xs" if stage > 3 else "out", True)
            if stage <= 3:
                continue
            ffn(1, 0, bi, "xs", "xs" if stage > 4 else "out", True)
            if stage <= 4:
                continue
            mixer_cd(bi)
            if stage <= 5:
                _dump(P, nc, top, xs, outT, bi, final_ops)
                continue
            ffn(1, 1, bi, "xs", "out", False)
        P.emit(final_ops)
    return nc


def _dump(P, nc, top, xs, outT, bi, final_ops):
    P.barrier()
    op = P.dma("sp", outT[bi], xs[:, :, 0:NX], [("xs", 0), ("xs", 1), ("xs", 2)], [("out", bi, "d")], "dump")
    final_ops.append(op)
    P.barrier()


def _rope_tables(rot_dim):
    nf = rot_dim // 4
    inv = (10000.0 ** (-np.arange(nf, dtype=np.float64) / nf))
    t = np.arange(NX)
    row = (t // 64).astype(np.float64)
    col = (t % 64).astype(np.float64)
    C = np.zeros((rot_dim, NX), np.float64)
    S = np.zeros((rot_dim, NX), np.float64)
    for d in range(rot_dim):
        axis = d // (2 * nf)
        half = (d % (2 * nf)) // nf
        f = d % nf
        ang = (row if axis == 0 else col) * inv[f]
        C[d] = np.cos(ang)
        S[d] = np.sin(ang) * (-1.0 if half == 0 else 1.0)
    return C.astype(np.float32), S.astype(np.float32)


def _perm(rot_dim):
    nf = rot_dim // 4
    Pm = np.zeros((rot_dim, rot_dim), np.float32)
    for d in range(rot_dim):
        half = (d % (2 * nf)) // nf
        partner = d + nf if half == 0 else d - nf
        Pm[partner, d] = 1.0
    return Pm


def _constants():
    cmat = np.zeros((128, 6, 128), np.float32)
    cmat[:, M_ONES, :] = 1.0
    cmat[0:64, M_BD64, 0:64] = 1.0
    cmat[64:128, M_BD64, 64:128] = 1.0
    p64 = _perm(64)
    cmat[0:64, M_PERM64, 0:64] = p64
    cmat[64:128, M_PERM64, 64:128] = p64
    cmat[0:64, M_BD96, 0:64] = 1.0
    cmat[64:96, M_BD96, 64:96] = 1.0
    p32 = _perm(32)
    cmat[0:64, M_PERM96, 0:64] = np.eye(64, dtype=np.float32)
    cmat[64:96, M_PERM96, 64:96] = p32
    cmat[0:32, M_PERM32, 0:32] = p32
    C64, S64 = _rope_tables(64)
    rope64 = np.stack([np.concatenate([C64, C64]), np.concatenate([S64, S64])]).astype(np.float32)
    C32, S32 = _rope_tables(32)
    ropeM = np.zeros((2, 128, NX), np.float32)
    ropeM[0] = 1.0
    ropeM[0, 0:32] = C32
    ropeM[1, 0:32] = S32
    ropeM[0, 64:96] = C32
    ropeM[1, 64:96] = S32
    maskA = np.zeros((128, 6, 512), np.float32)
    jj = np.arange(128)[:, None]
    qq = np.arange(512)[None, :]
    b = qq // 128
    ii = qq % 128
    for mi in range(6):
        d = mi - 1 - b
        maskA[:, mi, :] = (np.abs(ii - jj - 128 * d) <= 128).astype(np.float32)
    return cmat, rope64, ropeM, maskA


def _rpb_gather(rpb):
    kc = np.arange(64)[:, None]
    qc = np.arange(64)[None, :]
    dc = np.clip(kc - qc, -15, 15) + 15
    cs = np.clip(qc - 8, 0, 48)
    valid = (kc >= cs) & (kc < cs + 16)
    G = np.empty((2, 64, 14, 8, 64), np.float32)
    for rr in range(2):
        for i in range(14):
            g = rpb[:, i + rr, :][:, dc]
            g = np.where(valid[None], g, np.float32(-100.0))
            G[rr, :, i, :, :] = np.transpose(g, (1, 0, 2))
    return np.ascontiguousarray(G.reshape(128, 14, 512))


def _prep(inp):
    f = lambda a: np.ascontiguousarray(np.asarray(a, dtype=np.float32))
    cmat, rope64, ropeM, maskA = _constants()
    shared = {
        "w_mod": f(inp["w_mod"]),
        "b_modT": f(np.transpose(np.asarray(inp["b_mod"]).reshape(2, 72, 128), (2, 0, 1))),
        "f1g": f(inp["ffn1_w_gate"]), "f1u": f(inp["ffn1_w_up"]), "f1d": f(inp["ffn1_w_down"]),
        "f2g": f(inp["ffn2_w_gate"]), "f2u": f(inp["ffn2_w_up"]), "f2d": f(inp["ffn2_w_down"]),
        "ab_w_in": f(inp["ab_w_in"][0]), "ab_w_out": f(inp["ab_w_out"][0]),
        "cd_w_in": f(inp["cd_w_in"][0]), "cd_w_out": f(inp["cd_w_out"][0]),
        "c_w_uq": f(inp["c_w_uq"][0]), "c_w_ukv": f(inp["c_w_ukv"][0]),
        "cmat": cmat, "rope64": rope64, "ropeM": ropeM, "maskA": maskA,
        "rpbG": _rpb_gather(np.asarray(inp["d_rpb"][0], dtype=np.float32)),
    }
    cols = np.zeros((128, NCOL), np.float32)
    g = lambda k: np.asarray(inp[k][0], dtype=np.float32)
    cols[:, C_AQ] = np.tile(g("a_q_norm"), 2)
    cols[:, C_AK] = np.tile(g("a_k_norm"), 2)
    cols[:, C_BQ] = np.tile(g("b_q_norm"), 2)
    cols[:, C_BK] = np.tile(g("b_k_norm"), 2)
    cols[:, C_BSUB] = g("b_sub_norm")
    cols[:, C_SINK:C_SINK + 8] = g("a_sink")[None, :]
    cols[0:64, C_LQ1] = g("b_lambda_q1")
    cols[0:64, C_LK1] = g("b_lambda_k1")
    cols[0:64, C_LQ2] = g("b_lambda_q2")
    cols[0:64, C_LK2] = g("b_lambda_k2")
    cols[:, C_QA:C_QA + 6] = g("c_q_a_norm").reshape(6, 128).T
    cols[:, C_KVA:C_KVA + 2] = g("c_kv_a_norm").reshape(2, 128).T
    cols[:, C_Q96] = 1.0
    cols[0:64, C_Q96] = g("c_q_nope_norm")
    cols[64:96, C_Q96] = g("c_q_rope_norm")
    cols[:, C_INV96] = 1.0 / 64
    cols[64:96, C_INV96] = 1.0 / 32
    cols[:, C_KN] = np.tile(g("c_k_nope_norm"), 2)
    cols[:, C_KR] = np.tile(g("c_k_rope_norm"), 4)
    cols[:, C_DQ] = np.tile(g("d_q_norm"), 2)
    cols[:, C_DK] = np.tile(g("d_k_norm"), 2)
    shared["cols"] = cols
    x = np.asarray(inp["x"], dtype=np.float32)
    ctx = np.asarray(inp["ctx"], dtype=np.float32)
    c = np.asarray(inp["c"], dtype=np.float32)
    c_ctx = np.asarray(inp["c_ctx"], dtype=np.float32)
    in_maps = []
    for core in range(8):
        b0 = 2 * core
        m = dict(shared)
        m["xT"] = np.ascontiguousarray(np.transpose(x[b0:b0 + 2], (0, 2, 1)).reshape(2, 8, 128, NX))
        m["ctxT"] = np.ascontiguousarray(np.transpose(ctx[b0:b0 + 2], (0, 2, 1)).reshape(2, 8, 128, NCTX))
        cc = np.stack([c[b0], c[b0 + 1], c_ctx], axis=-1)
        m["cT"] = np.ascontiguousarray(np.transpose(cc.reshape(8, 128, 3), (1, 0, 2)))
        in_maps.append(m)
    return in_maps


def kernel(**inputs):
    stage = int(os.environ.get("KSTAGE", "99"))
    ncores = int(os.environ.get("KCORES", "8"))
    in_maps = _prep(inputs)
    nc = build(stage)
    res = run_bass_kernel_spmd(nc, in_maps[:ncores], core_ids=list(range(ncores)))
    out = np.zeros((16, NX, 1024), np.float32)
    for core in range(ncores):
        o = np.asarray(res.results[core]["outT"]).reshape(2, 1024, NX)
        out[2 * core:2 * core + 2] = np.transpose(o, (0, 2, 1))
    return out
```

```python
import contextlib
import math
import os
import numpy as np
import concourse.bass as bass
import concourse.mybir as mybir
from concourse.bass_utils import run_bass_kernel_spmd

F32 = mybir.dt.float32
BF16 = mybir.dt.bfloat16
AF = mybir.ActivationFunctionType
ALU = mybir.AluOpType
EPS = 1e-6


class Op:
    __slots__ = ("eng", "fn", "waits", "dwaits", "inc", "semval", "dma")

    def __init__(self, eng, fn):
        self.eng = eng
        self.fn = fn
        self.waits = []
        self.dwaits = []
        self.inc = False
        self.semval = None
        self.dma = None


class Prog:
    ENGS = ("pe", "act", "dve", "pool", "sp")

    def __init__(self, nc):
        self.nc = nc
        self.ops = {e: [] for e in self.ENGS}
        self.last_write = {}
        self.readers = {}
        self.dma_sem_of = {}
        self.dma_sem_cnt = []
        self.pending = {}

    def dma_sem(self, name):
        if name not in self.dma_sem_of:
            self.dma_sem_of[name] = len(self.dma_sem_cnt)
            self.dma_sem_cnt.append(0)
        return self.dma_sem_of[name]

    def _dep(self, op, prod):
        if prod is None or prod is op:
            return
        if prod.dma is not None:
            op.dwaits.append(prod.dma)
        elif prod.eng != op.eng or prod.eng != "pe":
            op.waits.append(prod)
            prod.inc = True

    def barrier(self):
        snap = []
        for e in self.ENGS:
            if self.ops[e]:
                o = self.ops[e][-1]
                if o.dma is None:
                    o.inc = True
                    snap.append(o)
        dsnap = [(i, c) for i, c in enumerate(self.dma_sem_cnt) if c > 0]
        self.pending = {e: (snap, dsnap) for e in self.ENGS}

    def add(self, eng, fn, reads=(), writes=(), dma_sem=None):
        op = Op(eng, fn)
        pend = self.pending.pop(eng, None)
        if pend is not None:
            for o in pend[0]:
                if o.eng != eng:
                    op.waits.append(o)
            op.dwaits.extend(pend[1])
        for k in reads:
            self._dep(op, self.last_write.get(k))
        for k in writes:
            self._dep(op, self.last_write.get(k))
            for r in self.readers.get(k, {}).values():
                self._dep(op, r)
        if dma_sem is not None:
            si = self.dma_sem(dma_sem)
            self.dma_sem_cnt[si] += 16
            op.dma = (si, self.dma_sem_cnt[si])
        for k in reads:
            rk = (eng, op.dma[0]) if op.dma is not None else eng
            self.readers.setdefault(k, {})[rk] = op
        for k in writes:
            self.last_write[k] = op
            self.readers[k] = {}
        self.ops[eng].append(op)
        return op

    def dma(self, q, out, in_, reads=(), writes=(), sem=None):
        return self.add(q, lambda e: e.dma_start(out=out, in_=in_), reads, writes, dma_sem=sem)

    def emit(self, final_ops=()):
        nc = self.nc
        for e in self.ENGS:
            c = 0
            for op in self.ops[e]:
                if op.inc and op.dma is None:
                    c += 1
                    op.semval = c
        with contextlib.ExitStack() as st:
            esem = {e: st.enter_context(nc.semaphore("s_" + e)) for e in self.ENGS}
            dsem = [st.enter_context(nc.semaphore("d_%d" % i)) for i in range(len(self.dma_sem_cnt))]
            block = st.enter_context(nc.Block())
            prog = self

            def run(e, eng):
                waited = {f: 0 for f in prog.ENGS}
                dwaited = {}
                for op in prog.ops[e]:
                    need = {}
                    for w in op.waits:
                        if w.semval > waited[w.eng] and w.semval > need.get(w.eng, 0):
                            need[w.eng] = w.semval
                    for f, v in need.items():
                        eng.wait_ge(esem[f], v)
                        waited[f] = v
                    dneed = {}
                    for si, v in op.dwaits:
                        if v > dwaited.get(si, 0) and v > dneed.get(si, 0):
                            dneed[si] = v
                    for si, v in dneed.items():
                        eng.wait_ge(dsem[si], v)
                        dwaited[si] = v
                    ins = op.fn(eng)
                    if op.dma is not None:
                        ins.then_inc(dsem[op.dma[0]], 16)
                    elif op.inc:
                        ins.then_inc(esem[e], 1)
                if e == "sp":
                    for op in final_ops:
                        si, v = op.dma
                        if v > dwaited.get(si, 0):
                            eng.wait_ge(dsem[si], v)
                            dwaited[si] = v

            block.tensor(lambda eng: run("pe", eng))
            block.scalar(lambda eng: run("act", eng))
            block.vector(lambda eng: run("dve", eng))
            block.gpsimd(lambda eng: run("pool", eng))
            block.sync(lambda eng: run("sp", eng))


class Rot:
    def __init__(self, name, tensors):
        self.name = name
        self.t = tensors
        self.i = 0

    def next(self):
        s = self.i % len(self.t)
        self.i += 1
        return self.t[s], (self.name, s)


NX = 2048
NCTX = 256
TOK = NX + NCTX
NFF = 22
TILES = [(0, 768), (768, 768), (1536, 768)]
SUBS5 = [(0, 512), (512, 512), (1024, 512), (1536, 512), (2048, 256)]
NCOL = 32
KSUB = int(os.environ.get('KSUB', '0'))
LAM_INIT0 = 0.8 - 0.6 * math.exp(-0.3 * 0)

C_AQ, C_AK, C_BQ, C_BK, C_BSUB, C_SINK, C_LQ1, C_LK1, C_LQ2, C_LK2 = 0, 1, 2, 3, 4, 5, 13, 14, 15, 16
C_QA, C_KVA, C_Q96, C_INV96, C_KN, C_KR, C_DQ, C_DK = 17, 23, 25, 26, 27, 28, 29, 30
M_ONES, M_BD64, M_PERM64, M_BD96, M_PERM96, M_PERM32 = 0, 1, 2, 3, 4, 5


def build(stage=99):
    nc = bass.Bass("TRN2", target_bir_lowering=False)

    def din(name, shape):
        return nc.dram_tensor(name, list(shape), F32, kind="ExternalInput").ap()

    xT = din("xT", [2, 8, 128, NX])
    ctxT = din("ctxT", [2, 8, 128, NCTX])
    cT = din("cT", [128, 8, 3])
    w_mod = din("w_mod", [2, 1024, 9216])
    b_modT = din("b_modT", [128, 2, 72])
    fgw = [din("f1g", [2, 1024, 2816]), din("f2g", [2, 1024, 2816])]
    fuw = [din("f1u", [2, 1024, 2816]), din("f2u", [2, 1024, 2816])]
    fdw = [din("f1d", [2, 2816, 1024]), din("f2d", [2, 2816, 1024])]
    ab_w_in = din("ab_w_in", [1024, 2304])
    ab_w_out = din("ab_w_out", [1024, 1024])
    cd_w_in = din("cd_w_in", [1024, 2592])
    cd_w_out = din("cd_w_out", [1024, 1024])
    c_w_uq = din("c_w_uq", [768, 768])
    c_w_ukv = din("c_w_ukv", [256, 1024])
    cols_d = din("cols", [128, NCOL])
    cmat_d = din("cmat", [128, 6, 128])
    rope64_d = din("rope64", [2, 128, NX])
    ropeM_d = din("ropeM", [2, 128, NX])
    maskA_d = din("maskA", [128, 6, 512])
    rpbG_d = din("rpbG", [128, 14, 512])
    outT = nc.dram_tensor("outT", [2, 8, 128, NX], F32, kind="ExternalOutput").ap()
    xs = nc.dram_tensor("xs_scratch", [8, 128, TOK], F32).ap()
    wc_g = [[nc.dram_tensor("wcg%d%d" % (w, l), [11, 128, 8, 256], BF16).ap() for l in range(2)] for w in range(2)]
    wc_u = [[nc.dram_tensor("wcu%d%d" % (w, l), [11, 128, 8, 256], BF16).ap() for l in range(2)] for w in range(2)]
    wc_d = [[nc.dram_tensor("wcd%d%d" % (w, l), [2, 4, 128, 6, 512], BF16).ap() for l in range(2)] for w in range(2)]

    P = Prog(nc)
    final_ops = []

    def ACT(out, in_, func, r, w, **kw):
        P.add("act", lambda e: e.activation(out=out, in_=in_, func=func, **kw), r, w)

    def MM(out, lhsT, rhs, start, stop, r, w, **kw):
        P.add("pe", lambda e: e.matmul(out, lhsT, rhs, start=start, stop=stop, **kw), r, w)

    def TT(eng, out, in0, in1, op, r, w):
        P.add(eng, lambda e: e.tensor_tensor(out=out, in0=in0, in1=in1, op=op), r, w)

    def TS(eng, out, in0, s1, op0, r, w):
        P.add(eng, lambda e: e.tensor_scalar(out=out, in0=in0, scalar1=s1, scalar2=None, op0=op0), r, w)

    def STT(eng, out, in0, scalar, in1, op0, op1, r, w):
        P.add(eng, lambda e: e.scalar_tensor_tensor(out=out, in0=in0, scalar=scalar, in1=in1, op0=op0, op1=op1), r, w)

    def RECIP(out, in_, r, w):
        P.add("dve", lambda e: e.reciprocal(out=out, in_=in_), r, w)

    def COPY(eng, out, in_, r, w):
        P.add(eng, lambda e: e.tensor_copy(out=out, in_=in_), r, w)

    with contextlib.ExitStack() as top:
        uid = [0]

        def sbt(st, name, shape, dt):
            uid[0] += 1
            return st.enter_context(nc.sbuf_tensor("sb_%s_%d" % (name, uid[0]), list(shape), dt))

        psb = [top.enter_context(nc.psum_tensor("ps%d" % i, [128, 512], F32)) for i in range(8)]

        def bank(i):
            return psb[i], ("ps", i)

        c_sb = sbt(top, "c_sb", [128, 8, 3], F32)
        c_act = sbt(top, "c_act", [128, 8, 3], BF16)
        bm = sbt(top, "bm", [128, 2, 72], F32)
        cols = sbt(top, "cols", [128, NCOL], F32)
        cm = sbt(top, "cm", [128, 6, 128], BF16)
        ones_f = sbt(top, "ones_f", [128, 128], F32)
        modT = sbt(top, "modT", [128, 2, 72, 3], F32)
        esink = sbt(top, "esink", [128, 8], F32)
        lamt = sbt(top, "lamt", [128, 8], F32)

        P.dma("sp", c_sb[:], cT, writes=["c_sb"], sem="c")
        P.dma("sp", bm[:], b_modT, writes=["bm"], sem="bm")
        P.dma("sp", cols[:], cols_d, writes=["cols"], sem="cols")
        P.dma("pool", cm[:], cmat_d, writes=["cm"], sem="cm")
        P.add("pool", lambda e: e.memset(ones_f[:], 1.0), (), ["ones_f"])
        ACT(c_act[:], c_sb[:], AF.Silu, ["c_sb"], ["c_act"])

        def mcol(l, i, c, j):
            return modT[:, l, 8 * i + c, j:j + 1]

        with contextlib.ExitStack() as st:
            wm = [sbt(st, "wm%d" % i, [128, 8, 512], BF16) for i in range(2)]
            it = 0
            for l in range(2):
                for sl in range(18):
                    s = it % 2
                    P.dma("pool", wm[s][:], w_mod[l][:, sl * 512:(sl + 1) * 512].rearrange("(k p) f -> p k f", p=128),
                          writes=[("wm", s)], sem="wm%d" % s)
                    pb, pk = bank(it % 2)
                    for cc in range(4):
                        for k in range(8):
                            MM(pb[:, cc * 4:cc * 4 + 3], wm[s][:, k, cc * 128:(cc + 1) * 128], c_act[:, k, :],
                               k == 0, k == 7, [("wm", s), "c_act"], [pk])
                    for cc in range(4):
                        col = sl * 4 + cc
                        TS("dve", modT[:, l, col, :], pb[:, cc * 4:cc * 4 + 3], bm[:, l, col:col + 1], ALU.add,
                           [pk, "bm"], ["modT"])
                    it += 1
            for l in range(2):
                for i in (1, 4, 7):
                    TS("dve", modT[:, l, 8 * i:8 * i + 8, :], modT[:, l, 8 * i:8 * i + 8, :], 1.0, ALU.add, ["modT"], ["modT"])
                for i in (2, 8):
                    TS("dve", modT[:, l, 8 * i:8 * i + 8, :], modT[:, l, 8 * i:8 * i + 8, :], 0.5, ALU.mult, ["modT"], ["modT"])
            ACT(esink[:], cols[:, C_SINK:C_SINK + 8], AF.Exp, ["cols"], ["esink"])
            TT("dve", lamt[:, 0:1], cols[:, C_LQ1:C_LQ1 + 1], cols[:, C_LK1:C_LK1 + 1], ALU.mult, ["cols"], ["lamt"])
            TT("dve", lamt[:, 1:2], cols[:, C_LQ2:C_LQ2 + 1], cols[:, C_LK2:C_LK2 + 1], ALU.mult, ["cols"], ["lamt"])
            pb, pk = bank(2)
            MM(pb[:, 0:2], ones_f[:], lamt[:, 0:2], True, True, ["ones_f", "lamt"], [pk])
            ACT(lamt[:, 2:4], pb[:, 0:2], AF.Exp, [pk], ["lamt"])
            TT("dve", lamt[:, 4:5], lamt[:, 3:4], lamt[:, 2:3], ALU.subtract, ["lamt"], ["lamt"])
            TS("dve", lamt[:, 4:5], lamt[:, 4:5], -LAM_INIT0, ALU.add, ["lamt"], ["lamt"])
            TS("dve", lamt[:, 5:6], cols[:, C_BSUB:C_BSUB + 1], 1.0 - LAM_INIT0, ALU.mult, ["cols", "lamt"], ["lamt"])
            P.barrier()

        def load_x(xrot, bi, ti, src):
            t, key = xrot.next()
            off = TILES[ti][0]
            sem = "x%d" % key[1]
            if src == "in":
                if ti < 2:
                    P.dma("sp", t[:, :, :], xT[bi][:, :, off:off + 768].rearrange("c p t -> p c t"), (), [key], sem)
                else:
                    P.dma("sp", t[:, :, 0:512], xT[bi][:, :, 1536:2048].rearrange("c p t -> p c t"), (), [key], sem)
                    P.dma("sp", t[:, :, 512:768], ctxT[bi].rearrange("c p t -> p c t"), (), [key], sem)
            else:
                P.dma("sp", t[:, :, :], xs[:, :, off:off + 768].rearrange("c p t -> p c t"), [("xs", ti)], [key], sem)
            return t, key

        def store_x(t, key, bi, ti, dst):
            off = TILES[ti][0]
            if dst == "out":
                if ti < 2:
                    op = P.dma("sp", outT[bi][:, :, off:off + 768].rearrange("c p t -> p c t"), t[:, :, :], [key], [("out", bi, ti)], "o%d" % key[1])
                else:
                    op = P.dma("sp", outT[bi][:, :, 1536:2048].rearrange("c p t -> p c t"), t[:, :, 0:512], [key], [("out", bi, ti)], "o%d" % key[1])
                final_ops.append(op)
            else:
                P.dma("sp", xs[:, :, off:off + 768].rearrange("c p t -> p c t"), t[:, :, :], [key], [("xs", ti)], "o%d" % key[1])

        def tile_subs(bi, ti, with_ctx=True):
            off = TILES[ti][0]
            if ti < 2:
                return [(0, 512, bi, off), (512, 256, bi, off + 512)]
            s = [(0, 512, bi, off)]
            if with_ctx:
                s.append((512, 256, 2, off + 512))
            return s

        def norm_mod(T, xt, xkey, loff, n, l, j, i_shift, i_scale, dst_fn, dst_key):
            ssb, ssk = T["ss"].next()
            for c in range(8):
                sq, sqk = T["sq"].next()
                ACT(sq[:, :n], xt[:, c, loff:loff + n], AF.Square, [xkey], [sqk])
                MM(ssb[:, :n], cm[:, M_ONES, :], sq[:, :n], c == 0, c == 7, [sqk, "cm"], [ssk])
            rs, rsk = T["rstd"].next()
            ACT(rs[:, :n], ssb[:, :n], AF.Sqrt, [ssk], [rsk], bias=EPS, scale=1.0 / 1024)
            RECIP(rs[:, :n], rs[:, :n], [rsk], [rsk])
            for c in range(8):
                tm, tmk = T["tmp"].next()
                STT("dve", tm[:, :n], xt[:, c, loff:loff + n], mcol(l, i_scale, c, j), rs[:, :n], ALU.mult, ALU.mult,
                    [xkey, rsk, "modT"], [tmk])
                ACT(dst_fn(c), tm[:, :n], AF.Identity, [tmk, "modT"], [dst_key], bias=mcol(l, i_shift, c, j), scale=1.0)

        FD_PIECES = [(0, 6), (6, 12), (12, 17), (17, 22)]

        def ffn(l, which, bi, src, dst, with_ctx):
            i_shift, i_scale, i_gate = (0, 1, 2) if which == 0 else (6, 7, 8)
            with contextlib.ExitStack() as st:
                xrot = Rot("xt", [sbt(st, "xt%d" % i, [128, 8, 768], F32) for i in range(2)])
                hT = sbt(st, "hT", [128, 8, 768], BF16)
                gT = sbt(st, "gT", [128, NFF, 768], BF16)
                T = make_rots(st, "f")
                T["ss"] = PsRot([6, 7])
                srot = Rot("s", [sbt(st, "s%d" % i, [128, 512], F32) for i in range(3)])
                wgu = Rot("wgu", [sbt(st, "wgu%d" % i, [128, 8, 256], BF16) for i in range(12)])
                wdr = Rot("wd", [sbt(st, "wd%d" % i, [128, 6, 512], BF16) for i in range(4)])
                P.barrier()
                pair = 0
                nxt = load_x(xrot, bi, 0, src)
                for ti in range(3):
                    subs = tile_subs(bi, ti, with_ctx)
                    xt, xkey = nxt
                    for si, (loff, n, j, tok) in enumerate(subs):
                        norm_mod(T, xt, xkey, loff, n, l, j, i_shift, i_scale,
                                 (lambda c, loff=loff, n=n: hT[:, c, loff:loff + n]), ("hT", si))
                    if ti + 1 < 3:
                        nxt = load_x(xrot, bi, ti + 1, src)
                    for g in range(11):
                        f0 = g * 256
                        wgt, wgk = wgu.next()
                        wut, wuk = wgu.next()
                        first = (bi == 0 and ti == 0)
                        for (wt_, wk_, src_, cache_, cn_) in ((wgt, wgk, fgw, wc_g, "wcg"), (wut, wuk, fuw, wc_u, "wcu")):
                            ck = (cn_, which, l, g)
                            if first:
                                P.dma("pool", wt_[:], src_[which][l][:, f0:f0 + 256].rearrange("(k p) f -> p k f", p=128), (), [wk_], "wgu%d" % wk_[1])
                                P.dma("sp", cache_[which][l][g], wt_[:], [wk_], [ck], "wcs%d" % wk_[1])
                            else:
                                P.dma("sp", wt_[:], cache_[which][l][g], [ck], [wk_], "wgu%d" % wk_[1])
                        for fc in range(2):
                            f = g * 2 + fc
                            for si, (loff, n, j, tok) in enumerate(subs):
                                Gb, Gk = bank(2 * (pair % 3))
                                Ub, Uk = bank(2 * (pair % 3) + 1)
                                pair += 1
                                for k in range(8):
                                    MM(Gb[:, :n], wgt[:, k, fc * 128:(fc + 1) * 128], hT[:, k, loff:loff + n], k == 0, k == 7,
                                       [wgk, ("hT", si)], [Gk])
                                for k in range(8):
                                    MM(Ub[:, :n], wut[:, k, fc * 128:(fc + 1) * 128], hT[:, k, loff:loff + n], k == 0, k == 7,
                                       [wuk, ("hT", si)], [Uk])
                                sg, sk = srot.next()
                                ACT(sg[:, :n], Gb[:, :n], AF.Silu, [Gk], [sk])
                                TT("dve", gT[:, f, loff:loff + n], Ub[:, :n], sg[:, :n], ALU.mult, [Uk, sk], [("gT", si)])
                    for half in range(2):
                        for (fc0, fc1) in FD_PIECES:
                            wdt, wdk = wdr.next()
                            pi = FD_PIECES.index((fc0, fc1))
                            ck = ("wcd", which, l, half, pi)
                            if bi == 0 and ti == 0:
                                P.dma("pool", wdt[:, 0:fc1 - fc0, :],
                                      fdw[which][l][fc0 * 128:fc1 * 128, half * 512:(half + 1) * 512].rearrange("(j p) d -> p j d", p=128),
                                      (), [wdk], "wd%d" % wdk[1])
                                P.dma("sp", wc_d[which][l][half, pi, :, 0:fc1 - fc0, :], wdt[:, 0:fc1 - fc0, :], [wdk], [ck], "wds%d" % wdk[1])
                            else:
                                P.dma("sp", wdt[:, 0:fc1 - fc0, :], wc_d[which][l][half, pi, :, 0:fc1 - fc0, :], [ck], [wdk], "wd%d" % wdk[1])
                            for jf in range(fc1 - fc0):
                                fc = fc0 + jf
                                for dc in range(4):
                                    for si, (loff, n, j, tok) in enumerate(subs):
                                        Ob, Ok = bank(dc * 2 + si)
                                        MM(Ob[:, :n], wdt[:, jf, dc * 128:(dc + 1) * 128], gT[:, fc, loff:loff + n], fc == 0, fc == NFF - 1,
                                           [wdk, ("gT", si)], [Ok])
                        for dc in range(4):
                            for si, (loff, n, j, tok) in enumerate(subs):
                                Ob, Ok = bank(dc * 2 + si)
                                ch = half * 4 + dc
                                STT("dve", xt[:, ch, loff:loff + n], Ob[:, :n], mcol(l, i_gate, ch, j), xt[:, ch, loff:loff + n],
                                    ALU.mult, ALU.add, [Ok, xkey, "modT"], [xkey])
                    store_x(xt, xkey, bi, ti, dst)
                P.barrier()

        def make_rots(st, pfx):
            R = {
                "sq": Rot(pfx + "sq", [sbt(st, pfx + "sq%d" % i, [128, 512], BF16) for i in range(3)]),
                "tmp": Rot(pfx + "tmp", [sbt(st, pfx + "tmp%d" % i, [128, 512], F32) for i in range(3)]),
                "rstd": Rot(pfx + "rstd", [sbt(st, pfx + "rstd%d" % i, [128, 512], F32) for i in range(2)]),
            }
            return R

        def compute_hall(l, bi, hy):
            with contextlib.ExitStack() as st:
                xrot = Rot("xt", [sbt(st, "hxt%d" % i, [128, 8, 768], F32) for i in range(2)])
                T = make_rots(st, "h")
                T["ss"] = PsRot([6, 7])
                P.barrier()
                for ti in range(3):
                    xt, xkey = load_x(xrot, bi, ti, "xs")
                    for si, (loff, n, j, tok) in enumerate(tile_subs(bi, ti, True)):
                        norm_mod(T, xt, xkey, loff, n, l, j, 3, 4,
                                 (lambda c, tok=tok, n=n: hy[:, c, tok:tok + n]), "hy")
                P.barrier()

        def pipelined(n_units, s_fn, rest_fn):
            ctx = {}
            if n_units:
                ctx[0] = s_fn(0)
            for i in range(n_units):
                if i + 1 < n_units:
                    ctx[i + 1] = s_fn(i + 1)
                rest_fn(i, ctx.pop(i))

        class PsRot:
            def __init__(self, idxs):
                self.idxs = idxs
                self.i = 0

            def next(self):
                b = self.idxs[self.i % len(self.idxs)]
                self.i += 1
                return bank(b)

        def proj_fm(pr, wt, wk, col0, M, hy, tok, n):
            pb, pk = pr.next()
            for k in range(8):
                MM(pb[:M, :n], wt[:, k, col0:col0 + M], hy[:, k, tok:tok + n], k == 0, k == 7, [wk, "hy"], [pk])
            return pb, pk

        def headnorm(R, pr, src, srck, M, n, gain_col, inv_n, bd, dst, dstk, rope=None, tok=0):
            sq, sqk = R["sq"].next()
            ACT(sq[:M, :n], src, AF.Square, [srck], [sqk])
            sb_, sk_ = pr.next()
            MM(sb_[:M, :n], bd, sq[:M, :n], True, True, [sqk, "cm"], [sk_])
            rs, rsk = R["rstd"].next()
            ACT(rs[:M, :n], sb_[:M, :n], AF.Sqrt, [sk_, "cols"], [rsk], bias=EPS, scale=inv_n)
            RECIP(rs[:M, :n], rs[:M, :n], [rsk], [rsk])
            STT("dve", dst, src, gain_col, rs[:M, :n], ALU.mult, ALU.mult, [srck, rsk, "cols"], [dstk])
            if rope is not None:
                perm, rC, rS, p0, p1 = rope
                swb, swk = pr.next()
                MM(swb[:M, :n], perm, dst, True, True, [dstk, "cm"], [swk])
                t1, t1k = R["tmp"].next()
                TT("pool", t1[p0:p1, :n], dst[p0:p1, :], rC[p0:p1, tok:tok + n], ALU.mult, [dstk, "rope"], [t1k])
                t2, t2k = R["tmp"].next()
                TT("dve", t2[p0:p1, :n], swb[p0:p1, :n], rS[p0:p1, tok:tok + n], ALU.mult, [swk, "rope"], [t2k])
                TT("pool", dst[p0:p1, :], t1[p0:p1, :n], t2[p0:p1, :n], ALU.add, [t1k, t2k, swk], [dstk])

        def out_proj(l, bi, w_out_d, ychunk, with_ctx):
            with contextlib.ExitStack() as st:
                xrot = Rot("xt", [sbt(st, "oxt%d" % i, [128, 8, 768], F32) for i in range(2)])
                wo = sbt(st, "wo", [128, 8, 1024], BF16)
                P.barrier()
                for hh in range(2):
                    P.dma("pool", wo[:, :, hh * 512:(hh + 1) * 512],
                          w_out_d[:, hh * 512:(hh + 1) * 512].rearrange("(k p) f -> p k f", p=128), (), [("wo", hh)], "wo%d" % hh)
                pr = PsRot([0, 1, 2, 3, 4, 5, 6, 7])
                for ti in range(3):
                    xt, xkey = load_x(xrot, bi, ti, "xs")
                    for si, (loff, n, j, tok) in enumerate(tile_subs(bi, ti, with_ctx)):
                        for dc in range(8):
                            pb, pk = pr.next()
                            for fc in range(8):
                                MM(pb[:, :n], wo[:, fc, dc * 128:(dc + 1) * 128], ychunk(fc)[:, tok:tok + n], fc == 0, fc == 7,
                                   [("wo", dc // 4), "hy", "yD"], [pk])
                            STT("dve", xt[:, dc, loff:loff + n], pb[:, :n], mcol(l, 5, dc, j), xt[:, dc, loff:loff + n],
                                ALU.mult, ALU.add, [pk, xkey, "modT"], [xkey])
                    store_x(xt, xkey, bi, ti, "xs")
                P.barrier()

        def mixer_ab(bi):
            l = 0
            with contextlib.ExitStack() as st0:
                hy = sbt(st0, "hy", [128, 8, TOK], BF16)
                compute_hall(l, bi, hy)
                with contextlib.ExitStack() as st:
                    aq = sbt(st, "aq", [128, 4, TOK], BF16)
                    bq = sbt(st, "bq", [128, 4, TOK], BF16)
                    bk = sbt(st, "bk", [128, 4, TOK], BF16)
                    bv = sbt(st, "bv", [128, 18, 512], BF16)
                    akd = sbt(st, "akd", [128, 2, TOK], BF16)
                    av = sbt(st, "av", [128, 18, 128], BF16)
                    rC = sbt(st, "rC", [128, NX], F32)
                    rS = sbt(st, "rS", [128, NX], F32)
                    maskA = sbt(st, "maskA", [128, 6, 512], BF16)
                    wr = Rot("w", [sbt(st, "wr%d" % i, [128, 8, 512], BF16) for i in range(2)])
                    R = make_rots(st, "m")
                    prot = Rot("P", [sbt(st, "P%d" % i, [128, 512], BF16) for i in range(3)])
                    rdrot = Rot("rd", [sbt(st, "rd%d" % i, [128, 512], F32) for i in range(2)])
                    ta = sbt(st, "ta", [128, 512], F32)
                    tb = sbt(st, "tb", [128, 512], F32)
                    yv = sbt(st, "yv", [128, 512], F32)
                    P.barrier()
                    P.dma("sp", rC[:], rope64_d[0], (), ["rope"], "rC")
                    P.dma("sp", rS[:], rope64_d[1], (), ["rope"], "rS")
                    P.dma("pool", maskA[:], maskA_d, (), ["maskA"], "maskA")
                    pr = PsRot([0, 1, 2, 3, 4, 5, 6, 7])
                    rope64 = (cm[:, M_PERM64, :], rC, rS, 0, 128)
                    bd64 = cm[:, M_BD64, :]

                    def load_w(src_ap, ncols, dst_off=0, slot=None):
                        if slot is None:
                            slot = wr.next()
                        wt, wk = slot
                        P.dma("pool", wt[:, :, dst_off:dst_off + ncols], src_ap.rearrange("(k p) f -> p k f", p=128), (), [wk], "w%d" % wk[1])
                        return slot

                    def qk_group(col_base, gain_c, dst, dname, subs_list):
                        wt, wk = load_w(ab_w_in[:, col_base:col_base + 512], 512)
                        for (tok, n) in subs_list:
                            for c in range(4):
                                pb, pk = proj_fm(pr, wt, wk, c * 128, 128, hy, tok, n)
                                headnorm(R, pr, pb[:, :n], pk, 128, n, cols[:, gain_c:gain_c + 1], 1.0 / 64, bd64,
                                         dst[:, c, tok:tok + n], (dname, c), rope=(rope64 if tok < NX else None), tok=tok)

                    qk_group(0, C_AQ, aq, "aq", SUBS5)
                    qk_group(512, C_BQ, bq, "bq", SUBS5)
                    slot = wr.next()
                    for g in range(2):
                        for d in range(2):
                            load_w(ab_w_in[:, 1024 + g * 64:1024 + (g + 1) * 64], 64, dst_off=g * 128 + d * 64, slot=slot)
                    load_w(ab_w_in[:, 1152:1280], 128, dst_off=256, slot=slot)
                    wt, wk = slot
                    for (tok, n) in SUBS5:
                        for g in range(2):
                            pb, pk = proj_fm(pr, wt, wk, g * 128, 128, hy, tok, n)
                            headnorm(R, pr, pb[:, :n], pk, 128, n, cols[:, C_AK:C_AK + 1], 1.0 / 64, bd64,
                                     akd[:, g, tok:tok + n], ("akd", g), rope=(rope64 if tok < NX else None), tok=tok)
                    for tt in range(18):
                        pb, pk = pr.next()
                        for k in range(8):
                            MM(pb[:, 0:128], hy[:, k, tt * 128:(tt + 1) * 128], wt[:, k, 256:384], k == 0, k == 7, [wk, "hy"], [pk])
                        COPY("dve", av[:, tt, :], pb[:, 0:128], [pk], ["av"])
                    qk_group(1280, C_BK, bk, "bk", SUBS5)
                    wt, wk = load_w(ab_w_in[:, 1792:2304], 512)
                    for tt in range(18):
                        pb, pk = pr.next()
                        for k in range(8):
                            MM(pb[:, :], hy[:, k, tt * 128:(tt + 1) * 128], wt[:, k, 0:512], k == 0, k == 7, [wk, "hy"], [pk])
                        ACT(bv[:, tt, :], pb[:, :], AF.Copy, [pk], ["bv"])

                    srot = PsRot([0, 1])
                    orot = PsRot([2, 4])
                    drot = PsRot([3, 5])
                    groupsA = []
                    for qg in range(4):
                        n0 = qg * 4
                        kts = [(kt, kt - n0 + 1) for kt in range(max(0, n0 - 1), min(15, n0 + 4) + 1)] + [(16, None), (17, None)]
                        groupsA.append((qg * 512, 512, kts))
                    groupsA.append((NX, NCTX, [(16, None), (17, None)]))
                    for h in range(8):
                        g, c, p0 = h // 4, h // 2, (h % 2) * 64
                        for (qoff, nq, kts) in groupsA:
                            Ob, Ok = orot.next()
                            Db, Dk = drot.next()
                            def s_fn(i):
                                kt, mi = kts[i]
                                Sb, Sk = srot.next()
                                MM(Sb[:, :nq], akd[p0:p0 + 64, g, kt * 128:(kt + 1) * 128], aq[p0:p0 + 64, c, qoff:qoff + nq], True, True,
                                   [("akd", g), ("aq", c)], [Sk])
                                return Sb, Sk

                            def rest_fn(i, sctx):
                                Sb, Sk = sctx
                                kt, mi = kts[i]
                                Pt, Pk = prot.next()
                                ACT(Pt[:, :nq], Sb[:, :nq], AF.Exp, [Sk], [Pk], scale=0.125)
                                if mi is not None:
                                    TT("dve", Pt[:, :nq], Pt[:, :nq], maskA[:, mi, :nq], ALU.mult, [Pk, "maskA"], [Pk])
                                MM(Ob[0:64, :nq], av[:, kt, g * 64:(g + 1) * 64], Pt[:, :nq], i == 0, i == len(kts) - 1, ["av", Pk], [Ok])
                                MM(Db[0:64, :nq], cm[:, M_ONES, 0:64], Pt[:, :nq], i == 0, i == len(kts) - 1, ["cm", Pk], [Dk])
                            pipelined(len(kts), s_fn, rest_fn)
                            rd, rdk = rdrot.next()
                            TS("dve", rd[0:64, :nq], Db[0:64, :nq], esink[0:64, h:h + 1], ALU.add, [Dk, "esink"], [rdk])
                            RECIP(rd[0:64, :nq], rd[0:64, :nq], [rdk], [rdk])
                            TT("dve", hy[p0:p0 + 64, c, qoff:qoff + nq], Ob[0:64, :nq], rd[0:64, :nq], ALU.mult, [Ok, rdk], ["hy"])

                    allk = [16, 17] + list(range(16))
                    groupsB = [(qg * 512, 512, allk) for qg in range(4)] + [(NX, NCTX, [16, 17])]
                    for h in range(4):
                        for (qoff, nq, kts) in groupsB:
                            for m in range(2):
                                Ob, Ok = bank(2 + 2 * m)
                                Db, Dk = bank(3 + 2 * m)
                                def s_fn(i):
                                    kt = kts[i]
                                    Sb, Sk = srot.next()
                                    MM(Sb[:, :nq], bk[m * 64:(m + 1) * 64, h, kt * 128:(kt + 1) * 128], bq[m * 64:(m + 1) * 64, h, qoff:qoff + nq],
                                       True, True, [("bk", h), ("bq", h)], [Sk])
                                    return Sb, Sk

                                def rest_fn(i, sctx):
                                    Sb, Sk = sctx
                                    kt = kts[i]
                                    Pt, Pk = prot.next()
                                    ACT(Pt[:, :nq], Sb[:, :nq], AF.Exp, [Sk], [Pk], scale=0.125)
                                    MM(Ob[:, :nq], bv[:, kt, h * 128:(h + 1) * 128], Pt[:, :nq], i == 0, i == len(kts) - 1, ["bv", Pk], [Ok])
                                    MM(Db[:, :nq], cm[:, M_ONES, :], Pt[:, :nq], i == 0, i == len(kts) - 1, ["cm", Pk], [Dk])
                                pipelined(len(kts), s_fn, rest_fn)
                            r1, r1k = rdrot.next()
                            RECIP(r1[:, :nq], psb[3][:, :nq], [("ps", 3)], [r1k])
                            r2, r2k = rdrot.next()
                            RECIP(r2[:, :nq], psb[5][:, :nq], [("ps", 5)], [r2k])
                            TT("dve", ta[:, :nq], psb[2][:, :nq], r1[:, :nq], ALU.mult, [("ps", 2), r1k], ["ta"])
                            TT("dve", tb[:, :nq], psb[4][:, :nq], r2[:, :nq], ALU.mult, [("ps", 4), r2k], ["tb"])
                            STT("dve", yv[:, :nq], tb[:, :nq], lamt[:, 4:5], ta[:, :nq], ALU.mult, ALU.add, ["ta", "tb", "lamt"], ["yv"])
                            headnorm(R, PsRot([6, 7]), yv[:, :nq], "yv", 128, nq, lamt[:, 5:6], 1.0 / 128, cm[:, M_ONES, :],
                                     hy[:, 4 + h, qoff:qoff + nq], "hy")
                    P.barrier()
                out_proj(l, bi, ab_w_out, (lambda fc: hy[:, fc, :]), True)

        def mixer_cd(bi):
            l = 1
            with contextlib.ExitStack() as st0:
                hy = sbt(st0, "hy1", [128, 8, TOK], BF16)
                yD = sbt(st0, "yD", [128, 4, NX], BF16)
                compute_hall(l, bi, hy)
                srot = PsRot([0, 1])
                with contextlib.ExitStack() as st:
                    dq = sbt(st, "dq", [128, 4, NX], BF16)
                    dk = sbt(st, "dk", [128, 4, TOK], BF16)
                    vD = sbt(st, "vD", [128, 33, 512], BF16)
                    E2 = sbt(st, "E2", [128, 14, 512], BF16)
                    eg = sbt(st, "eg", [128, 2, 512], F32)
                    wr = Rot("w", [sbt(st, "dwr%d" % i, [128, 8, 512], BF16) for i in range(2)])
                    R = make_rots(st, "d")
                    prot = Rot("P", [sbt(st, "dP%d" % i, [128, 512], BF16) for i in range(3)])
                    rdrot = Rot("rd", [sbt(st, "drd%d" % i, [128, 512], F32) for i in range(2)])
                    P.barrier()
                    pr = PsRot([0, 1, 2, 3, 4, 5, 6, 7])
                    bd64 = cm[:, M_BD64, :]
                    for i in range(14):
                        s = i % 2
                        P.dma("sp", eg[:, s, :], rpbG_d[:, i, :], (), [("eg", s)], "eg%d" % s)
                        ACT(E2[:, i, :], eg[:, s, :], AF.Exp, [("eg", s)], ["E2"])

                    def load_w(src_ap, ncols):
                        wt, wk = wr.next()
                        P.dma("pool", wt[:, :, 0:ncols], src_ap.rearrange("(k p) f -> p k f", p=128), (), [wk], "w%d" % wk[1])
                        return wt, wk

                    wt, wk = load_w(cd_w_in[:, 768:1280], 512)
                    for (tok, n) in SUBS5[:4]:
                        for c in range(4):
                            pb, pk = proj_fm(pr, wt, wk, c * 128, 128, hy, tok, n)
                            headnorm(R, pr, pb[:, :n], pk, 128, n, cols[:, C_DQ:C_DQ + 1], 1.0 / 64, bd64, dq[:, c, tok:tok + n], ("dq", c))
                    wt, wk = load_w(cd_w_in[:, 1568:2080], 512)
                    for (tok, n) in SUBS5:
                        for c in range(4):
                            pb, pk = proj_fm(pr, wt, wk, c * 128, 128, hy, tok, n)
                            headnorm(R, pr, pb[:, :n], pk, 128, n, cols[:, C_DK:C_DK + 1], 1.0 / 64, bd64, dk[:, c, tok:tok + n], ("dk", c))
                    wt, wk = load_w(cd_w_in[:, 2080:2592], 512)

                    def wtok(w):
                        return 64 * w if w < 31 else NX + 128 * (w - 31)
                    for w in range(33):
                        pb, pk = pr.next()
                        t0 = wtok(w)
                        for k in range(8):
                            MM(pb[:, :], hy[:, k, t0:t0 + 128], wt[:, k, 0:512], k == 0, k == 7, [wk, "hy"], [pk])
                        if w % 2 == 0:
                            ACT(vD[:, w, :], pb[:, :], AF.Copy, [pk], ["vD"])
                        else:
                            COPY("dve", vD[:, w, :], pb[:, :], [pk], ["vD"])
                    orot = PsRot([2, 4])
                    drot = PsRot([3, 5])
                    srotE = PsRot([0, 6])
                    srotO = PsRot([1, 7])
                    for r in range(0 if (KSUB & 1) else 32):
                        rs_ = min(max(r - 4, 0), 24)
                        units = [(rs_ + 2 * i, rs_ + 2 * i - r + 7) for i in range(4)] + [(31, None), (32, None)]
                        Ob, Ok = orot.next()
                        Db, Dk = drot.next()
                        def s_fn(ui):
                            w, ei = units[ui]
                            t0 = wtok(w)
                            SbE, SkE = srotE.next()
                            SbO, SkO = srotO.next()
                            for h in range(8):
                                c, p0 = h // 2, (h % 2) * 64
                                Sb, Sk = (SbE, SkE) if h % 2 == 0 else (SbO, SkO)
                                MM(Sb[:, c * 64:(c + 1) * 64], dk[p0:p0 + 64, c, t0:t0 + 128], dq[p0:p0 + 64, c, r * 64:(r + 1) * 64],
                                   c == 0, c == 3, [("dk", c), ("dq", c)], [Sk], skip_group_check=True)
                            return SbE, SkE, SbO, SkO

                        def rest_fn(ui, sctx):
                            SbE, SkE, SbO, SkO = sctx
                            w, ei = units[ui]
                            Pt, Pk = prot.next()
                            PtV = Pt[:, :].rearrange("p (c two q) -> p c two q", two=2, q=64)
                            ACT(PtV[:, :, 0, :], SbE[:, 0:256].rearrange("p (c q) -> p c q", q=64), AF.Exp, [SkE], [Pk], scale=0.125)
                            ACT(PtV[:, :, 1, :], SbO[:, 0:256].rearrange("p (c q) -> p c q", q=64), AF.Exp, [SkO, Pk], [Pk], scale=0.125)
                            if ei is not None:
                                TT("dve", Pt[:, :], Pt[:, :], E2[:, ei, :], ALU.mult, [Pk, "E2"], [Pk])
                            for h in range(8):
                                MM(Ob[0:64, h * 64:(h + 1) * 64], vD[:, w, h * 64:(h + 1) * 64], Pt[:, h * 64:(h + 1) * 64],
                                   ui == 0 and h == 0, ui == len(units) - 1 and h == 7, ["vD", Pk], [Ok], skip_group_check=True)
                            MM(Db[0:64, :], cm[:, M_ONES, 0:64], Pt[:, :], ui == 0, ui == len(units) - 1, ["cm", Pk], [Dk])
                        pipelined(len(units), s_fn, rest_fn)
                        rd, rdk = rdrot.next()
                        RECIP(rd[0:64, :], Db[0:64, :], [Dk], [rdk])
                        for h in range(8):
                            c, p0 = h // 2, (h % 2) * 64
                            TT("dve", yD[p0:p0 + 64, c, r * 64:(r + 1) * 64], Ob[0:64, h * 64:(h + 1) * 64], rd[0:64, h * 64:(h + 1) * 64],
                               ALU.mult, [Ok, rdk], ["yD"])
                    P.barrier()
                with contextlib.ExitStack() as st:
                    cqg = sbt(st, "cqg", [128, 6, NX], BF16)
                    rsa = sbt(st, "rsa", [128, NX], F32)
                    ckvn = sbt(st, "ckvn", [128, 2, TOK], BF16)
                    krope = sbt(st, "krope", [32, TOK], BF16)
                    rC = sbt(st, "rCm", [128, NX], F32)
                    rS = sbt(st, "rSm", [128, NX], F32)
                    vC = sbt(st, "vC", [128, 18, 512], BF16)
                    qcat = sbt(st, "qcat", [96, NX], BF16)
                    kcat = sbt(st, "kcat", [96, TOK], BF16)
                    wr = Rot("w", [sbt(st, "cwr%d" % i, [128, 8, 512], BF16) for i in range(2)])
                    wqr = Rot("wq", [sbt(st, "wq%d" % i, [128, 6, 96], BF16) for i in range(2)])
                    wkr = Rot("wk", [sbt(st, "wk%d" % i, [128, 2, 64], BF16) for i in range(2)])
                    R = make_rots(st, "c")
                    prot = Rot("P", [sbt(st, "cP%d" % i, [128, 512], BF16) for i in range(3)])
                    rdrot = Rot("rd", [sbt(st, "crd%d" % i, [128, 512], F32) for i in range(2)])
                    P.barrier()
                    P.dma("sp", rC[:], ropeM_d[0], (), ["rope"], "rC")
                    P.dma("sp", rS[:], ropeM_d[1], (), ["rope"], "rS")
                    pr = PsRot([2, 3, 4, 5, 6, 7])

                    def load_w(src_ap, ncols):
                        wt, wk = wr.next()
                        P.dma("pool", wt[:, :, 0:ncols], src_ap.rearrange("(k p) f -> p k f", p=128), (), [wk], "w%d" % wk[1])
                        return wt, wk

                    wa, wak = load_w(cd_w_in[:, 0:512], 512)
                    wb, wbk = load_w(cd_w_in[:, 512:768], 256)
                    for (tok, n) in SUBS5[:4]:
                        ssb, ssk = bank(0 + (tok // 512) % 2)
                        for c in range(6):
                            wt, wk, cc = (wa, wak, c) if c < 4 else (wb, wbk, c - 4)
                            pb, pk = proj_fm(pr, wt, wk, cc * 128, 128, hy, tok, n)
                            sq, sqk = R["sq"].next()
                            ACT(sq[:, :n], pb[:, :n], AF.Square, [pk], [sqk])
                            MM(ssb[:, :n], cm[:, M_ONES, :], sq[:, :n], c == 0, c == 5, [sqk, "cm"], [ssk])
                            TS("dve", cqg[:, c, tok:tok + n], pb[:, :n], cols[:, C_QA + c:C_QA + c + 1], ALU.mult, [pk, "cols", sqk], ["cqg"])
                        ACT(rsa[:, tok:tok + n], ssb[:, :n], AF.Sqrt, [ssk], ["rsa"], bias=EPS, scale=1.0 / 768)
                        RECIP(rsa[:, tok:tok + n], rsa[:, tok:tok + n], ["rsa"], ["rsa"])
                    wt, wk = load_w(cd_w_in[:, 1280:1568], 288)
                    rope32 = (cm[0:32, M_PERM32, 0:32], rC, rS, 0, 32)
                    for (tok, n) in SUBS5:
                        ssb, ssk = bank(0 + (tok // 512) % 2)
                        pbs = []
                        for c in range(2):
                            pb, pk = proj_fm(pr, wt, wk, c * 128, 128, hy, tok, n)
                            pbs.append((pb, pk))
                            sq, sqk = R["sq"].next()
                            ACT(sq[:, :n], pb[:, :n], AF.Square, [pk], [sqk])
                            MM(ssb[:, :n], cm[:, M_ONES, :], sq[:, :n], c == 0, c == 1, [sqk, "cm"], [ssk])
                        rs, rsk = R["rstd"].next()
                        ACT(rs[:, :n], ssb[:, :n], AF.Sqrt, [ssk], [rsk], bias=EPS, scale=1.0 / 256)
                        RECIP(rs[:, :n], rs[:, :n], [rsk], [rsk])
                        for c in range(2):
                            pb, pk = pbs[c]
                            STT("dve", ckvn[:, c, tok:tok + n], pb[:, :n], cols[:, C_KVA + c:C_KVA + c + 1], rs[:, :n], ALU.mult, ALU.mult,
                                [pk, rsk, "cols"], ["ckvn"])
                        pb, pk = proj_fm(pr, wt, wk, 256, 32, hy, tok, n)
                        headnorm(R, pr, pb[0:32, :n], pk, 32, n, cols[0:32, C_KR:C_KR + 1], 1.0 / 32, cm[0:32, M_ONES, 0:32],
                                 krope[0:32, tok:tok + n], "krope", rope=(rope32 if tok < NX else None), tok=tok)
                    wt, wk = wr.next()
                    for kk in range(2):
                        P.dma("pool", wt[:, kk, 0:512].rearrange("p (h e) -> p h e", e=64),
                              c_w_ukv[kk * 128:(kk + 1) * 128, :].rearrange("p (h e) -> p h e", e=128)[:, :, 64:128], (), [wk], "w%d" % wk[1])
                    for tt in range(18):
                        pb, pk = pr.next()
                        for c in range(2):
                            MM(pb[:, :], ckvn[:, c, tt * 128:(tt + 1) * 128], wt[:, c, 0:512], c == 0, c == 1, [wk, "ckvn"], [pk])
                        ACT(vC[:, tt, :], pb[:, :], AF.Copy, [pk], ["vC"])
                    orot = PsRot([2, 4])
                    drot = PsRot([3, 5])
                    pr2 = PsRot([6, 7])
                    allk = [16, 17] + list(range(16))
                    sc = 96.0 ** -0.5
                    for h in range(0 if (KSUB & 2) else 8):
                        c_, p0 = h // 2, (h % 2) * 64
                        wq, wqk = wqr.next()
                        P.dma("pool", wq[:], c_w_uq[:, h * 96:(h + 1) * 96].rearrange("(k p) f -> p k f", p=128), (), [wqk], "wq%d" % wqk[1])
                        wkk, wkkk = wkr.next()
                        P.dma("pool", wkk[:], c_w_ukv[:, h * 128:h * 128 + 64].rearrange("(k p) f -> p k f", p=128), (), [wkkk], "wk%d" % wkkk[1])
                        for (tok, n) in SUBS5[:4]:
                            pb, pk = pr2.next()
                            for c in range(6):
                                MM(pb[0:96, :n], wq[:, c, :], cqg[:, c, tok:tok + n], c == 0, c == 5, [wqk, "cqg"], [pk])
                            qs, qsk = R["tmp"].next()
                            TT("dve", qs[0:96, :n], pb[0:96, :n], rsa[0:96, tok:tok + n], ALU.mult, [pk, "rsa"], [qsk])
                            headnorm(R, pr2, qs[0:96, :n], qsk, 96, n, cols[0:96, C_Q96:C_Q96 + 1], cols[0:96, C_INV96:C_INV96 + 1],
                                     cm[0:96, M_BD96, 0:96], qcat[0:96, tok:tok + n], "qcat",
                                     rope=(cm[0:96, M_PERM96, 0:96], rC, rS, 64, 96), tok=tok)
                        for (tok, n) in SUBS5:
                            pb, pk = pr2.next()
                            for c in range(2):
                                MM(pb[0:64, :n], wkk[:, c, :], ckvn[:, c, tok:tok + n], c == 0, c == 1, [wkkk, "ckvn"], [pk])
                            headnorm(R, pr2, pb[0:64, :n], pk, 64, n, cols[0:64, C_KN:C_KN + 1], 1.0 / 64, cm[0:64, M_ONES, 0:64],
                                     kcat[0:64, tok:tok + n], "kcat")
                        P.dma("sp", kcat[64:96, :], krope[0:32, :], ["krope"], ["kcat"], "kcr")
                        for qg in range(4):
                            qoff = qg * 512
                            Ob, Ok = orot.next()
                            Db, Dk = drot.next()
                            def s_fn(i):
                                kt = allk[i]
                                Sb, Sk = srot.next()
                                MM(Sb[:, :], kcat[0:96, kt * 128:(kt + 1) * 128], qcat[0:96, qoff:qoff + 512], True, True, ["kcat", "qcat"], [Sk])
                                return Sb, Sk

                            def rest_fn(i, sctx):
                                Sb, Sk = sctx
                                kt = allk[i]
                                Pt, Pk = prot.next()
                                ACT(Pt[:, :], Sb[:, :], AF.Exp, [Sk], [Pk], scale=sc)
                                MM(Ob[0:64, :], vC[:, kt, h * 64:(h + 1) * 64], Pt[:, :], i == 0, i == 17, ["vC", Pk], [Ok])
                                MM(Db[0:64, :], cm[:, M_ONES, 0:64], Pt[:, :], i == 0, i == 17, ["cm", Pk], [Dk])
                            pipelined(18, s_fn, rest_fn)
                            rd, rdk = rdrot.next()
                            RECIP(rd[0:64, :], Db[0:64, :], [Dk], [rdk])
                            TT("dve", hy[p0:p0 + 64, c_, qoff:qoff + 512], Ob[0:64, :], rd[0:64, :], ALU.mult, [Ok, rdk], ["hy"])
                    P.barrier()
                out_proj(l, bi, cd_w_out, (lambda fc: hy[:, fc, :] if fc < 4 else yD[:, fc - 4, :]), False)

        for bi in range(2):
            ffn(0, 0, bi, "in", "xs" if stage > 1 else "out", True)
            if stage <= 1:
                continue
            mixer_ab(bi)
            if stage <= 2:
                _dump(P, nc, top, xs, outT, bi, final_ops)
                continue
            ffn(0, 1, bi, "xs", "xs" if stage > 3 else "out", True)
            if stage <= 3:
                continue
            ffn(1, 0, bi, "xs", "xs" if stage > 4 else "out", True)
            if stage <= 4:
                continue
            mixer_cd(bi)
            if stage <= 5:
                _dump(P, nc, top, xs, outT, bi, final_ops)
                continue
            ffn(1, 1, bi, "xs", "out", False)
        P.emit(final_ops)
    return nc


def _dump(P, nc, top, xs, outT, bi, final_ops):
    P.barrier()
    op = P.dma("sp", outT[bi], xs[:, :, 0:NX], [("xs", 0), ("xs", 1), ("xs", 2)], [("out", bi, "d")], "dump")
    final_ops.append(op)
    P.barrier()


def _rope_tables(rot_dim):
    nf = rot_dim // 4
    inv = (10000.0 ** (-np.arange(nf, dtype=np.float64) / nf))
    t = np.arange(NX)
    row = (t // 64).astype(np.float64)
    col = (t % 64).astype(np.float64)
    C = np.zeros((rot_dim, NX), np.float64)
    S = np.zeros((rot_dim, NX), np.float64)
    for d in range(rot_dim):
        axis = d // (2 * nf)
        half = (d % (2 * nf)) // nf
        f = d % nf
        ang = (row if axis == 0 else col) * inv[f]
        C[d] = np.cos(ang)
        S[d] = np.sin(ang) * (-1.0 if half == 0 else 1.0)
    return C.astype(np.float32), S.astype(np.float32)


def _perm(rot_dim):
    nf = rot_dim // 4
    Pm = np.zeros((rot_dim, rot_dim), np.float32)
    for d in range(rot_dim):
        half = (d % (2 * nf)) // nf
        partner = d + nf if half == 0 else d - nf
        Pm[partner, d] = 1.0
    return Pm


def _constants():
    cmat = np.zeros((128, 6, 128), np.float32)
    cmat[:, M_ONES, :] = 1.0
    cmat[0:64, M_BD64, 0:64] = 1.0
    cmat[64:128, M_BD64, 64:128] = 1.0
    p64 = _perm(64)
    cmat[0:64, M_PERM64, 0:64] = p64
    cmat[64:128, M_PERM64, 64:128] = p64
    cmat[0:64, M_BD96, 0:64] = 1.0
    cmat[64:96, M_BD96, 64:96] = 1.0
    p32 = _perm(32)
    cmat[0:64, M_PERM96, 0:64] = np.eye(64, dtype=np.float32)
    cmat[64:96, M_PERM96, 64:96] = p32
    cmat[0:32, M_PERM32, 0:32] = p32
    C64, S64 = _rope_tables(64)
    rope64 = np.stack([np.concatenate([C64, C64]), np.concatenate([S64, S64])]).astype(np.float32)
    C32, S32 = _rope_tables(32)
    ropeM = np.zeros((2, 128, NX), np.float32)
    ropeM[0] = 1.0
    ropeM[0, 0:32] = C32
    ropeM[1, 0:32] = S32
    ropeM[0, 64:96] = C32
    ropeM[1, 64:96] = S32
    maskA = np.zeros((128, 6, 512), np.float32)
    jj = np.arange(128)[:, None]
    qq = np.arange(512)[None, :]
    b = qq // 128
    ii = qq % 128
    for mi in range(6):
        d = mi - 1 - b
        maskA[:, mi, :] = (np.abs(ii - jj - 128 * d) <= 128).astype(np.float32)
    return cmat, rope64, ropeM, maskA


def _rpb_gather(rpb):
    kc = np.arange(64)[:, None]
    qc = np.arange(64)[None, :]
    dc = np.clip(kc - qc, -15, 15) + 15
    cs = np.clip(qc - 8, 0, 48)
    valid = (kc >= cs) & (kc < cs + 16)
    G = np.empty((2, 64, 14, 8, 64), np.float32)
    for rr in range(2):
        for i in range(14):
            g = rpb[:, i + rr, :][:, dc]
            g = np.where(valid[None], g, np.float32(-100.0))
            G[rr, :, i, :, :] = np.transpose(g, (1, 0, 2))
    return np.ascontiguousarray(G.reshape(128, 14, 512))


def _prep(inp):
    f = lambda a: np.ascontiguousarray(np.asarray(a, dtype=np.float32))
    cmat, rope64, ropeM, maskA = _constants()
    shared = {
        "w_mod": f(inp["w_mod"]),
        "b_modT": f(np.transpose(np.asarray(inp["b_mod"]).reshape(2, 72, 128), (2, 0, 1))),
        "f1g": f(inp["ffn1_w_gate"]), "f1u": f(inp["ffn1_w_up"]), "f1d": f(inp["ffn1_w_down"]),
        "f2g": f(inp["ffn2_w_gate"]), "f2u": f(inp["ffn2_w_up"]), "f2d": f(inp["ffn2_w_down"]),
        "ab_w_in": f(inp["ab_w_in"][0]), "ab_w_out": f(inp["ab_w_out"][0]),
        "cd_w_in": f(inp["cd_w_in"][0]), "cd_w_out": f(inp["cd_w_out"][0]),
        "c_w_uq": f(inp["c_w_uq"][0]), "c_w_ukv": f(inp["c_w_ukv"][0]),
        "cmat": cmat, "rope64": rope64, "ropeM": ropeM, "maskA": maskA,
        "rpbG": _rpb_gather(np.asarray(inp["d_rpb"][0], dtype=np.float32)),
    }
    cols = np.zeros((128, NCOL), np.float32)
    g = lambda k: np.asarray(inp[k][0], dtype=np.float32)
    cols[:, C_AQ] = np.tile(g("a_q_norm"), 2)
    cols[:, C_AK] = np.tile(g("a_k_norm"), 2)
    cols[:, C_BQ] = np.tile(g("b_q_norm"), 2)
    cols[:, C_BK] = np.tile(g("b_k_norm"), 2)
    cols[:, C_BSUB] = g("b_sub_norm")
    cols[:, C_SINK:C_SINK + 8] = g("a_sink")[None, :]
    cols[0:64, C_LQ1] = g("b_lambda_q1")
    cols[0:64, C_LK1] = g("b_lambda_k1")
    cols[0:64, C_LQ2] = g("b_lambda_q2")
    cols[0:64, C_LK2] = g("b_lambda_k2")
    cols[:, C_QA:C_QA + 6] = g("c_q_a_norm").reshape(6, 128).T
    cols[:, C_KVA:C_KVA + 2] = g("c_kv_a_norm").reshape(2, 128).T
    cols[:, C_Q96] = 1.0
    cols[0:64, C_Q96] = g("c_q_nope_norm")
    cols[64:96, C_Q96] = g("c_q_rope_norm")
    cols[:, C_INV96] = 1.0 / 64
    cols[64:96, C_INV96] = 1.0 / 32
    cols[:, C_KN] = np.tile(g("c_k_nope_norm"), 2)
    cols[:, C_KR] = np.tile(g("c_k_rope_norm"), 4)
    cols[:, C_DQ] = np.tile(g("d_q_norm"), 2)
    cols[:, C_DK] = np.tile(g("d_k_norm"), 2)
    shared["cols"] = cols
    x = np.asarray(inp["x"], dtype=np.float32)
    ctx = np.asarray(inp["ctx"], dtype=np.float32)
    c = np.asarray(inp["c"], dtype=np.float32)
    c_ctx = np.asarray(inp["c_ctx"], dtype=np.float32)
    in_maps = []
    for core in range(8):
        b0 = 2 * core
        m = dict(shared)
        m["xT"] = np.ascontiguousarray(np.transpose(x[b0:b0 + 2], (0, 2, 1)).reshape(2, 8, 128, NX))
        m["ctxT"] = np.ascontiguousarray(np.transpose(ctx[b0:b0 + 2], (0, 2, 1)).reshape(2, 8, 128, NCTX))
        cc = np.stack([c[b0], c[b0 + 1], c_ctx], axis=-1)
        m["cT"] = np.ascontiguousarray(np.transpose(cc.reshape(8, 128, 3), (1, 0, 2)))
        in_maps.append(m)
    return in_maps


def kernel(**inputs):
    stage = int(os.environ.get("KSTAGE", "99"))
    ncores = int(os.environ.get("KCORES", "8"))
    in_maps = _prep(inputs)
    nc = build(stage)
    res = run_bass_kernel_spmd(nc, in_maps[:ncores], core_ids=list(range(ncores)))
    out = np.zeros((16, NX, 1024), np.float32)
    for core in range(ncores):
        o = np.asarray(res.results[core]["outT"]).reshape(2, 1024, NX)
        out[2 * core:2 * core + 2] = np.transpose(o, (0, 2, 1))
    return out
```

```python
import contextlib
import math
import os
import numpy as np
import concourse.bass as bass
import concourse.mybir as mybir
from concourse.bass_utils import run_bass_kernel_spmd

F32 = mybir.dt.float32
BF16 = mybir.dt.bfloat16
AF = mybir.ActivationFunctionType
ALU = mybir.AluOpType
EPS = 1e-6


class Op:
    __slots__ = ("eng", "fn", "waits", "dwaits", "inc", "semval", "dma")

    def __init__(self, eng, fn):
        self.eng = eng
        self.fn = fn
        self.waits = []
        self.dwaits = []
        self.inc = False
        self.semval = None
        self.dma = None


class Prog:
    ENGS = ("pe", "act", "dve", "pool", "sp")

    def __init__(self, nc):
        self.nc = nc
        self.ops = {e: [] for e in self.ENGS}
        self.last_write = {}
        self.readers = {}
        self.dma_sem_of = {}
        self.dma_sem_cnt = []
        self.pending = {}

    def dma_sem(self, name):
        if name not in self.dma_sem_of:
            self.dma_sem_of[name] = len(self.dma_sem_cnt)
            self.dma_sem_cnt.append(0)
        return self.dma_sem_of[name]

    def _dep(self, op, prod):
        if prod is None or prod is op:
            return
        if prod.dma is not None:
            op.dwaits.append(prod.dma)
        elif prod.eng != op.eng or prod.eng != "pe":
            op.waits.append(prod)
            prod.inc = True

    def barrier(self):
        snap = []
        for e in self.ENGS:
            if self.ops[e]:
                o = self.ops[e][-1]
                if o.dma is None:
                    o.inc = True
                    snap.append(o)
        dsnap = [(i, c) for i, c in enumerate(self.dma_sem_cnt) if c > 0]
        self.pending = {e: (snap, dsnap) for e in self.ENGS}

    def add(self, eng, fn, reads=(), writes=(), dma_sem=None):
        op = Op(eng, fn)
        pend = self.pending.pop(eng, None)
        if pend is not None:
            for o in pend[0]:
                if o.eng != eng:
                    op.waits.append(o)
            op.dwaits.extend(pend[1])
        for k in reads:
            self._dep(op, self.last_write.get(k))
        for k in writes:
            self._dep(op, self.last_write.get(k))
            for r in self.readers.get(k, {}).values():
                self._dep(op, r)
        if dma_sem is not None:
            si = self.dma_sem(dma_sem)
            self.dma_sem_cnt[si] += 16
            op.dma = (si, self.dma_sem_cnt[si])
        for k in reads:
            rk = (eng, op.dma[0]) if op.dma is not None else eng
            self.readers.setdefault(k, {})[rk] = op
        for k in writes:
            self.last_write[k] = op
            self.readers[k] = {}
        self.ops[eng].append(op)
        return op

    def dma(self, q, out, in_, reads=(), writes=(), sem=None):
        return self.add(q, lambda e: e.dma_start(out=out, in_=in_), reads, writes, dma_sem=sem)

    def emit(self, final_ops=()):
        nc = self.nc
        for e in self.ENGS:
            c = 0
            for op in self.ops[e]:
                if op.inc and op.dma is None:
                    c += 1
                    op.semval = c
        with contextlib.ExitStack() as st:
            esem = {e: st.enter_context(nc.semaphore("s_" + e)) for e in self.ENGS}
            dsem = [st.enter_context(nc.semaphore("d_%d" % i)) for i in range(len(self.dma_sem_cnt))]
            block = st.enter_context(nc.Block())
            prog = self

            def run(e, eng):
                waited = {f: 0 for f in prog.ENGS}
                dwaited = {}
                for op in prog.ops[e]:
                    need = {}
                    for w in op.waits:
                        if w.semval > waited[w.eng] and w.semval > need.get(w.eng, 0):
                            need[w.eng] = w.semval
                    for f, v in need.items():
                        eng.wait_ge(esem[f], v)
                        waited[f] = v
                    dneed = {}
                    for si, v in op.dwaits:
                        if v > dwaited.get(si, 0) and v > dneed.get(si, 0):
                            dneed[si] = v
                    for si, v in dneed.items():
                        eng.wait_ge(dsem[si], v)
                        dwaited[si] = v
                    ins = op.fn(eng)
                    if op.dma is not None:
                        ins.then_inc(dsem[op.dma[0]], 16)
                    elif op.inc:
                        ins.then_inc(esem[e], 1)
                if e == "sp":
                    for op in final_ops:
                        si, v = op.dma
                        if v > dwaited.get(si, 0):
                            eng.wait_ge(dsem[si], v)
                            dwaited[si] = v

            block.tensor(lambda eng: run("pe", eng))
            block.scalar(lambda eng: run("act", eng))
            block.vector(lambda eng: run("dve", eng))
            block.gpsimd(lambda eng: run("pool", eng))
            block.sync(lambda eng: run("sp", eng))


class Rot:
    def __init__(self, name, tensors):
        self.name = name
        self.t = tensors
        self.i = 0

    def next(self):
        s = self.i % len(self.t)
        self.i += 1
        return self.t[s], (self.name, s)


NX = 2048
NCTX = 256
TOK = NX + NCTX
NFF = 22
TILES = [(0, 768), (768, 768), (1536, 768)]
SUBS5 = [(0, 512), (512, 512), (1024, 512), (1536, 512), (2048, 256)]
NCOL = 32
KSUB = int(os.environ.get('KSUB', '0'))
LAM_INIT0 = 0.8 - 0.6 * math.exp(-0.3 * 0)

C_AQ, C_AK, C_BQ, C_BK, C_BSUB, C_SINK, C_LQ1, C_LK1, C_LQ2, C_LK2 = 0, 1, 2, 3, 4, 5, 13, 14, 15, 16
C_QA, C_KVA, C_Q96, C_INV96, C_KN, C_KR, C_DQ, C_DK = 17, 23, 25, 26, 27, 28, 29, 30
M_ONES, M_BD64, M_PERM64, M_BD96, M_PERM96, M_PERM32 = 0, 1, 2, 3, 4, 5


def build(stage=99):
    nc = bass.Bass("TRN2", target_bir_lowering=False)

    def din(name, shape):
        return nc.dram_tensor(name, list(shape), F32, kind="ExternalInput").ap()

    xT = din("xT", [2, 8, 128, NX])
    ctxT = din("ctxT", [2, 8, 128, NCTX])
    cT = din("cT", [128, 8, 3])
    w_mod = din("w_mod", [2, 1024, 9216])
    b_modT = din("b_modT", [128, 2, 72])
    fgw = [din("f1g", [2, 1024, 2816]), din("f2g", [2, 1024, 2816])]
    fuw = [din("f1u", [2, 1024, 2816]), din("f2u", [2, 1024, 2816])]
    fdw = [din("f1d", [2, 2816, 1024]), din("f2d", [2, 2816, 1024])]
    ab_w_in = din("ab_w_in", [1024, 2304])
    ab_w_out = din("ab_w_out", [1024, 1024])
    cd_w_in = din("cd_w_in", [1024, 2592])
    cd_w_out = din("cd_w_out", [1024, 1024])
    c_w_uq = din("c_w_uq", [768, 768])
    c_w_ukv = din("c_w_ukv", [256, 1024])
    cols_d = din("cols", [128, NCOL])
    cmat_d = din("cmat", [128, 6, 128])
    rope64_d = din("rope64", [2, 128, NX])
    ropeM_d = din("ropeM", [2, 128, NX])
    maskA_d = din("maskA", [128, 6, 512])
    rpbG_d = din("rpbG", [128, 14, 512])
    outT = nc.dram_tensor("outT", [2, 8, 128, NX], F32, kind="ExternalOutput").ap()
    xs = nc.dram_tensor("xs_scratch", [8, 128, TOK], F32).ap()
    wc_g = [[nc.dram_tensor("wcg%d%d" % (w, l), [11, 128, 8, 256], BF16).ap() for l in range(2)] for w in range(2)]
    wc_u = [[nc.dram_tensor("wcu%d%d" % (w, l), [11, 128, 8, 256], BF16).ap() for l in range(2)] for w in range(2)]
    wc_d = [[nc.dram_tensor("wcd%d%d" % (w, l), [2, 4, 128, 6, 512], BF16).ap() for l in range(2)] for w in range(2)]

    P = Prog(nc)
    final_ops = []

    def ACT(out, in_, func, r, w, **kw):
        P.add("act", lambda e: e.activation(out=out, in_=in_, func=func, **kw), r, w)

    def MM(out, lhsT, rhs, start, stop, r, w, **kw):
        P.add("pe", lambda e: e.matmul(out, lhsT, rhs, start=start, stop=stop, **kw), r, w)

    def TT(eng, out, in0, in1, op, r, w):
        P.add(eng, lambda e: e.tensor_tensor(out=out, in0=in0, in1=in1, op=op), r, w)

    def TS(eng, out, in0, s1, op0, r, w):
        P.add(eng, lambda e: e.tensor_scalar(out=out, in0=in0, scalar1=s1, scalar2=None, op0=op0), r, w)

    def STT(eng, out, in0, scalar, in1, op0, op1, r, w):
        P.add(eng, lambda e: e.scalar_tensor_tensor(out=out, in0=in0, scalar=scalar, in1=in1, op0=op0, op1=op1), r, w)

    def RECIP(out, in_, r, w):
        P.add("dve", lambda e: e.reciprocal(out=out, in_=in_), r, w)

    def COPY(eng, out, in_, r, w):
        P.add(eng, lambda e: e.tensor_copy(out=out, in_=in_), r, w)

    with contextlib.ExitStack() as top:
        uid = [0]

        def sbt(st, name, shape, dt):
            uid[0] += 1
            return st.enter_context(nc.sbuf_tensor("sb_%s_%d" % (name, uid[0]), list(shape), dt))

        psb = [top.enter_context(nc.psum_tensor("ps%d" % i, [128, 512], F32)) for i in range(8)]

        def bank(i):
            return psb[i], ("ps", i)

        c_sb = sbt(top, "c_sb", [128, 8, 3], F32)
        c_act = sbt(top, "c_act", [128, 8, 3], BF16)
        bm = sbt(top, "bm", [128, 2, 72], F32)
        cols = sbt(top, "cols", [128, NCOL], F32)
        cm = sbt(top, "cm", [128, 6, 128], BF16)
        ones_f = sbt(top, "ones_f", [128, 128], F32)
        modT = sbt(top, "modT", [128, 2, 72, 3], F32)
        esink = sbt(top, "esink", [128, 8], F32)
        lamt = sbt(top, "lamt", [128, 8], F32)

        P.dma("sp", c_sb[:], cT, writes=["c_sb"], sem="c")
        P.dma("sp", bm[:], b_modT, writes=["bm"], sem="bm")
        P.dma("sp", cols[:], cols_d, writes=["cols"], sem="cols")
        P.dma("pool", cm[:], cmat_d, writes=["cm"], sem="cm")
        P.add("pool", lambda e: e.memset(ones_f[:], 1.0), (), ["ones_f"])
        ACT(c_act[:], c_sb[:], AF.Silu, ["c_sb"], ["c_act"])

        def mcol(l, i, c, j):
            return modT[:, l, 8 * i + c, j:j + 1]

        with contextlib.ExitStack() as st:
            wm = [sbt(st, "wm%d" % i, [128, 8, 512], BF16) for i in range(2)]
            it = 0
            for l in range(2):
                for sl in range(18):
                    s = it % 2
                    P.dma("pool", wm[s][:], w_mod[l][:, sl * 512:(sl + 1) * 512].rearrange("(k p) f -> p k f", p=128),
                          writes=[("wm", s)], sem="wm%d" % s)
                    pb, pk = bank(it % 2)
                    for cc in range(4):
                        for k in range(8):
                            MM(pb[:, cc * 4:cc * 4 + 3], wm[s][:, k, cc * 128:(cc + 1) * 128], c_act[:, k, :],
                               k == 0, k == 7, [("wm", s), "c_act"], [pk])
                    for cc in range(4):
                        col = sl * 4 + cc
                        TS("dve", modT[:, l, col, :], pb[:, cc * 4:cc * 4 + 3], bm[:, l, col:col + 1], ALU.add,
                           [pk, "bm"], ["modT"])
                    it += 1
            for l in range(2):
                for i in (1, 4, 7):
                    TS("dve", modT[:, l, 8 * i:8 * i + 8, :], modT[:, l, 8 * i:8 * i + 8, :], 1.0, ALU.add, ["modT"], ["modT"])
                for i in (2, 8):
                    TS("dve", modT[:, l, 8 * i:8 * i + 8, :], modT[:, l, 8 * i:8 * i + 8, :], 0.5, ALU.mult, ["modT"], ["modT"])
            ACT(esink[:], cols[:, C_SINK:C_SINK + 8], AF.Exp, ["cols"], ["esink"])
            TT("dve", lamt[:, 0:1], cols[:, C_LQ1:C_LQ1 + 1], cols[:, C_LK1:C_LK1 + 1], ALU.mult, ["cols"], ["lamt"])
            TT("dve", lamt[:, 1:2], cols[:, C_LQ2:C_LQ2 + 1], cols[:, C_LK2:C_LK2 + 1], ALU.mult, ["cols"], ["lamt"])
            pb, pk = bank(2)
            MM(pb[:, 0:2], ones_f[:], lamt[:, 0:2], True, True, ["ones_f", "lamt"], [pk])
            ACT(lamt[:, 2:4], pb[:, 0:2], AF.Exp, [pk], ["lamt"])
            TT("dve", lamt[:, 4:5], lamt[:, 3:4], lamt[:, 2:3], ALU.subtract, ["lamt"], ["lamt"])
            TS("dve", lamt[:, 4:5], lamt[:, 4:5], -LAM_INIT0, ALU.add, ["lamt"], ["lamt"])
            TS("dve", lamt[:, 5:6], cols[:, C_BSUB:C_BSUB + 1], 1.0 - LAM_INIT0, ALU.mult, ["cols", "lamt"], ["lamt"])
            P.barrier()

        def load_x(xrot, bi, ti, src):
            t, key = xrot.next()
            off = TILES[ti][0]
            sem = "x%d" % key[1]
            if src == "in":
                if ti < 2:
                    P.dma("sp", t[:, :, :], xT[bi][:, :, off:off + 768].rearrange("c p t -> p c t"), (), [key], sem)
                else:
                    P.dma("sp", t[:, :, 0:512], xT[bi][:, :, 1536:2048].rearrange("c p t -> p c t"), (), [key], sem)
                    P.dma("sp", t[:, :, 512:768], ctxT[bi].rearrange("c p t -> p c t"), (), [key], sem)
            else:
                P.dma("sp", t[:, :, :], xs[:, :, off:off + 768].rearrange("c p t -> p c t"), [("xs", ti)], [key], sem)
            return t, key

        def store_x(t, key, bi, ti, dst):
            off = TILES[ti][0]
            if dst == "out":
                if ti < 2:
                    op = P.dma("sp", outT[bi][:, :, off:off + 768].rearrange("c p t -> p c t"), t[:, :, :], [key], [("out", bi, ti)], "o%d" % key[1])
                else:
                    op = P.dma("sp", outT[bi][:, :, 1536:2048].rearrange("c p t -> p c t"), t[:, :, 0:512], [key], [("out", bi, ti)], "o%d" % key[1])
                final_ops.append(op)
            else:
                P.dma("sp", xs[:, :, off:off + 768].rearrange("c p t -> p c t"), t[:, :, :], [key], [("xs", ti)], "o%d" % key[1])

        def tile_subs(bi, ti, with_ctx=True):
            off = TILES[ti][0]
            if ti < 2:
                return [(0, 512, bi, off), (512, 256, bi, off + 512)]
            s = [(0, 512, bi, off)]
            if with_ctx:
                s.append((512, 256, 2, off + 512))
            return s

        def norm_mod(T, xt, xkey, loff, n, l, j, i_shift, i_scale, dst_fn, dst_key):
            ssb, ssk = T["ss"].next()
            for c in range(8):
                sq, sqk = T["sq"].next()
                ACT(sq[:, :n], xt[:, c, loff:loff + n], AF.Square, [xkey], [sqk])
                MM(ssb[:, :n], cm[:, M_ONES, :], sq[:, :n], c == 0, c == 7, [sqk, "cm"], [ssk])
            rs, rsk = T["rstd"].next()
            ACT(rs[:, :n], ssb[:, :n], AF.Sqrt, [ssk], [rsk], bias=EPS, scale=1.0 / 1024)
            RECIP(rs[:, :n], rs[:, :n], [rsk], [rsk])
            for c in range(8):
                tm, tmk = T["tmp"].next()
                STT("dve", tm[:, :n], xt[:, c, loff:loff + n], mcol(l, i_scale, c, j), rs[:, :n], ALU.mult, ALU.mult,
                    [xkey, rsk, "modT"], [tmk])
                ACT(dst_fn(c), tm[:, :n], AF.Identity, [tmk, "modT"], [dst_key], bias=mcol(l, i_shift, c, j), scale=1.0)

        FD_PIECES = [(0, 6), (6, 12), (12, 17), (17, 22)]

        def ffn(l, which, bi, src, dst, with_ctx):
            i_shift, i_scale, i_gate = (0, 1, 2) if which == 0 else (6, 7, 8)
            with contextlib.ExitStack() as st:
                xrot = Rot("xt", [sbt(st, "xt%d" % i, [128, 8, 768], F32) for i in range(2)])
                hT = sbt(st, "hT", [128, 8, 768], BF16)
                gT = sbt(st, "gT", [128, NFF, 768], BF16)
                T = make_rots(st, "f")
                T["ss"] = PsRot([6, 7])
                srot = Rot("s", [sbt(st, "s%d" % i, [128, 512], F32) for i in range(3)])
                wgu = Rot("wgu", [sbt(st, "wgu%d" % i, [128, 8, 256], BF16) for i in range(12)])
                wdr = Rot("wd", [sbt(st, "wd%d" % i, [128, 6, 512], BF16) for i in range(4)])
                P.barrier()
                pair = 0
                nxt = load_x(xrot, bi, 0, src)
                for ti in range(3):
                    subs = tile_subs(bi, ti, with_ctx)
                    xt, xkey = nxt
                    for si, (loff, n, j, tok) in enumerate(subs):
                        norm_mod(T, xt, xkey, loff, n, l, j, i_shift, i_scale,
                                 (lambda c, loff=loff, n=n: hT[:, c, loff:loff + n]), ("hT", si))
                    if ti + 1 < 3:
                        nxt = load_x(xrot, bi, ti + 1, src)
                    for g in range(11):
                        f0 = g * 256
                        wgt, wgk = wgu.next()
                        wut, wuk = wgu.next()
                        first = (bi == 0 and ti == 0)
                        for (wt_, wk_, src_, cache_, cn_) in ((wgt, wgk, fgw, wc_g, "wcg"), (wut, wuk, fuw, wc_u, "wcu")):
                            ck = (cn_, which, l, g)
                            if first:
                                P.dma("pool", wt_[:], src_[which][l][:, f0:f0 + 256].rearrange("(k p) f -> p k f", p=128), (), [wk_], "wgu%d" % wk_[1])
                                P.dma("sp", cache_[which][l][g], wt_[:], [wk_], [ck], "wcs%d" % wk_[1])
                            else:
                                P.dma("sp", wt_[:], cache_[which][l][g], [ck], [wk_], "wgu%d" % wk_[1])
                        for fc in range(2):
                            f = g * 2 + fc
                            for si, (loff, n, j, tok) in enumerate(subs):
                                Gb, Gk = bank(2 * (pair % 3))
                                Ub, Uk = bank(2 * (pair % 3) + 1)
                                pair += 1
                                for k in range(8):
                                    MM(Gb[:, :n], wgt[:, k, fc * 128:(fc + 1) * 128], hT[:, k, loff:loff + n], k == 0, k == 7,
                                       [wgk, ("hT", si)], [Gk])
                                for k in range(8):
                                    MM(Ub[:, :n], wut[:, k, fc * 128:(fc + 1) * 128], hT[:, k, loff:loff + n], k == 0, k == 7,
                                       [wuk, ("hT", si)], [Uk])
                                sg, sk = srot.next()
                                ACT(sg[:, :n], Gb[:, :n], AF.Silu, [Gk], [sk])
                                TT("dve", gT[:, f, loff:loff + n], Ub[:, :n], sg[:, :n], ALU.mult, [Uk, sk], [("gT", si)])
                    for half in range(2):
                        for (fc0, fc1) in FD_PIECES:
                            wdt, wdk = wdr.next()
                            pi = FD_PIECES.index((fc0, fc1))
                            ck = ("wcd", which, l, half, pi)
                            if bi == 0 and ti == 0:
                                P.dma("pool", wdt[:, 0:fc1 - fc0, :],
                                      fdw[which][l][fc0 * 128:fc1 * 128, half * 512:(half + 1) * 512].rearrange("(j p) d -> p j d", p=128),
                                      (), [wdk], "wd%d" % wdk[1])
                                P.dma("sp", wc_d[which][l][half, pi, :, 0:fc1 - fc0, :], wdt[:, 0:fc1 - fc0, :], [wdk], [ck], "wds%d" % wdk[1])
                            else:
                                P.dma("sp", wdt[:, 0:fc1 - fc0, :], wc_d[which][l][half, pi, :, 0:fc1 - fc0, :], [ck], [wdk], "wd%d" % wdk[1])
                            for jf in range(fc1 - fc0):
                                fc = fc0 + jf
                                for dc in range(4):
                                    for si, (loff, n, j, tok) in enumerate(subs):
                                        Ob, Ok = bank(dc * 2 + si)
                                        MM(Ob[:, :n], wdt[:, jf, dc * 128:(dc + 1) * 128], gT[:, fc, loff:loff + n], fc == 0, fc == NFF - 1,
                                           [wdk, ("gT", si)], [Ok])
                        for dc in range(4):
                            for si, (loff, n, j, tok) in enumerate(subs):
                                Ob, Ok = bank(dc * 2 + si)
                                ch = half * 4 + dc
                                STT("dve", xt[:, ch, loff:loff + n], Ob[:, :n], mcol(l, i_gate, ch, j), xt[:, ch, loff:loff + n],
                                    ALU.mult, ALU.add, [Ok, xkey, "modT"], [xkey])
                    store_x(xt, xkey, bi, ti, dst)
                P.barrier()

        def make_rots(st, pfx):
            R = {
                "sq": Rot(pfx + "sq", [sbt(st, pfx + "sq%d" % i, [128, 512], BF16) for i in range(3)]),
                "tmp": Rot(pfx + "tmp", [sbt(st, pfx + "tmp%d" % i, [128, 512], F32) for i in range(3)]),
                "rstd": Rot(pfx + "rstd", [sbt(st, pfx + "rstd%d" % i, [128, 512], F32) for i in range(2)]),
            }
            return R

        def compute_hall(l, bi, hy):
            with contextlib.ExitStack() as st:
                xrot = Rot("xt", [sbt(st, "hxt%d" % i, [128, 8, 768], F32) for i in range(2)])
                T = make_rots(st, "h")
                T["ss"] = PsRot([6, 7])
                P.barrier()
                for ti in range(3):
                    xt, xkey = load_x(xrot, bi, ti, "xs")
                    for si, (loff, n, j, tok) in enumerate(tile_subs(bi, ti, True)):
                        norm_mod(T, xt, xkey, loff, n, l, j, 3, 4,
                                 (lambda c, tok=tok, n=n: hy[:, c, tok:tok + n]), "hy")
                P.barrier()

        def pipelined(n_units, s_fn, rest_fn):
            ctx = {}
            if n_units:
                ctx[0] = s_fn(0)
            for i in range(n_units):
                if i + 1 < n_units:
                    ctx[i + 1] = s_fn(i + 1)
                rest_fn(i, ctx.pop(i))

        class PsRot:
            def __init__(self, idxs):
                self.idxs = idxs
                self.i = 0

            def next(self):
                b = self.idxs[self.i % len(self.idxs)]
                self.i += 1
                return bank(b)

        def proj_fm(pr, wt, wk, col0, M, hy, tok, n):
            pb, pk = pr.next()
            for k in range(8):
                MM(pb[:M, :n], wt[:, k, col0:col0 + M], hy[:, k, tok:tok + n], k == 0, k == 7, [wk, "hy"], [pk])
            return pb, pk

        def headnorm(R, pr, src, srck, M, n, gain_col, inv_n, bd, dst, dstk, rope=None, tok=0):
            sq, sqk = R["sq"].next()
            ACT(sq[:M, :n], src, AF.Square, [srck], [sqk])
            sb_, sk_ = pr.next()
            MM(sb_[:M, :n], bd, sq[:M, :n], True, True, [sqk, "cm"], [sk_])
            rs, rsk = R["rstd"].next()
            ACT(rs[:M, :n], sb_[:M, :n], AF.Sqrt, [sk_, "cols"], [rsk], bias=EPS, scale=inv_n)
            RECIP(rs[:M, :n], rs[:M, :n], [rsk], [rsk])
            STT("dve", dst, src, gain_col, rs[:M, :n], ALU.mult, ALU.mult, [srck, rsk, "cols"], [dstk])
            if rope is not None:
                perm, rC, rS, p0, p1 = rope
                swb, swk = pr.next()
                MM(swb[:M, :n], perm, dst, True, True, [dstk, "cm"], [swk])
                t1, t1k = R["tmp"].next()
                TT("pool", t1[p0:p1, :n], dst[p0:p1, :], rC[p0:p1, tok:tok + n], ALU.mult, [dstk, "rope"], [t1k])
                t2, t2k = R["tmp"].next()
                TT("dve", t2[p0:p1, :n], swb[p0:p1, :n], rS[p0:p1, tok:tok + n], ALU.mult, [swk, "rope"], [t2k])
                TT("pool", dst[p0:p1, :], t1[p0:p1, :n], t2[p0:p1, :n], ALU.add, [t1k, t2k, swk], [dstk])

        def out_proj(l, bi, w_out_d, ychunk, with_ctx):
            with contextlib.ExitStack() as st:
                xrot = Rot("xt", [sbt(st, "oxt%d" % i, [128, 8, 768], F32) for i in range(2)])
                wo = sbt(st, "wo", [128, 8, 1024], BF16)
                P.barrier()
                for hh in range(2):
                    P.dma("pool", wo[:, :, hh * 512:(hh + 1) * 512],
                          w_out_d[:, hh * 512:(hh + 1) * 512].rearrange("(k p) f -> p k f", p=128), (), [("wo", hh)], "wo%d" % hh)
                pr = PsRot([0, 1, 2, 3, 4, 5, 6, 7])
                for ti in range(3):
                    xt, xkey = load_x(xrot, bi, ti, "xs")
                    for si, (loff, n, j, tok) in enumerate(tile_subs(bi, ti, with_ctx)):
                        for dc in range(8):
                            pb, pk = pr.next()
                            for fc in range(8):
                                MM(pb[:, :n], wo[:, fc, dc * 128:(dc + 1) * 128], ychunk(fc)[:, tok:tok + n], fc == 0, fc == 7,
                                   [("wo", dc // 4), "hy", "yD"], [pk])
                            STT("dve", xt[:, dc, loff:loff + n], pb[:, :n], mcol(l, 5, dc, j), xt[:, dc, loff:loff + n],
                                ALU.mult, ALU.add, [pk, xkey, "modT"], [xkey])
                    store_x(xt, xkey, bi, ti, "xs")
                P.barrier()

        def mixer_ab(bi):
            l = 0
            with contextlib.ExitStack() as st0:
                hy = sbt(st0, "hy", [128, 8, TOK], BF16)
                compute_hall(l, bi, hy)
                with contextlib.ExitStack() as st:
                    aq = sbt(st, "aq", [128, 4, TOK], BF16)
                    bq = sbt(st, "bq", [128, 4, TOK], BF16)
                    bk = sbt(st, "bk", [128, 4, TOK], BF16)
                    bv = sbt(st, "bv", [128, 18, 512], BF16)
                    akd = sbt(st, "akd", [128, 2, TOK], BF16)
                    av = sbt(st, "av", [128, 18, 128], BF16)
                    rC = sbt(st, "rC", [128, NX], F32)
                    rS = sbt(st, "rS", [128, NX], F32)
                    maskA = sbt(st, "maskA", [128, 6, 512], BF16)
                    wr = Rot("w", [sbt(st, "wr%d" % i, [128, 8, 512], BF16) for i in range(2)])
                    R = make_rots(st, "m")
                    prot = Rot("P", [sbt(st, "P%d" % i, [128, 512], BF16) for i in range(3)])
                    rdrot = Rot("rd", [sbt(st, "rd%d" % i, [128, 512], F32) for i in range(2)])
                    ta = sbt(st, "ta", [128, 512], F32)
                    tb = sbt(st, "tb", [128, 512], F32)
                    yv = sbt(st, "yv", [128, 512], F32)
                    P.barrier()
                    P.dma("sp", rC[:], rope64_d[0], (), ["rope"], "rC")
                    P.dma("sp", rS[:], rope64_d[1], (), ["rope"], "rS")
                    P.dma("pool", maskA[:], maskA_d, (), ["maskA"], "maskA")
                    pr = PsRot([0, 1, 2, 3, 4, 5, 6, 7])
                    rope64 = (cm[:, M_PERM64, :], rC, rS, 0, 128)
                    bd64 = cm[:, M_BD64, :]

                    def load_w(src_ap, ncols, dst_off=0, slot=None):
                        if slot is None:
                            slot = wr.next()
                        wt, wk = slot
                        P.dma("pool", wt[:, :, dst_off:dst_off + ncols], src_ap.rearrange("(k p) f -> p k f", p=128), (), [wk], "w%d" % wk[1])
                        return slot

                    def qk_group(col_base, gain_c, dst, dname, subs_list):
                        wt, wk = load_w(ab_w_in[:, col_base:col_base + 512], 512)
                        for (tok, n) in subs_list:
                            for c in range(4):
                                pb, pk = proj_fm(pr, wt, wk, c * 128, 128, hy, tok, n)
                                headnorm(R, pr, pb[:, :n], pk, 128, n, cols[:, gain_c:gain_c + 1], 1.0 / 64, bd64,
                                         dst[:, c, tok:tok + n], (dname, c), rope=(rope64 if tok < NX else None), tok=tok)

                    qk_group(0, C_AQ, aq, "aq", SUBS5)
                    qk_group(512, C_BQ, bq, "bq", SUBS5)
                    slot = wr.next()
                    for g in range(2):
                        for d in range(2):
                            load_w(ab_w_in[:, 1024 + g * 64:1024 + (g + 1) * 64], 64, dst_off=g * 128 + d * 64, slot=slot)
                    load_w(ab_w_in[:, 1152:1280], 128, dst_off=256, slot=slot)
                    wt, wk = slot
                    for (tok, n) in SUBS5:
                        for g in range(2):
                            pb, pk = proj_fm(pr, wt, wk, g * 128, 128, hy, tok, n)
                            headnorm(R, pr, pb[:, :n], pk, 128, n, cols[:, C_AK:C_AK + 1], 1.0 / 64, bd64,
                                     akd[:, g, tok:tok + n], ("akd", g), rope=(rope64 if tok < NX else None), tok=tok)
                    for tt in range(18):
                        pb, pk = pr.next()
                        for k in range(8):
                            MM(pb[:, 0:128], hy[:, k, tt * 128:(tt + 1) * 128], wt[:, k, 256:384], k == 0, k == 7, [wk, "hy"], [pk])
                        COPY("dve", av[:, tt, :], pb[:, 0:128], [pk], ["av"])
                    qk_group(1280, C_BK, bk, "bk", SUBS5)
                    wt, wk = load_w(ab_w_in[:, 1792:2304], 512)
                    for tt in range(18):
                        pb, pk = pr.next()
                        for k in range(8):
                            MM(pb[:, :], hy[:, k, tt * 128:(tt + 1) * 128], wt[:, k, 0:512], k == 0, k == 7, [wk, "hy"], [pk])
                        ACT(bv[:, tt, :], pb[:, :], AF.Copy, [pk], ["bv"])

                    srot = PsRot([0, 1])
                    orot = PsRot([2, 4])
                    drot = PsRot([3, 5])
                    groupsA = []
                    for qg in range(4):
                        n0 = qg * 4
                        kts = [(kt, kt - n0 + 1) for kt in range(max(0, n0 - 1), min(15, n0 + 4) + 1)] + [(16, None), (17, None)]
                        groupsA.append((qg * 512, 512, kts))
                    groupsA.append((NX, NCTX, [(16, None), (17, None)]))
                    for h in range(8):
                        g, c, p0 = h // 4, h // 2, (h % 2) * 64
                        for (qoff, nq, kts) in groupsA:
                            Ob, Ok = orot.next()
                            Db, Dk = drot.next()
                            def s_fn(i):
                                kt, mi = kts[i]
                                Sb, Sk = srot.next()
                                MM(Sb[:, :nq], akd[p0:p0 + 64, g, kt * 128:(kt + 1) * 128], aq[p0:p0 + 64, c, qoff:qoff + nq], True, True,
                                   [("akd", g), ("aq", c)], [Sk])
                                return Sb, Sk

                            def rest_fn(i, sctx):
                                Sb, Sk = sctx
                                kt, mi = kts[i]
                                Pt, Pk = prot.next()
                                ACT(Pt[:, :nq], Sb[:, :nq], AF.Exp, [Sk], [Pk], scale=0.125)
                                if mi is not None:
                                    TT("dve", Pt[:, :nq], Pt[:, :nq], maskA[:, mi, :nq], ALU.mult, [Pk, "maskA"], [Pk])
                                MM(Ob[0:64, :nq], av[:, kt, g * 64:(g + 1) * 64], Pt[:, :nq], i == 0, i == len(kts) - 1, ["av", Pk], [Ok])
                                MM(Db[0:64, :nq], cm[:, M_ONES, 0:64], Pt[:, :nq], i == 0, i == len(kts) - 1, ["cm", Pk], [Dk])
                            pipelined(len(kts), s_fn, rest_fn)
                            rd, rdk = rdrot.next()
                            TS("dve", rd[0:64, :nq], Db[0:64, :nq], esink[0:64, h:h + 1], ALU.add, [Dk, "esink"], [rdk])
                            RECIP(rd[0:64, :nq], rd[0:64, :nq], [rdk], [rdk])
                            TT("dve", hy[p0:p0 + 64, c, qoff:qoff + nq], Ob[0:64, :nq], rd[0:64, :nq], ALU.mult, [Ok, rdk], ["hy"])

                    allk = [16, 17] + list(range(16))
                    groupsB = [(qg * 512, 512, allk) for qg in range(4)] + [(NX, NCTX, [16, 17])]
                    yv2 = sbt(st, "yv2", [128, 512], F32)
                    yvrot = Rot("yv", [yv, yv2])
                    pend = []
                    for h in range(4):
                        for (qoff, nq, kts) in groupsB:
                            for m in range(2):
                                Ob, Ok = bank(2 + 2 * m)
                                Db, Dk = bank(3 + 2 * m)

                                def s_fn(i):
                                    kt = kts[i]
                                    Sb, Sk = srot.next()
                                    MM(Sb[:, :nq], bk[m * 64:(m + 1) * 64, h, kt * 128:(kt + 1) * 128], bq[m * 64:(m + 1) * 64, h, qoff:qoff + nq],
                                       True, True, [("bk", h), ("bq", h)], [Sk])
                                    return Sb, Sk

                                def rest_fn(i, sctx):
                                    Sb, Sk = sctx
                                    kt = kts[i]
                                    Pt, Pk = prot.next()
                                    ACT(Pt[:, :nq], Sb[:, :nq], AF.Exp, [Sk], [Pk], scale=0.125)
                                    MM(Ob[:, :nq], bv[:, kt, h * 128:(h + 1) * 128], Pt[:, :nq], i == 0, i == len(kts) - 1, ["bv", Pk], [Ok])
                                    MM(Db[:, :nq], cm[:, M_ONES, :], Pt[:, :nq], i == 0, i == len(kts) - 1, ["cm", Pk], [Dk])
                                pipelined(len(kts), s_fn, rest_fn)
                                if m == 0 and pend:
                                    pend.pop()()
                            r1, r1k = rdrot.next()
                            RECIP(r1[:, :nq], psb[3][:, :nq], [("ps", 3)], [r1k])
                            r2, r2k = rdrot.next()
                            RECIP(r2[:, :nq], psb[5][:, :nq], [("ps", 5)], [r2k])
                            TT("dve", ta[:, :nq], psb[2][:, :nq], r1[:, :nq], ALU.mult, [("ps", 2), r1k], ["ta"])
                            TT("dve", tb[:, :nq], psb[4][:, :nq], r2[:, :nq], ALU.mult, [("ps", 4), r2k], ["tb"])
                            yvt, yvk = yvrot.next()
                            STT("dve", yvt[:, :nq], tb[:, :nq], lamt[:, 4:5], ta[:, :nq], ALU.mult, ALU.add, ["ta", "tb", "lamt"], [yvk])

                            def part2(yvt=yvt, yvk=yvk, h=h, qoff=qoff, nq=nq):
                                headnorm(R, PsRot([6, 7]), yvt[:, :nq], yvk, 128, nq, lamt[:, 5:6], 1.0 / 128, cm[:, M_ONES, :],
                                         hy[:, 4 + h, qoff:qoff + nq], "hy")
                            pend.append(part2)
                    while pend:
                        pend.pop()()
                    P.barrier()
                out_proj(l, bi, ab_w_out, (lambda fc: hy[:, fc, :]), True)

        def mixer_cd(bi):
            l = 1
            with contextlib.ExitStack() as st0:
                hy = sbt(st0, "hy1", [128, 8, TOK], BF16)
                yD = sbt(st0, "yD", [128, 4, NX], BF16)
                compute_hall(l, bi, hy)
                srot = PsRot([0, 1])
                with contextlib.ExitStack() as st:
                    dq = sbt(st, "dq", [128, 4, NX], BF16)
                    dk = sbt(st, "dk", [128, 4, TOK], BF16)
                    vD = sbt(st, "vD", [128, 33, 512], BF16)
                    E2 = sbt(st, "E2", [128, 14, 512], BF16)
                    eg = sbt(st, "eg", [128, 2, 512], F32)
                    wr = Rot("w", [sbt(st, "dwr%d" % i, [128, 8, 512], BF16) for i in range(2)])
                    R = make_rots(st, "d")
                    prot = Rot("P", [sbt(st, "dP%d" % i, [128, 512], BF16) for i in range(3)])
                    rdrot = Rot("rd", [sbt(st, "drd%d" % i, [128, 512], F32) for i in range(2)])
                    P.barrier()
                    pr = PsRot([0, 1, 2, 3, 4, 5, 6, 7])
                    bd64 = cm[:, M_BD64, :]
                    for i in range(14):
                        s = i % 2
                        P.dma("sp", eg[:, s, :], rpbG_d[:, i, :], (), [("eg", s)], "eg%d" % s)
                        ACT(E2[:, i, :], eg[:, s, :], AF.Exp, [("eg", s)], ["E2"])

                    def load_w(src_ap, ncols):
                        wt, wk = wr.next()
                        P.dma("pool", wt[:, :, 0:ncols], src_ap.rearrange("(k p) f -> p k f", p=128), (), [wk], "w%d" % wk[1])
                        return wt, wk

                    wt, wk = load_w(cd_w_in[:, 768:1280], 512)
                    for (tok, n) in SUBS5[:4]:
                        for c in range(4):
                            pb, pk = proj_fm(pr, wt, wk, c * 128, 128, hy, tok, n)
                            headnorm(R, pr, pb[:, :n], pk, 128, n, cols[:, C_DQ:C_DQ + 1], 1.0 / 64, bd64, dq[:, c, tok:tok + n], ("dq", c))
                    wt, wk = load_w(cd_w_in[:, 1568:2080], 512)
                    for (tok, n) in SUBS5:
                        for c in range(4):
                            pb, pk = proj_fm(pr, wt, wk, c * 128, 128, hy, tok, n)
                            headnorm(R, pr, pb[:, :n], pk, 128, n, cols[:, C_DK:C_DK + 1], 1.0 / 64, bd64, dk[:, c, tok:tok + n], ("dk", c))
                    wt, wk = load_w(cd_w_in[:, 2080:2592], 512)

                    def wtok(w):
                        return 64 * w if w < 31 else NX + 128 * (w - 31)
                    for w in range(33):
                        pb, pk = pr.next()
                        t0 = wtok(w)
                        for k in range(8):
                            MM(pb[:, :], hy[:, k, t0:t0 + 128], wt[:, k, 0:512], k == 0, k == 7, [wk, "hy"], [pk])
                        if w % 2 == 0:
                            ACT(vD[:, w, :], pb[:, :], AF.Copy, [pk], ["vD"])
                        else:
                            COPY("dve", vD[:, w, :], pb[:, :], [pk], ["vD"])
                    orot = PsRot([2, 4])
                    drot = PsRot([3, 5])
                    srotE = PsRot([0, 6])
                    srotO = PsRot([1, 7])
                    for r in range(0 if (KSUB & 1) else 32):
                        rs_ = min(max(r - 4, 0), 24)
                        units = [(rs_ + 2 * i, rs_ + 2 * i - r + 7) for i in range(4)] + [(31, None), (32, None)]
                        Ob, Ok = orot.next()
                        Db, Dk = drot.next()
                        def s_fn(ui):
                            w, ei = units[ui]
                            t0 = wtok(w)
                            SbE, SkE = srotE.next()
                            SbO, SkO = srotO.next()
                            for h in range(8):
                                c, p0 = h // 2, (h % 2) * 64
                                Sb, Sk = (SbE, SkE) if h % 2 == 0 else (SbO, SkO)
                                MM(Sb[:, c * 64:(c + 1) * 64], dk[p0:p0 + 64, c, t0:t0 + 128], dq[p0:p0 + 64, c, r * 64:(r + 1) * 64],
                                   c == 0, c == 3, [("dk", c), ("dq", c)], [Sk], skip_group_check=True)
                            return SbE, SkE, SbO, SkO

                        def rest_fn(ui, sctx):
                            SbE, SkE, SbO, SkO = sctx
                            w, ei = units[ui]
                            Pt, Pk = prot.next()
                            PtV = Pt[:, :].rearrange("p (c two q) -> p c two q", two=2, q=64)
                            ACT(PtV[:, :, 0, :], SbE[:, 0:256].rearrange("p (c q) -> p c q", q=64), AF.Exp, [SkE], [Pk], scale=0.125)
                            ACT(PtV[:, :, 1, :], SbO[:, 0:256].rearrange("p (c q) -> p c q", q=64), AF.Exp, [SkO, Pk], [Pk], scale=0.125)
                            if ei is not None:
                                TT("dve", Pt[:, :], Pt[:, :], E2[:, ei, :], ALU.mult, [Pk, "E2"], [Pk])
                            for h in range(8):
                                MM(Ob[0:64, h * 64:(h + 1) * 64], vD[:, w, h * 64:(h + 1) * 64], Pt[:, h * 64:(h + 1) * 64],
                                   ui == 0 and h == 0, ui == len(units) - 1 and h == 7, ["vD", Pk], [Ok], skip_group_check=True)
                            MM(Db[0:64, :], cm[:, M_ONES, 0:64], Pt[:, :], ui == 0, ui == len(units) - 1, ["cm", Pk], [Dk])
                        pipelined(len(units), s_fn, rest_fn)
                        rd, rdk = rdrot.next()
                        RECIP(rd[0:64, :], Db[0:64, :], [Dk], [rdk])
                        for h in range(8):
                            c, p0 = h // 2, (h % 2) * 64
                            TT("dve", yD[p0:p0 + 64, c, r * 64:(r + 1) * 64], Ob[0:64, h * 64:(h + 1) * 64], rd[0:64, h * 64:(h + 1) * 64],
                               ALU.mult, [Ok, rdk], ["yD"])
                    P.barrier()
                with contextlib.ExitStack() as st:
                    cqg = sbt(st, "cqg", [128, 6, NX], BF16)
                    rsa = sbt(st, "rsa", [128, NX], F32)
                    ckvn = sbt(st, "ckvn", [128, 2, TOK], BF16)
                    krope = sbt(st, "krope", [32, TOK], BF16)
                    rC = sbt(st, "rCm", [128, NX], F32)
                    rS = sbt(st, "rSm", [128, NX], F32)
                    vC = sbt(st, "vC", [128, 18, 512], BF16)
                    qcat = sbt(st, "qcat", [96, NX], BF16)
                    kcat = sbt(st, "kcat", [96, TOK], BF16)
                    wr = Rot("w", [sbt(st, "cwr%d" % i, [128, 8, 512], BF16) for i in range(2)])
                    wqr = Rot("wq", [sbt(st, "wq%d" % i, [128, 6, 96], BF16) for i in range(2)])
                    wkr = Rot("wk", [sbt(st, "wk%d" % i, [128, 2, 64], BF16) for i in range(2)])
                    R = make_rots(st, "c")
                    prot = Rot("P", [sbt(st, "cP%d" % i, [128, 512], BF16) for i in range(3)])
                    rdrot = Rot("rd", [sbt(st, "crd%d" % i, [128, 512], F32) for i in range(2)])
                    P.barrier()
                    P.dma("sp", rC[:], ropeM_d[0], (), ["rope"], "rC")
                    P.dma("sp", rS[:], ropeM_d[1], (), ["rope"], "rS")
                    pr = PsRot([2, 3, 4, 5, 6, 7])

                    def load_w(src_ap, ncols):
                        wt, wk = wr.next()
                        P.dma("pool", wt[:, :, 0:ncols], src_ap.rearrange("(k p) f -> p k f", p=128), (), [wk], "w%d" % wk[1])
                        return wt, wk

                    wa, wak = load_w(cd_w_in[:, 0:512], 512)
                    wb, wbk = load_w(cd_w_in[:, 512:768], 256)
                    for (tok, n) in SUBS5[:4]:
                        ssb, ssk = bank(0 + (tok // 512) % 2)
                        for c in range(6):
                            wt, wk, cc = (wa, wak, c) if c < 4 else (wb, wbk, c - 4)
                            pb, pk = proj_fm(pr, wt, wk, cc * 128, 128, hy, tok, n)
                            sq, sqk = R["sq"].next()
                            ACT(sq[:, :n], pb[:, :n], AF.Square, [pk], [sqk])
                            MM(ssb[:, :n], cm[:, M_ONES, :], sq[:, :n], c == 0, c == 5, [sqk, "cm"], [ssk])
                            TS("dve", cqg[:, c, tok:tok + n], pb[:, :n], cols[:, C_QA + c:C_QA + c + 1], ALU.mult, [pk, "cols", sqk], ["cqg"])
                        ACT(rsa[:, tok:tok + n], ssb[:, :n], AF.Sqrt, [ssk], ["rsa"], bias=EPS, scale=1.0 / 768)
                        RECIP(rsa[:, tok:tok + n], rsa[:, tok:tok + n], ["rsa"], ["rsa"])
                    wt, wk = load_w(cd_w_in[:, 1280:1568], 288)
                    rope32 = (cm[0:32, M_PERM32, 0:32], rC, rS, 0, 32)
                    for (tok, n) in SUBS5:
                        ssb, ssk = bank(0 + (tok // 512) % 2)
                        pbs = []
                        for c in range(2):
                            pb, pk = proj_fm(pr, wt, wk, c * 128, 128, hy, tok, n)
                            pbs.append((pb, pk))
                            sq, sqk = R["sq"].next()
                            ACT(sq[:, :n], pb[:, :n], AF.Square, [pk], [sqk])
                            MM(ssb[:, :n], cm[:, M_ONES, :], sq[:, :n], c == 0, c == 1, [sqk, "cm"], [ssk])
                        rs, rsk = R["rstd"].next()
                        ACT(rs[:, :n], ssb[:, :n], AF.Sqrt, [ssk], [rsk], bias=EPS, scale=1.0 / 256)
                        RECIP(rs[:, :n], rs[:, :n], [rsk], [rsk])
                        for c in range(2):
                            pb, pk = pbs[c]
                            STT("dve", ckvn[:, c, tok:tok + n], pb[:, :n], cols[:, C_KVA + c:C_KVA + c + 1], rs[:, :n], ALU.mult, ALU.mult,
                                [pk, rsk, "cols"], ["ckvn"])
                        pb, pk = proj_fm(pr, wt, wk, 256, 32, hy, tok, n)
                        headnorm(R, pr, pb[0:32, :n], pk, 32, n, cols[0:32, C_KR:C_KR + 1], 1.0 / 32, cm[0:32, M_ONES, 0:32],
                                 krope[0:32, tok:tok + n], "krope", rope=(rope32 if tok < NX else None), tok=tok)
                    wt, wk = wr.next()
                    for kk in range(2):
                        P.dma("pool", wt[:, kk, 0:512].rearrange("p (h e) -> p h e", e=64),
                              c_w_ukv[kk * 128:(kk + 1) * 128, :].rearrange("p (h e) -> p h e", e=128)[:, :, 64:128], (), [wk], "w%d" % wk[1])
                    for tt in range(18):
                        pb, pk = pr.next()
                        for c in range(2):
                            MM(pb[:, :], ckvn[:, c, tt * 128:(tt + 1) * 128], wt[:, c, 0:512], c == 0, c == 1, [wk, "ckvn"], [pk])
                        ACT(vC[:, tt, :], pb[:, :], AF.Copy, [pk], ["vC"])
                    orot = PsRot([2, 4])
                    drot = PsRot([3, 5])
                    pr2 = PsRot([6, 7])
                    allk = [16, 17] + list(range(16))
                    sc = 96.0 ** -0.5
                    for h in range(0 if (KSUB & 2) else 8):
                        c_, p0 = h // 2, (h % 2) * 64
                        wq, wqk = wqr.next()
                        P.dma("pool", wq[:], c_w_uq[:, h * 96:(h + 1) * 96].rearrange("(k p) f -> p k f", p=128), (), [wqk], "wq%d" % wqk[1])
                        wkk, wkkk = wkr.next()
                        P.dma("pool", wkk[:], c_w_ukv[:, h * 128:h * 128 + 64].rearrange("(k p) f -> p k f", p=128), (), [wkkk], "wk%d" % wkkk[1])
                        for (tok, n) in SUBS5[:4]:
                            pb, pk = pr2.next()
                            for c in range(6):
                                MM(pb[0:96, :n], wq[:, c, :], cqg[:, c, tok:tok + n], c == 0, c == 5, [wqk, "cqg"], [pk])
                            qs, qsk = R["tmp"].next()
                            TT("dve", qs[0:96, :n], pb[0:96, :n], rsa[0:96, tok:tok + n], ALU.mult, [pk, "rsa"], [qsk])
                            headnorm(R, pr2, qs[0:96, :n], qsk, 96, n, cols[0:96, C_Q96:C_Q96 + 1], cols[0:96, C_INV96:C_INV96 + 1],
                                     cm[0:96, M_BD96, 0:96], qcat[0:96, tok:tok + n], "qcat",
                                     rope=(cm[0:96, M_PERM96, 0:96], rC, rS, 64, 96), tok=tok)
                        for (tok, n) in SUBS5:
                            pb, pk = pr2.next()
                            for c in range(2):
                                MM(pb[0:64, :n], wkk[:, c, :], ckvn[:, c, tok:tok + n], c == 0, c == 1, [wkkk, "ckvn"], [pk])
                            headnorm(R, pr2, pb[0:64, :n], pk, 64, n, cols[0:64, C_KN:C_KN + 1], 1.0 / 64, cm[0:64, M_ONES, 0:64],
                                     kcat[0:64, tok:tok + n], "kcat")
                        P.dma("sp", kcat[64:96, :], krope[0:32, :], ["krope"], ["kcat"], "kcr")
                        for qg in range(4):
                            qoff = qg * 512
                            Ob, Ok = orot.next()
                            Db, Dk = drot.next()
                            def s_fn(i):
                                kt = allk[i]
                                Sb, Sk = srot.next()
                                MM(Sb[:, :], kcat[0:96, kt * 128:(kt + 1) * 128], qcat[0:96, qoff:qoff + 512], True, True, ["kcat", "qcat"], [Sk])
                                return Sb, Sk

                            def rest_fn(i, sctx):
                                Sb, Sk = sctx
                                kt = allk[i]
                                Pt, Pk = prot.next()
                                ACT(Pt[:, :], Sb[:, :], AF.Exp, [Sk], [Pk], scale=sc)
                                MM(Ob[0:64, :], vC[:, kt, h * 64:(h + 1) * 64], Pt[:, :], i == 0, i == 17, ["vC", Pk], [Ok])
                                MM(Db[0:64, :], cm[:, M_ONES, 0:64], Pt[:, :], i == 0, i == 17, ["cm", Pk], [Dk])
                            pipelined(18, s_fn, rest_fn)
                            rd, rdk = rdrot.next()
                            RECIP(rd[0:64, :], Db[0:64, :], [Dk], [rdk])
                            TT("dve", hy[p0:p0 + 64, c_, qoff:qoff + 512], Ob[0:64, :], rd[0:64, :], ALU.mult, [Ok, rdk], ["hy"])
                    P.barrier()
                out_proj(l, bi, cd_w_out, (lambda fc: hy[:, fc, :] if fc < 4 else yD[:, fc - 4, :]), False)

        for bi in range(2):
            ffn(0, 0, bi, "in", "xs" if stage > 1 else "out", True)
            if stage <= 1:
                continue
            mixer_ab(bi)
            if stage <= 2:
                _dump(P, nc, top, xs, outT, bi, final_ops)
                continue
            ffn(0, 1, bi, "xs", "xs" if stage > 3 else "out", True)
            if stage <= 3:
                continue
            ffn(1, 0, bi, "xs", "xs" if stage > 4 else "out", True)
            if stage <= 4:
                continue
            mixer_cd(bi)
            if stage <= 5:
                _dump(P, nc, top, xs, outT, bi, final_ops)
                continue
            ffn(1, 1, bi, "xs", "out", False)
        P.emit(final_ops)
    return nc


def _dump(P, nc, top, xs, outT, bi, final_ops):
    P.barrier()
    op = P.dma("sp", outT[bi], xs[:, :, 0:NX], [("xs", 0), ("xs", 1), ("xs", 2)], [("out", bi, "d")], "dump")
    final_ops.append(op)
    P.barrier()


def _rope_tables(rot_dim):
    nf = rot_dim // 4
    inv = (10000.0 ** (-np.arange(nf, dtype=np.float64) / nf))
    t = np.arange(NX)
    row = (t // 64).astype(np.float64)
    col = (t % 64).astype(np.float64)
    C = np.zeros((rot_dim, NX), np.float64)
    S = np.zeros((rot_dim, NX), np.float64)
    for d in range(rot_dim):
        axis = d // (2 * nf)
        half = (d % (2 * nf)) // nf
        f = d % nf
        ang = (row if axis == 0 else col) * inv[f]
        C[d] = np.cos(ang)
        S[d] = np.sin(ang) * (-1.0 if half == 0 else 1.0)
    return C.astype(np.float32), S.astype(np.float32)


def _perm(rot_dim):
    nf = rot_dim // 4
    Pm = np.zeros((rot_dim, rot_dim), np.float32)
    for d in range(rot_dim):
        half = (d % (2 * nf)) // nf
        partner = d + nf if half == 0 else d - nf
        Pm[partner, d] = 1.0
    return Pm


def _constants():
    cmat = np.zeros((128, 6, 128), np.float32)
    cmat[:, M_ONES, :] = 1.0
    cmat[0:64, M_BD64, 0:64] = 1.0
    cmat[64:128, M_BD64, 64:128] = 1.0
    p64 = _perm(64)
    cmat[0:64, M_PERM64, 0:64] = p64
    cmat[64:128, M_PERM64, 64:128] = p64
    cmat[0:64, M_BD96, 0:64] = 1.0
    cmat[64:96, M_BD96, 64:96] = 1.0
    p32 = _perm(32)
    cmat[0:64, M_PERM96, 0:64] = np.eye(64, dtype=np.float32)
    cmat[64:96, M_PERM96, 64:96] = p32
    cmat[0:32, M_PERM32, 0:32] = p32
    C64, S64 = _rope_tables(64)
    rope64 = np.stack([np.concatenate([C64, C64]), np.concatenate([S64, S64])]).astype(np.float32)
    C32, S32 = _rope_tables(32)
    ropeM = np.zeros((2, 128, NX), np.float32)
    ropeM[0] = 1.0
    ropeM[0, 0:32] = C32
    ropeM[1, 0:32] = S32
    ropeM[0, 64:96] = C32
    ropeM[1, 64:96] = S32
    maskA = np.zeros((128, 6, 512), np.float32)
    jj = np.arange(128)[:, None]
    qq = np.arange(512)[None, :]
    b = qq // 128
    ii = qq % 128
    for mi in range(6):
        d = mi - 1 - b
        maskA[:, mi, :] = (np.abs(ii - jj - 128 * d) <= 128).astype(np.float32)
    return cmat, rope64, ropeM, maskA


def _rpb_gather(rpb):
    kc = np.arange(64)[:, None]
    qc = np.arange(64)[None, :]
    dc = np.clip(kc - qc, -15, 15) + 15
    cs = np.clip(qc - 8, 0, 48)
    valid = (kc >= cs) & (kc < cs + 16)
    G = np.empty((2, 64, 14, 8, 64), np.float32)
    for rr in range(2):
        for i in range(14):
            g = rpb[:, i + rr, :][:, dc]
            g = np.where(valid[None], g, np.float32(-100.0))
            G[rr, :, i, :, :] = np.transpose(g, (1, 0, 2))
    return np.ascontiguousarray(G.reshape(128, 14, 512))


def _prep(inp):
    f = lambda a: np.ascontiguousarray(np.asarray(a, dtype=np.float32))
    cmat, rope64, ropeM, maskA = _constants()
    shared = {
        "w_mod": f(inp["w_mod"]),
        "b_modT": f(np.transpose(np.asarray(inp["b_mod"]).reshape(2, 72, 128), (2, 0, 1))),
        "f1g": f(inp["ffn1_w_gate"]), "f1u": f(inp["ffn1_w_up"]), "f1d": f(inp["ffn1_w_down"]),
        "f2g": f(inp["ffn2_w_gate"]), "f2u": f(inp["ffn2_w_up"]), "f2d": f(inp["ffn2_w_down"]),
        "ab_w_in": f(inp["ab_w_in"][0]), "ab_w_out": f(inp["ab_w_out"][0]),
        "cd_w_in": f(inp["cd_w_in"][0]), "cd_w_out": f(inp["cd_w_out"][0]),
        "c_w_uq": f(inp["c_w_uq"][0]), "c_w_ukv": f(inp["c_w_ukv"][0]),
        "cmat": cmat, "rope64": rope64, "ropeM": ropeM, "maskA": maskA,
        "rpbG": _rpb_gather(np.asarray(inp["d_rpb"][0], dtype=np.float32)),
    }
    cols = np.zeros((128, NCOL), np.float32)
    g = lambda k: np.asarray(inp[k][0], dtype=np.float32)
    cols[:, C_AQ] = np.tile(g("a_q_norm"), 2)
    cols[:, C_AK] = np.tile(g("a_k_norm"), 2)
    cols[:, C_BQ] = np.tile(g("b_q_norm"), 2)
    cols[:, C_BK] = np.tile(g("b_k_norm"), 2)
    cols[:, C_BSUB] = g("b_sub_norm")
    cols[:, C_SINK:C_SINK + 8] = g("a_sink")[None, :]
    cols[0:64, C_LQ1] = g("b_lambda_q1")
    cols[0:64, C_LK1] = g("b_lambda_k1")
    cols[0:64, C_LQ2] = g("b_lambda_q2")
    cols[0:64, C_LK2] = g("b_lambda_k2")
    cols[:, C_QA:C_QA + 6] = g("c_q_a_norm").reshape(6, 128).T
    cols[:, C_KVA:C_KVA + 2] = g("c_kv_a_norm").reshape(2, 128).T
    cols[:, C_Q96] = 1.0
    cols[0:64, C_Q96] = g("c_q_nope_norm")
    cols[64:96, C_Q96] = g("c_q_rope_norm")
    cols[:, C_INV96] = 1.0 / 64
    cols[64:96, C_INV96] = 1.0 / 32
    cols[:, C_KN] = np.tile(g("c_k_nope_norm"), 2)
    cols[:, C_KR] = np.tile(g("c_k_rope_norm"), 4)
    cols[:, C_DQ] = np.tile(g("d_q_norm"), 2)
    cols[:, C_DK] = np.tile(g("d_k_norm"), 2)
    shared["cols"] = cols
    x = np.asarray(inp["x"], dtype=np.float32)
    ctx = np.asarray(inp["ctx"], dtype=np.float32)
    c = np.asarray(inp["c"], dtype=np.float32)
    c_ctx = np.asarray(inp["c_ctx"], dtype=np.float32)
    in_maps = []
    for core in range(8):
        b0 = 2 * core
        m = dict(shared)
        m["xT"] = np.ascontiguousarray(np.transpose(x[b0:b0 + 2], (0, 2, 1)).reshape(2, 8, 128, NX))
        m["ctxT"] = np.ascontiguousarray(np.transpose(ctx[b0:b0 + 2], (0, 2, 1)).reshape(2, 8, 128, NCTX))
        cc = np.stack([c[b0], c[b0 + 1], c_ctx], axis=-1)
        m["cT"] = np.ascontiguousarray(np.transpose(cc.reshape(8, 128, 3), (1, 0, 2)))
        in_maps.append(m)
    return in_maps


def kernel(**inputs):
    stage = int(os.environ.get("KSTAGE", "99"))
    ncores = int(os.environ.get("KCORES", "8"))
    in_maps = _prep(inputs)
    nc = build(stage)
    res = run_bass_kernel_spmd(nc, in_maps[:ncores], core_ids=list(range(ncores)))
    out = np.zeros((16, NX, 1024), np.float32)
    for core in range(ncores):
        o = np.asarray(res.results[core]["outT"]).reshape(2, 1024, NX)
        out[2 * core:2 * core + 2] = np.transpose(o, (0, 2, 1))
    return out
```

```python
import contextlib
import math
import os
import numpy as np
import concourse.bass as bass
import concourse.mybir as mybir
from concourse.bass_utils import run_bass_kernel_spmd

F32 = mybir.dt.float32
BF16 = mybir.dt.bfloat16
AF = mybir.ActivationFunctionType
ALU = mybir.AluOpType
EPS = 1e-6


class Op:
    __slots__ = ("eng", "fn", "waits", "dwaits", "inc", "semval", "dma")

    def __init__(self, eng, fn):
        self.eng = eng
        self.fn = fn
        self.waits = []
        self.dwaits = []
        self.inc = False
        self.semval = None
        self.dma = None


class Prog:
    ENGS = ("pe", "act", "dve", "pool", "sp")

    def __init__(self, nc):
        self.nc = nc
        self.ops = {e: [] for e in self.ENGS}
        self.last_write = {}
        self.readers = {}
        self.dma_sem_of = {}
        self.dma_sem_cnt = []
        self.pending = {}

    def dma_sem(self, name):
        if name not in self.dma_sem_of:
            self.dma_sem_of[name] = len(self.dma_sem_cnt)
            self.dma_sem_cnt.append(0)
        return self.dma_sem_of[name]

    def _dep(self, op, prod):
        if prod is None or prod is op:
            return
        if prod.dma is not None:
            op.dwaits.append(prod.dma)
        elif prod.eng != op.eng or prod.eng != "pe":
            op.waits.append(prod)
            prod.inc = True

    def barrier(self):
        snap = []
        for e in self.ENGS:
            if self.ops[e]:
                o = self.ops[e][-1]
                if o.dma is None:
                    o.inc = True
                    snap.append(o)
        dsnap = [(i, c) for i, c in enumerate(self.dma_sem_cnt) if c > 0]
        self.pending = {e: (snap, dsnap) for e in self.ENGS}

    def add(self, eng, fn, reads=(), writes=(), dma_sem=None):
        op = Op(eng, fn)
        pend = self.pending.pop(eng, None)
        if pend is not None:
            for o in pend[0]:
                if o.eng != eng:
                    op.waits.append(o)
            op.dwaits.extend(pend[1])
        for k in reads:
            self._dep(op, self.last_write.get(k))
        for k in writes:
            self._dep(op, self.last_write.get(k))
            for r in self.readers.get(k, {}).values():
                self._dep(op, r)
        if dma_sem is not None:
            si = self.dma_sem(dma_sem)
            self.dma_sem_cnt[si] += 16
            op.dma = (si, self.dma_sem_cnt[si])
        for k in reads:
            rk = (eng, op.dma[0]) if op.dma is not None else eng
            self.readers.setdefault(k, {})[rk] = op
        for k in writes:
            self.last_write[k] = op
            self.readers[k] = {}
        self.ops[eng].append(op)
        return op

    def dma(self, q, out, in_, reads=(), writes=(), sem=None):
        return self.add(q, lambda e: e.dma_start(out=out, in_=in_), reads, writes, dma_sem=sem)

    def emit(self, final_ops=()):
        nc = self.nc
        for e in self.ENGS:
            c = 0
            for op in self.ops[e]:
                if op.inc and op.dma is None:
                    c += 1
                    op.semval = c
        with contextlib.ExitStack() as st:
            esem = {e: st.enter_context(nc.semaphore("s_" + e)) for e in self.ENGS}
            dsem = [st.enter_context(nc.semaphore("d_%d" % i)) for i in range(len(self.dma_sem_cnt))]
            block = st.enter_context(nc.Block())
            prog = self

            def run(e, eng):
                waited = {f: 0 for f in prog.ENGS}
                dwaited = {}
                for op in prog.ops[e]:
                    need = {}
                    for w in op.waits:
                        if w.semval > waited[w.eng] and w.semval > need.get(w.eng, 0):
                            need[w.eng] = w.semval
                    for f, v in need.items():
                        eng.wait_ge(esem[f], v)
                        waited[f] = v
                    dneed = {}
                    for si, v in op.dwaits:
                        if v > dwaited.get(si, 0) and v > dneed.get(si, 0):
                            dneed[si] = v
                    for si, v in dneed.items():
                        eng.wait_ge(dsem[si], v)
                        dwaited[si] = v
                    ins = op.fn(eng)
                    if op.dma is not None:
                        ins.then_inc(dsem[op.dma[0]], 16)
                    elif op.inc:
                        ins.then_inc(esem[e], 1)
                if e == "sp":
                    for op in final_ops:
                        si, v = op.dma
                        if v > dwaited.get(si, 0):
                            eng.wait_ge(dsem[si], v)
                            dwaited[si] = v

            block.tensor(lambda eng: run("pe", eng))
            block.scalar(lambda eng: run("act", eng))
            block.vector(lambda eng: run("dve", eng))
            block.gpsimd(lambda eng: run("pool", eng))
            block.sync(lambda eng: run("sp", eng))


class Rot:
    def __init__(self, name, tensors):
        self.name = name
        self.t = tensors
        self.i = 0

    def next(self):
        s = self.i % len(self.t)
        self.i += 1
        return self.t[s], (self.name, s)


NX = 2048
NCTX = 256
TOK = NX + NCTX
NFF = 22
TILES = [(0, 768), (768, 768), (1536, 768)]
SUBS5 = [(0, 512), (512, 512), (1024, 512), (1536, 512), (2048, 256)]
NCOL = 32
KSUB = int(os.environ.get('KSUB', '0'))
LAM_INIT0 = 0.8 - 0.6 * math.exp(-0.3 * 0)

C_AQ, C_AK, C_BQ, C_BK, C_BSUB, C_SINK, C_LQ1, C_LK1, C_LQ2, C_LK2 = 0, 1, 2, 3, 4, 5, 13, 14, 15, 16
C_QA, C_KVA, C_Q96, C_INV96, C_KN, C_KR, C_DQ, C_DK = 17, 23, 25, 26, 27, 28, 29, 30
M_ONES, M_BD64, M_PERM64, M_BD96, M_PERM96, M_PERM32 = 0, 1, 2, 3, 4, 5


def build(stage=99):
    nc = bass.Bass("TRN2", target_bir_lowering=False)

    def din(name, shape):
        return nc.dram_tensor(name, list(shape), F32, kind="ExternalInput").ap()

    xT = din("xT", [2, 8, 128, NX])
    ctxT = din("ctxT", [2, 8, 128, NCTX])
    cT = din("cT", [128, 8, 3])
    w_mod = din("w_mod", [2, 1024, 9216])
    b_modT = din("b_modT", [128, 2, 72])
    fgw = [din("f1g", [2, 1024, 2816]), din("f2g", [2, 1024, 2816])]
    fuw = [din("f1u", [2, 1024, 2816]), din("f2u", [2, 1024, 2816])]
    fdw = [din("f1d", [2, 2816, 1024]), din("f2d", [2, 2816, 1024])]
    ab_w_in = din("ab_w_in", [1024, 2304])
    ab_w_out = din("ab_w_out", [1024, 1024])
    cd_w_in = din("cd_w_in", [1024, 2592])
    cd_w_out = din("cd_w_out", [1024, 1024])
    c_w_uq = din("c_w_uq", [768, 768])
    c_w_ukv = din("c_w_ukv", [256, 1024])
    cols_d = din("cols", [128, NCOL])
    cmat_d = din("cmat", [128, 6, 128])
    rope64_d = din("rope64", [2, 128, NX])
    ropeM_d = din("ropeM", [2, 128, NX])
    maskA_d = din("maskA", [128, 6, 512])
    rpbG_d = din("rpbG", [128, 14, 512])
    outT = nc.dram_tensor("outT", [2, 8, 128, NX], F32, kind="ExternalOutput").ap()
    xs = nc.dram_tensor("xs_scratch", [8, 128, TOK], F32).ap()
    wc_g = [[nc.dram_tensor("wcg%d%d" % (w, l), [11, 128, 8, 256], BF16).ap() for l in range(2)] for w in range(2)]
    wc_u = [[nc.dram_tensor("wcu%d%d" % (w, l), [11, 128, 8, 256], BF16).ap() for l in range(2)] for w in range(2)]
    wc_d = [[nc.dram_tensor("wcd%d%d" % (w, l), [2, 4, 128, 6, 512], BF16).ap() for l in range(2)] for w in range(2)]

    P = Prog(nc)
    final_ops = []

    def ACT(out, in_, func, r, w, **kw):
        P.add("act", lambda e: e.activation(out=out, in_=in_, func=func, **kw), r, w)

    def MM(out, lhsT, rhs, start, stop, r, w, **kw):
        P.add("pe", lambda e: e.matmul(out, lhsT, rhs, start=start, stop=stop, **kw), r, w)

    def TT(eng, out, in0, in1, op, r, w):
        P.add(eng, lambda e: e.tensor_tensor(out=out, in0=in0, in1=in1, op=op), r, w)

    def TS(eng, out, in0, s1, op0, r, w):
        P.add(eng, lambda e: e.tensor_scalar(out=out, in0=in0, scalar1=s1, scalar2=None, op0=op0), r, w)

    def STT(eng, out, in0, scalar, in1, op0, op1, r, w):
        P.add(eng, lambda e: e.scalar_tensor_tensor(out=out, in0=in0, scalar=scalar, in1=in1, op0=op0, op1=op1), r, w)

    def RECIP(out, in_, r, w):
        P.add("dve", lambda e: e.reciprocal(out=out, in_=in_), r, w)

    def COPY(eng, out, in_, r, w):
        P.add(eng, lambda e: e.tensor_copy(out=out, in_=in_), r, w)

    with contextlib.ExitStack() as top:
        uid = [0]

        def sbt(st, name, shape, dt):
            uid[0] += 1
            return st.enter_context(nc.sbuf_tensor("sb_%s_%d" % (name, uid[0]), list(shape), dt))

        psb = [top.enter_context(nc.psum_tensor("ps%d" % i, [128, 512], F32)) for i in range(8)]

        def bank(i):
            return psb[i], ("ps", i)

        c_sb = sbt(top, "c_sb", [128, 8, 3], F32)
        c_act = sbt(top, "c_act", [128, 8, 3], BF16)
        bm = sbt(top, "bm", [128, 2, 72], F32)
        cols = sbt(top, "cols", [128, NCOL], F32)
        cm = sbt(top, "cm", [128, 6, 128], BF16)
        ones_f = sbt(top, "ones_f", [128, 128], F32)
        modT = sbt(top, "modT", [128, 2, 72, 3], F32)
        esink = sbt(top, "esink", [128, 8], F32)
        lamt = sbt(top, "lamt", [128, 8], F32)

        P.dma("sp", c_sb[:], cT, writes=["c_sb"], sem="c")
        P.dma("sp", bm[:], b_modT, writes=["bm"], sem="bm")
        P.dma("sp", cols[:], cols_d, writes=["cols"], sem="cols")
        P.dma("pool", cm[:], cmat_d, writes=["cm"], sem="cm")
        P.add("pool", lambda e: e.memset(ones_f[:], 1.0), (), ["ones_f"])
        ACT(c_act[:], c_sb[:], AF.Silu, ["c_sb"], ["c_act"])

        def mcol(l, i, c, j):
            return modT[:, l, 8 * i + c, j:j + 1]

        with contextlib.ExitStack() as st:
            wm = [sbt(st, "wm%d" % i, [128, 8, 512], BF16) for i in range(2)]
            it = 0
            for l in range(2):
                for sl in range(18):
                    s = it % 2
                    P.dma("pool", wm[s][:], w_mod[l][:, sl * 512:(sl + 1) * 512].rearrange("(k p) f -> p k f", p=128),
                          writes=[("wm", s)], sem="wm%d" % s)
                    pb, pk = bank(it % 2)
                    for cc in range(4):
                        for k in range(8):
                            MM(pb[:, cc * 4:cc * 4 + 3], wm[s][:, k, cc * 128:(cc + 1) * 128], c_act[:, k, :],
                               k == 0, k == 7, [("wm", s), "c_act"], [pk])
                    for cc in range(4):
                        col = sl * 4 + cc
                        TS("dve", modT[:, l, col, :], pb[:, cc * 4:cc * 4 + 3], bm[:, l, col:col + 1], ALU.add,
                           [pk, "bm"], ["modT"])
                    it += 1
            for l in range(2):
                for i in (1, 4, 7):
                    TS("dve", modT[:, l, 8 * i:8 * i + 8, :], modT[:, l, 8 * i:8 * i + 8, :], 1.0, ALU.add, ["modT"], ["modT"])
                for i in (2, 8):
                    TS("dve", modT[:, l, 8 * i:8 * i + 8, :], modT[:, l, 8 * i:8 * i + 8, :], 0.5, ALU.mult, ["modT"], ["modT"])
            ACT(esink[:], cols[:, C_SINK:C_SINK + 8], AF.Exp, ["cols"], ["esink"])
            TT("dve", lamt[:, 0:1], cols[:, C_LQ1:C_LQ1 + 1], cols[:, C_LK1:C_LK1 + 1], ALU.mult, ["cols"], ["lamt"])
            TT("dve", lamt[:, 1:2], cols[:, C_LQ2:C_LQ2 + 1], cols[:, C_LK2:C_LK2 + 1], ALU.mult, ["cols"], ["lamt"])
            pb, pk = bank(2)
            MM(pb[:, 0:2], ones_f[:], lamt[:, 0:2], True, True, ["ones_f", "lamt"], [pk])
            ACT(lamt[:, 2:4], pb[:, 0:2], AF.Exp, [pk], ["lamt"])
            TT("dve", lamt[:, 4:5], lamt[:, 3:4], lamt[:, 2:3], ALU.subtract, ["lamt"], ["lamt"])
            TS("dve", lamt[:, 4:5], lamt[:, 4:5], -LAM_INIT0, ALU.add, ["lamt"], ["lamt"])
            TS("dve", lamt[:, 5:6], cols[:, C_BSUB:C_BSUB + 1], 1.0 - LAM_INIT0, ALU.mult, ["cols", "lamt"], ["lamt"])
            P.barrier()

        def load_x(xrot, bi, ti, src):
            t, key = xrot.next()
            off = TILES[ti][0]
            sem = "x%d" % key[1]
            if src == "in":
                if ti < 2:
                    P.dma("sp", t[:, :, :], xT[bi][:, :, off:off + 768].rearrange("c p t -> p c t"), (), [key], sem)
                else:
                    P.dma("sp", t[:, :, 0:512], xT[bi][:, :, 1536:2048].rearrange("c p t -> p c t"), (), [key], sem)
                    P.dma("sp", t[:, :, 512:768], ctxT[bi].rearrange("c p t -> p c t"), (), [key], sem)
            else:
                P.dma("sp", t[:, :, :], xs[:, :, off:off + 768].rearrange("c p t -> p c t"), [("xs", ti)], [key], sem)
            return t, key

        def store_x(t, key, bi, ti, dst):
            off = TILES[ti][0]
            if dst == "out":
                if ti < 2:
                    op = P.dma("sp", outT[bi][:, :, off:off + 768].rearrange("c p t -> p c t"), t[:, :, :], [key], [("out", bi, ti)], "o%d" % key[1])
                else:
                    op = P.dma("sp", outT[bi][:, :, 1536:2048].rearrange("c p t -> p c t"), t[:, :, 0:512], [key], [("out", bi, ti)], "o%d" % key[1])
                final_ops.append(op)
            else:
                P.dma("sp", xs[:, :, off:off + 768].rearrange("c p t -> p c t"), t[:, :, :], [key], [("xs", ti)], "o%d" % key[1])

        def tile_subs(bi, ti, with_ctx=True):
            off = TILES[ti][0]
            if ti < 2:
                return [(0, 512, bi, off), (512, 256, bi, off + 512)]
            s = [(0, 512, bi, off)]
            if with_ctx:
                s.append((512, 256, 2, off + 512))
            return s

        def norm_mod(T, xt, xkey, loff, n, l, j, i_shift, i_scale, dst_fn, dst_key):
            ssb, ssk = T["ss"].next()
            for c in range(8):
                sq, sqk = T["sq"].next()
                ACT(sq[:, :n], xt[:, c, loff:loff + n], AF.Square, [xkey], [sqk])
                MM(ssb[:, :n], cm[:, M_ONES, :], sq[:, :n], c == 0, c == 7, [sqk, "cm"], [ssk])
            rs, rsk = T["rstd"].next()
            ACT(rs[:, :n], ssb[:, :n], AF.Sqrt, [ssk], [rsk], bias=EPS, scale=1.0 / 1024)
            RECIP(rs[:, :n], rs[:, :n], [rsk], [rsk])
            for c in range(8):
                tm, tmk = T["tmp"].next()
                STT("dve", tm[:, :n], xt[:, c, loff:loff + n], mcol(l, i_scale, c, j), rs[:, :n], ALU.mult, ALU.mult,
                    [xkey, rsk, "modT"], [tmk])
                ACT(dst_fn(c), tm[:, :n], AF.Identity, [tmk, "modT"], [dst_key], bias=mcol(l, i_shift, c, j), scale=1.0)

        FD_PIECES = [(0, 6), (6, 12), (12, 17), (17, 22)]

        def ffn(l, which, bi, src, dst, with_ctx):
            i_shift, i_scale, i_gate = (0, 1, 2) if which == 0 else (6, 7, 8)
            with contextlib.ExitStack() as st:
                xrot = Rot("xt", [sbt(st, "xt%d" % i, [128, 8, 768], F32) for i in range(2)])
                hT = sbt(st, "hT", [128, 8, 768], BF16)
                gT = sbt(st, "gT", [128, NFF, 768], BF16)
                T = make_rots(st, "f")
                T["ss"] = PsRot([6, 7])
                srot = Rot("s", [sbt(st, "s%d" % i, [128, 512], F32) for i in range(3)])
                wgu = Rot("wgu", [sbt(st, "wgu%d" % i, [128, 8, 256], BF16) for i in range(12)])
                wdr = Rot("wd", [sbt(st, "wd%d" % i, [128, 6, 512], BF16) for i in range(4)])
                P.barrier()
                pair = 0
                nxt = load_x(xrot, bi, 0, src)
                for ti in range(3):
                    subs = tile_subs(bi, ti, with_ctx)
                    xt, xkey = nxt
                    for si, (loff, n, j, tok) in enumerate(subs):
                        norm_mod(T, xt, xkey, loff, n, l, j, i_shift, i_scale,
                                 (lambda c, loff=loff, n=n: hT[:, c, loff:loff + n]), ("hT", si))
                    if ti + 1 < 3:
                        nxt = load_x(xrot, bi, ti + 1, src)
                    for g in range(11):
                        f0 = g * 256
                        wgt, wgk = wgu.next()
                        wut, wuk = wgu.next()
                        first = (bi == 0 and ti == 0)
                        for (wt_, wk_, src_, cache_, cn_) in ((wgt, wgk, fgw, wc_g, "wcg"), (wut, wuk, fuw, wc_u, "wcu")):
                            ck = (cn_, which, l, g)
                            if first:
                                P.dma("pool", wt_[:], src_[which][l][:, f0:f0 + 256].rearrange("(k p) f -> p k f", p=128), (), [wk_], "wgu%d" % wk_[1])
                                P.dma("sp", cache_[which][l][g], wt_[:], [wk_], [ck], "wcs%d" % wk_[1])
                            else:
                                P.dma("sp", wt_[:], cache_[which][l][g], [ck], [wk_], "wgub%d" % wk_[1])
                        for fc in range(2):
                            f = g * 2 + fc
                            for si, (loff, n, j, tok) in enumerate(subs):
                                Gb, Gk = bank(2 * (pair % 3))
                                Ub, Uk = bank(2 * (pair % 3) + 1)
                                pair += 1
                                for k in range(8):
                                    MM(Gb[:, :n], wgt[:, k, fc * 128:(fc + 1) * 128], hT[:, k, loff:loff + n], k == 0, k == 7,
                                       [wgk, ("hT", si)], [Gk])
                                for k in range(8):
                                    MM(Ub[:, :n], wut[:, k, fc * 128:(fc + 1) * 128], hT[:, k, loff:loff + n], k == 0, k == 7,
                                       [wuk, ("hT", si)], [Uk])
                                sg, sk = srot.next()
                                ACT(sg[:, :n], Gb[:, :n], AF.Silu, [Gk], [sk])
                                TT("dve", gT[:, f, loff:loff + n], Ub[:, :n], sg[:, :n], ALU.mult, [Uk, sk], [("gT", si)])
                    for half in range(2):
                        for (fc0, fc1) in FD_PIECES:
                            wdt, wdk = wdr.next()
                            pi = FD_PIECES.index((fc0, fc1))
                            ck = ("wcd", which, l, half, pi)
                            if bi == 0 and ti == 0:
                                P.dma("pool", wdt[:, 0:fc1 - fc0, :],
                                      fdw[which][l][fc0 * 128:fc1 * 128, half * 512:(half + 1) * 512].rearrange("(j p) d -> p j d", p=128),
                                      (), [wdk], "wd%d" % wdk[1])
                                P.dma("sp", wc_d[which][l][half, pi, :, 0:fc1 - fc0, :], wdt[:, 0:fc1 - fc0, :], [wdk], [ck], "wds%d" % wdk[1])
                            else:
                                P.dma("sp", wdt[:, 0:fc1 - fc0, :], wc_d[which][l][half, pi, :, 0:fc1 - fc0, :], [ck], [wdk], "wdb%d" % wdk[1])
                            for jf in range(fc1 - fc0):
                                fc = fc0 + jf
                                for dc in range(4):
                                    for si, (loff, n, j, tok) in enumerate(subs):
                                        Ob, Ok = bank(dc * 2 + si)
                                        MM(Ob[:, :n], wdt[:, jf, dc * 128:(dc + 1) * 128], gT[:, fc, loff:loff + n], fc == 0, fc == NFF - 1,
                                           [wdk, ("gT", si)], [Ok])
                        for dc in range(4):
                            for si, (loff, n, j, tok) in enumerate(subs):
                                Ob, Ok = bank(dc * 2 + si)
                                ch = half * 4 + dc
                                STT("dve", xt[:, ch, loff:loff + n], Ob[:, :n], mcol(l, i_gate, ch, j), xt[:, ch, loff:loff + n],
                                    ALU.mult, ALU.add, [Ok, xkey, "modT"], [xkey])
                    store_x(xt, xkey, bi, ti, dst)
                P.barrier()

        def make_rots(st, pfx):
            R = {
                "sq": Rot(pfx + "sq", [sbt(st, pfx + "sq%d" % i, [128, 512], BF16) for i in range(3)]),
                "tmp": Rot(pfx + "tmp", [sbt(st, pfx + "tmp%d" % i, [128, 512], F32) for i in range(3)]),
                "rstd": Rot(pfx + "rstd", [sbt(st, pfx + "rstd%d" % i, [128, 512], F32) for i in range(2)]),
            }
            return R

        def compute_hall(l, bi, hy):
            with contextlib.ExitStack() as st:
                xrot = Rot("xt", [sbt(st, "hxt%d" % i, [128, 8, 768], F32) for i in range(2)])
                T = make_rots(st, "h")
                T["ss"] = PsRot([6, 7])
                P.barrier()
                for ti in range(3):
                    xt, xkey = load_x(xrot, bi, ti, "xs")
                    for si, (loff, n, j, tok) in enumerate(tile_subs(bi, ti, True)):
                        norm_mod(T, xt, xkey, loff, n, l, j, 3, 4,
                                 (lambda c, tok=tok, n=n: hy[:, c, tok:tok + n]), "hy")
                P.barrier()

        def pipelined(n_units, s_fn, rest_fn):
            ctx = {}
            if n_units:
                ctx[0] = s_fn(0)
            for i in range(n_units):
                if i + 1 < n_units:
                    ctx[i + 1] = s_fn(i + 1)
                rest_fn(i, ctx.pop(i))

        class PsRot:
            def __init__(self, idxs):
                self.idxs = idxs
                self.i = 0

            def next(self):
                b = self.idxs[self.i % len(self.idxs)]
                self.i += 1
                return bank(b)

        def proj_fm(pr, wt, wk, col0, M, hy, tok, n):
            pb, pk = pr.next()
            for k in range(8):
                MM(pb[:M, :n], wt[:, k, col0:col0 + M], hy[:, k, tok:tok + n], k == 0, k == 7, [wk, "hy"], [pk])
            return pb, pk

        def headnorm(R, pr, src, srck, M, n, gain_col, inv_n, bd, dst, dstk, rope=None, tok=0):
            sq, sqk = R["sq"].next()
            ACT(sq[:M, :n], src, AF.Square, [srck], [sqk])
            sb_, sk_ = pr.next()
            MM(sb_[:M, :n], bd, sq[:M, :n], True, True, [sqk, "cm"], [sk_])
            rs, rsk = R["rstd"].next()
            ACT(rs[:M, :n], sb_[:M, :n], AF.Sqrt, [sk_, "cols"], [rsk], bias=EPS, scale=inv_n)
            RECIP(rs[:M, :n], rs[:M, :n], [rsk], [rsk])
            STT("dve", dst, src, gain_col, rs[:M, :n], ALU.mult, ALU.mult, [srck, rsk, "cols"], [dstk])
            if rope is not None:
                perm, rC, rS, p0, p1 = rope
                swb, swk = pr.next()
                MM(swb[:M, :n], perm, dst, True, True, [dstk, "cm"], [swk])
                t1, t1k = R["tmp"].next()
                TT("pool", t1[p0:p1, :n], dst[p0:p1, :], rC[p0:p1, tok:tok + n], ALU.mult, [dstk, "rope"], [t1k])
                t2, t2k = R["tmp"].next()
                TT("dve", t2[p0:p1, :n], swb[p0:p1, :n], rS[p0:p1, tok:tok + n], ALU.mult, [swk, "rope"], [t2k])
                TT("pool", dst[p0:p1, :], t1[p0:p1, :n], t2[p0:p1, :n], ALU.add, [t1k, t2k, swk], [dstk])

        def out_proj(l, bi, w_out_d, ychunk, with_ctx):
            with contextlib.ExitStack() as st:
                xrot = Rot("xt", [sbt(st, "oxt%d" % i, [128, 8, 768], F32) for i in range(2)])
                wo = sbt(st, "wo", [128, 8, 1024], BF16)
                P.barrier()
                for hh in range(2):
                    P.dma("pool", wo[:, :, hh * 512:(hh + 1) * 512],
                          w_out_d[:, hh * 512:(hh + 1) * 512].rearrange("(k p) f -> p k f", p=128), (), [("wo", hh)], "wo%d" % hh)
                pr = PsRot([0, 1, 2, 3, 4, 5, 6, 7])
                for ti in range(3):
                    xt, xkey = load_x(xrot, bi, ti, "xs")
                    for si, (loff, n, j, tok) in enumerate(tile_subs(bi, ti, with_ctx)):
                        for dc in range(8):
                            pb, pk = pr.next()
                            for fc in range(8):
                                MM(pb[:, :n], wo[:, fc, dc * 128:(dc + 1) * 128], ychunk(fc)[:, tok:tok + n], fc == 0, fc == 7,
                                   [("wo", dc // 4), "hy", "yD"], [pk])
                            STT("dve", xt[:, dc, loff:loff + n], pb[:, :n], mcol(l, 5, dc, j), xt[:, dc, loff:loff + n],
                                ALU.mult, ALU.add, [pk, xkey, "modT"], [xkey])
                    store_x(xt, xkey, bi, ti, "xs")
                P.barrier()

        def mixer_ab(bi):
            l = 0
            with contextlib.ExitStack() as st0:
                hy = sbt(st0, "hy", [128, 8, TOK], BF16)
                compute_hall(l, bi, hy)
                with contextlib.ExitStack() as st:
                    aq = sbt(st, "aq", [128, 4, TOK], BF16)
                    bq = sbt(st, "bq", [128, 4, TOK], BF16)
                    bk = sbt(st, "bk", [128, 4, TOK], BF16)
                    bv = sbt(st, "bv", [128, 18, 512], BF16)
                    akd = sbt(st, "akd", [128, 2, TOK], BF16)
                    av = sbt(st, "av", [128, 18, 128], BF16)
                    rC = sbt(st, "rC", [128, NX], F32)
                    rS = sbt(st, "rS", [128, NX], F32)
                    maskA = sbt(st, "maskA", [128, 6, 512], BF16)
                    wr = Rot("w", [sbt(st, "wr%d" % i, [128, 8, 512], BF16) for i in range(2)])
                    R = make_rots(st, "m")
                    prot = Rot("P", [sbt(st, "P%d" % i, [128, 512], BF16) for i in range(3)])
                    rdrot = Rot("rd", [sbt(st, "rd%d" % i, [128, 512], F32) for i in range(2)])
                    ta = sbt(st, "ta", [128, 512], F32)
                    tb = sbt(st, "tb", [128, 512], F32)
                    yv = sbt(st, "yv", [128, 512], F32)
                    P.barrier()
                    P.dma("sp", rC[:], rope64_d[0], (), ["rope"], "rC")
                    P.dma("sp", rS[:], rope64_d[1], (), ["rope"], "rS")
                    P.dma("pool", maskA[:], maskA_d, (), ["maskA"], "maskA")
                    pr = PsRot([0, 1, 2, 3, 4, 5, 6, 7])
                    rope64 = (cm[:, M_PERM64, :], rC, rS, 0, 128)
                    bd64 = cm[:, M_BD64, :]

                    def load_w(src_ap, ncols, dst_off=0, slot=None):
                        if slot is None:
                            slot = wr.next()
                        wt, wk = slot
                        P.dma("pool", wt[:, :, dst_off:dst_off + ncols], src_ap.rearrange("(k p) f -> p k f", p=128), (), [wk], "w%d" % wk[1])
                        return slot

                    def qk_group(col_base, gain_c, dst, dname, subs_list):
                        wt, wk = load_w(ab_w_in[:, col_base:col_base + 512], 512)
                        for (tok, n) in subs_list:
                            for c in range(4):
                                pb, pk = proj_fm(pr, wt, wk, c * 128, 128, hy, tok, n)
                                headnorm(R, pr, pb[:, :n], pk, 128, n, cols[:, gain_c:gain_c + 1], 1.0 / 64, bd64,
                                         dst[:, c, tok:tok + n], (dname, c), rope=(rope64 if tok < NX else None), tok=tok)

                    qk_group(0, C_AQ, aq, "aq", SUBS5)
                    qk_group(512, C_BQ, bq, "bq", SUBS5)
                    slot = wr.next()
                    for g in range(2):
                        for d in range(2):
                            load_w(ab_w_in[:, 1024 + g * 64:1024 + (g + 1) * 64], 64, dst_off=g * 128 + d * 64, slot=slot)
                    load_w(ab_w_in[:, 1152:1280], 128, dst_off=256, slot=slot)
                    wt, wk = slot
                    for (tok, n) in SUBS5:
                        for g in range(2):
                            pb, pk = proj_fm(pr, wt, wk, g * 128, 128, hy, tok, n)
                            headnorm(R, pr, pb[:, :n], pk, 128, n, cols[:, C_AK:C_AK + 1], 1.0 / 64, bd64,
                                     akd[:, g, tok:tok + n], ("akd", g), rope=(rope64 if tok < NX else None), tok=tok)
                    for tt in range(18):
                        pb, pk = pr.next()
                        for k in range(8):
                            MM(pb[:, 0:128], hy[:, k, tt * 128:(tt + 1) * 128], wt[:, k, 256:384], k == 0, k == 7, [wk, "hy"], [pk])
                        COPY("dve", av[:, tt, :], pb[:, 0:128], [pk], ["av"])
                    qk_group(1280, C_BK, bk, "bk", SUBS5)
                    wt, wk = load_w(ab_w_in[:, 1792:2304], 512)
                    for tt in range(18):
                        pb, pk = pr.next()
                        for k in range(8):
                            MM(pb[:, :], hy[:, k, tt * 128:(tt + 1) * 128], wt[:, k, 0:512], k == 0, k == 7, [wk, "hy"], [pk])
                        ACT(bv[:, tt, :], pb[:, :], AF.Copy, [pk], ["bv"])

                    srot = PsRot([0, 1])
                    orot = PsRot([2, 4])
                    drot = PsRot([3, 5])
                    groupsA = []
                    for qg in range(4):
                        n0 = qg * 4
                        kts = [(kt, kt - n0 + 1) for kt in range(max(0, n0 - 1), min(15, n0 + 4) + 1)] + [(16, None), (17, None)]
                        groupsA.append((qg * 512, 512, kts))
                    groupsA.append((NX, NCTX, [(16, None), (17, None)]))
                    for h in range(8):
                        g, c, p0 = h // 4, h // 2, (h % 2) * 64
                        for (qoff, nq, kts) in groupsA:
                            Ob, Ok = orot.next()
                            Db, Dk = drot.next()
                            def s_fn(i):
                                kt, mi = kts[i]
                                Sb, Sk = srot.next()
                                MM(Sb[:, :nq], akd[p0:p0 + 64, g, kt * 128:(kt + 1) * 128], aq[p0:p0 + 64, c, qoff:qoff + nq], True, True,
                                   [("akd", g), ("aq", c)], [Sk])
                                return Sb, Sk

                            def rest_fn(i, sctx):
                                Sb, Sk = sctx
                                kt, mi = kts[i]
                                Pt, Pk = prot.next()
                                ACT(Pt[:, :nq], Sb[:, :nq], AF.Exp, [Sk], [Pk], scale=0.125)
                                if mi is not None:
                                    TT("dve", Pt[:, :nq], Pt[:, :nq], maskA[:, mi, :nq], ALU.mult, [Pk, "maskA"], [Pk])
                                MM(Ob[0:64, :nq], av[:, kt, g * 64:(g + 1) * 64], Pt[:, :nq], i == 0, i == len(kts) - 1, ["av", Pk], [Ok])
                                MM(Db[0:64, :nq], cm[:, M_ONES, 0:64], Pt[:, :nq], i == 0, i == len(kts) - 1, ["cm", Pk], [Dk])
                            pipelined(len(kts), s_fn, rest_fn)
                            rd, rdk = rdrot.next()
                            TS("dve", rd[0:64, :nq], Db[0:64, :nq], esink[0:64, h:h + 1], ALU.add, [Dk, "esink"], [rdk])
                            RECIP(rd[0:64, :nq], rd[0:64, :nq], [rdk], [rdk])
                            TT("dve", hy[p0:p0 + 64, c, qoff:qoff + nq], Ob[0:64, :nq], rd[0:64, :nq], ALU.mult, [Ok, rdk], ["hy"])

                    allk = [16, 17] + list(range(16))
                    groupsB = [(qg * 512, 512, allk) for qg in range(4)] + [(NX, NCTX, [16, 17])]
                    yv2 = sbt(st, "yv2", [128, 512], F32)
                    yvrot = Rot("yv", [yv, yv2])
                    pend = []
                    for h in range(4):
                        for (qoff, nq, kts) in groupsB:
                            for m in range(2):
                                Ob, Ok = bank(2 + 2 * m)
                                Db, Dk = bank(3 + 2 * m)

                                def s_fn(i):
                                    kt = kts[i]
                                    Sb, Sk = srot.next()
                                    MM(Sb[:, :nq], bk[m * 64:(m + 1) * 64, h, kt * 128:(kt + 1) * 128], bq[m * 64:(m + 1) * 64, h, qoff:qoff + nq],
                                       True, True, [("bk", h), ("bq", h)], [Sk])
                                    return Sb, Sk

                                def rest_fn(i, sctx):
                                    Sb, Sk = sctx
                                    kt = kts[i]
                                    Pt, Pk = prot.next()
                                    ACT(Pt[:, :nq], Sb[:, :nq], AF.Exp, [Sk], [Pk], scale=0.125)
                                    MM(Ob[:, :nq], bv[:, kt, h * 128:(h + 1) * 128], Pt[:, :nq], i == 0, i == len(kts) - 1, ["bv", Pk], [Ok])
                                    MM(Db[:, :nq], cm[:, M_ONES, :], Pt[:, :nq], i == 0, i == len(kts) - 1, ["cm", Pk], [Dk])
                                pipelined(len(kts), s_fn, rest_fn)
                                if m == 0 and pend:
                                    pend.pop()()
                            r1, r1k = rdrot.next()
                            RECIP(r1[:, :nq], psb[3][:, :nq], [("ps", 3)], [r1k])
                            r2, r2k = rdrot.next()
                            RECIP(r2[:, :nq], psb[5][:, :nq], [("ps", 5)], [r2k])
                            TT("dve", ta[:, :nq], psb[2][:, :nq], r1[:, :nq], ALU.mult, [("ps", 2), r1k], ["ta"])
                            TT("dve", tb[:, :nq], psb[4][:, :nq], r2[:, :nq], ALU.mult, [("ps", 4), r2k], ["tb"])
                            yvt, yvk = yvrot.next()
                            STT("dve", yvt[:, :nq], tb[:, :nq], lamt[:, 4:5], ta[:, :nq], ALU.mult, ALU.add, ["ta", "tb", "lamt"], [yvk])

                            def part2(yvt=yvt, yvk=yvk, h=h, qoff=qoff, nq=nq):
                                headnorm(R, PsRot([6, 7]), yvt[:, :nq], yvk, 128, nq, lamt[:, 5:6], 1.0 / 128, cm[:, M_ONES, :],
                                         hy[:, 4 + h, qoff:qoff + nq], "hy")
                            pend.append(part2)
                    while pend:
                        pend.pop()()
                    P.barrier()
                out_proj(l, bi, ab_w_out, (lambda fc: hy[:, fc, :]), True)

        def mixer_cd(bi):
            l = 1
            with contextlib.ExitStack() as st0:
                hy = sbt(st0, "hy1", [128, 8, TOK], BF16)
                yD = sbt(st0, "yD", [128, 4, NX], BF16)
                compute_hall(l, bi, hy)
                srot = PsRot([0, 1])
                with contextlib.ExitStack() as st:
                    dq = sbt(st, "dq", [128, 4, NX], BF16)
                    dk = sbt(st, "dk", [128, 4, TOK], BF16)
                    vD = sbt(st, "vD", [128, 33, 512], BF16)
                    E2 = sbt(st, "E2", [128, 14, 512], BF16)
                    eg = sbt(st, "eg", [128, 2, 512], F32)
                    wr = Rot("w", [sbt(st, "dwr%d" % i, [128, 8, 512], BF16) for i in range(2)])
                    R = make_rots(st, "d")
                    prot = Rot("P", [sbt(st, "dP%d" % i, [128, 512], BF16) for i in range(3)])
                    rdrot = Rot("rd", [sbt(st, "drd%d" % i, [128, 512], F32) for i in range(2)])
                    P.barrier()
                    pr = PsRot([0, 1, 2, 3, 4, 5, 6, 7])
                    bd64 = cm[:, M_BD64, :]
                    for i in range(14):
                        s = i % 2
                        P.dma("sp", eg[:, s, :], rpbG_d[:, i, :], (), [("eg", s)], "eg%d" % s)
                        ACT(E2[:, i, :], eg[:, s, :], AF.Exp, [("eg", s)], ["E2"])

                    def load_w(src_ap, ncols):
                        wt, wk = wr.next()
                        P.dma("pool", wt[:, :, 0:ncols], src_ap.rearrange("(k p) f -> p k f", p=128), (), [wk], "w%d" % wk[1])
                        return wt, wk

                    wt, wk = load_w(cd_w_in[:, 768:1280], 512)
                    for (tok, n) in SUBS5[:4]:
                        for c in range(4):
                            pb, pk = proj_fm(pr, wt, wk, c * 128, 128, hy, tok, n)
                            headnorm(R, pr, pb[:, :n], pk, 128, n, cols[:, C_DQ:C_DQ + 1], 1.0 / 64, bd64, dq[:, c, tok:tok + n], ("dq", c))
                    wt, wk = load_w(cd_w_in[:, 1568:2080], 512)
                    for (tok, n) in SUBS5:
                        for c in range(4):
                            pb, pk = proj_fm(pr, wt, wk, c * 128, 128, hy, tok, n)
                            headnorm(R, pr, pb[:, :n], pk, 128, n, cols[:, C_DK:C_DK + 1], 1.0 / 64, bd64, dk[:, c, tok:tok + n], ("dk", c))
                    wt, wk = load_w(cd_w_in[:, 2080:2592], 512)

                    def wtok(w):
                        return 64 * w if w < 31 else NX + 128 * (w - 31)
                    for w in range(33):
                        pb, pk = pr.next()
                        t0 = wtok(w)
                        for k in range(8):
                            MM(pb[:, :], hy[:, k, t0:t0 + 128], wt[:, k, 0:512], k == 0, k == 7, [wk, "hy"], [pk])
                        if w % 2 == 0:
                            ACT(vD[:, w, :], pb[:, :], AF.Copy, [pk], ["vD"])
                        else:
                            COPY("dve", vD[:, w, :], pb[:, :], [pk], ["vD"])
                    orot = PsRot([2, 4])
                    drot = PsRot([3, 5])
                    srotE = PsRot([0, 6])
                    srotO = PsRot([1, 7])
                    for r in range(0 if (KSUB & 1) else 32):
                        rs_ = min(max(r - 4, 0), 24)
                        units = [(rs_ + 2 * i, rs_ + 2 * i - r + 7) for i in range(4)] + [(31, None), (32, None)]
                        Ob, Ok = orot.next()
                        Db, Dk = drot.next()
                        def s_fn(ui):
                            w, ei = units[ui]
                            t0 = wtok(w)
                            SbE, SkE = srotE.next()
                            SbO, SkO = srotO.next()
                            for h in range(8):
                                c, p0 = h // 2, (h % 2) * 64
                                Sb, Sk = (SbE, SkE) if h % 2 == 0 else (SbO, SkO)
                                MM(Sb[:, c * 64:(c + 1) * 64], dk[p0:p0 + 64, c, t0:t0 + 128], dq[p0:p0 + 64, c, r * 64:(r + 1) * 64],
                                   c == 0, c == 3, [("dk", c), ("dq", c)], [Sk], skip_group_check=True)
                            return SbE, SkE, SbO, SkO

                        def rest_fn(ui, sctx):
                            SbE, SkE, SbO, SkO = sctx
                            w, ei = units[ui]
                            Pt, Pk = prot.next()
                            PtV = Pt[:, :].rearrange("p (c two q) -> p c two q", two=2, q=64)
                            ACT(PtV[:, :, 0, :], SbE[:, 0:256].rearrange("p (c q) -> p c q", q=64), AF.Exp, [SkE], [Pk], scale=0.125)
                            ACT(PtV[:, :, 1, :], SbO[:, 0:256].rearrange("p (c q) -> p c q", q=64), AF.Exp, [SkO, Pk], [Pk], scale=0.125)
                            if ei is not None:
                                TT("dve", Pt[:, :], Pt[:, :], E2[:, ei, :], ALU.mult, [Pk, "E2"], [Pk])
                            for h in range(8):
                                MM(Ob[0:64, h * 64:(h + 1) * 64], vD[:, w, h * 64:(h + 1) * 64], Pt[:, h * 64:(h + 1) * 64],
                                   ui == 0 and h == 0, ui == len(units) - 1 and h == 7, ["vD", Pk], [Ok], skip_group_check=True)
                            MM(Db[0:64, :], cm[:, M_ONES, 0:64], Pt[:, :], ui == 0, ui == len(units) - 1, ["cm", Pk], [Dk])
                        pipelined(len(units), s_fn, rest_fn)
                        rd, rdk = rdrot.next()
                        RECIP(rd[0:64, :], Db[0:64, :], [Dk], [rdk])
                        for h in range(8):
                            c, p0 = h // 2, (h % 2) * 64
                            TT("dve", yD[p0:p0 + 64, c, r * 64:(r + 1) * 64], Ob[0:64, h * 64:(h + 1) * 64], rd[0:64, h * 64:(h + 1) * 64],
                               ALU.mult, [Ok, rdk], ["yD"])
                    P.barrier()
                with contextlib.ExitStack() as st:
                    cqg = sbt(st, "cqg", [128, 6, NX], BF16)
                    rsa = sbt(st, "rsa", [128, NX], F32)
                    ckvn = sbt(st, "ckvn", [128, 2, TOK], BF16)
                    krope = sbt(st, "krope", [32, TOK], BF16)
                    rC = sbt(st, "rCm", [128, NX], F32)
                    rS = sbt(st, "rSm", [128, NX], F32)
                    vC = sbt(st, "vC", [128, 18, 512], BF16)
                    qcat = sbt(st, "qcat", [96, NX], BF16)
                    kcat = sbt(st, "kcat", [96, TOK], BF16)
                    wr = Rot("w", [sbt(st, "cwr%d" % i, [128, 8, 512], BF16) for i in range(2)])
                    wqr = Rot("wq", [sbt(st, "wq%d" % i, [128, 6, 96], BF16) for i in range(2)])
                    wkr = Rot("wk", [sbt(st, "wk%d" % i, [128, 2, 64], BF16) for i in range(2)])
                    R = make_rots(st, "c")
                    prot = Rot("P", [sbt(st, "cP%d" % i, [128, 512], BF16) for i in range(3)])
                    rdrot = Rot("rd", [sbt(st, "crd%d" % i, [128, 512], F32) for i in range(2)])
                    P.barrier()
                    P.dma("sp", rC[:], ropeM_d[0], (), ["rope"], "rC")
                    P.dma("sp", rS[:], ropeM_d[1], (), ["rope"], "rS")
                    pr = PsRot([2, 3, 4, 5, 6, 7])

                    def load_w(src_ap, ncols):
                        wt, wk = wr.next()
                        P.dma("pool", wt[:, :, 0:ncols], src_ap.rearrange("(k p) f -> p k f", p=128), (), [wk], "w%d" % wk[1])
                        return wt, wk

                    wa, wak = load_w(cd_w_in[:, 0:512], 512)
                    wb, wbk = load_w(cd_w_in[:, 512:768], 256)
                    for (tok, n) in SUBS5[:4]:
                        ssb, ssk = bank(0 + (tok // 512) % 2)
                        for c in range(6):
                            wt, wk, cc = (wa, wak, c) if c < 4 else (wb, wbk, c - 4)
                            pb, pk = proj_fm(pr, wt, wk, cc * 128, 128, hy, tok, n)
                            sq, sqk = R["sq"].next()
                            ACT(sq[:, :n], pb[:, :n], AF.Square, [pk], [sqk])
                            MM(ssb[:, :n], cm[:, M_ONES, :], sq[:, :n], c == 0, c == 5, [sqk, "cm"], [ssk])
                            TS("dve", cqg[:, c, tok:tok + n], pb[:, :n], cols[:, C_QA + c:C_QA + c + 1], ALU.mult, [pk, "cols", sqk], ["cqg"])
                        ACT(rsa[:, tok:tok + n], ssb[:, :n], AF.Sqrt, [ssk], ["rsa"], bias=EPS, scale=1.0 / 768)
                        RECIP(rsa[:, tok:tok + n], rsa[:, tok:tok + n], ["rsa"], ["rsa"])
                    wt, wk = load_w(cd_w_in[:, 1280:1568], 288)
                    rope32 = (cm[0:32, M_PERM32, 0:32], rC, rS, 0, 32)
                    for (tok, n) in SUBS5:
                        ssb, ssk = bank(0 + (tok // 512) % 2)
                        pbs = []
                        for c in range(2):
                            pb, pk = proj_fm(pr, wt, wk, c * 128, 128, hy, tok, n)
                            pbs.append((pb, pk))
                            sq, sqk = R["sq"].next()
                            ACT(sq[:, :n], pb[:, :n], AF.Square, [pk], [sqk])
                            MM(ssb[:, :n], cm[:, M_ONES, :], sq[:, :n], c == 0, c == 1, [sqk, "cm"], [ssk])
                        rs, rsk = R["rstd"].next()
                        ACT(rs[:, :n], ssb[:, :n], AF.Sqrt, [ssk], [rsk], bias=EPS, scale=1.0 / 256)
                        RECIP(rs[:, :n], rs[:, :n], [rsk], [rsk])
                        for c in range(2):
                            pb, pk = pbs[c]
                            STT("dve", ckvn[:, c, tok:tok + n], pb[:, :n], cols[:, C_KVA + c:C_KVA + c + 1], rs[:, :n], ALU.mult, ALU.mult,
                                [pk, rsk, "cols"], ["ckvn"])
                        pb, pk = proj_fm(pr, wt, wk, 256, 32, hy, tok, n)
                        headnorm(R, pr, pb[0:32, :n], pk, 32, n, cols[0:32, C_KR:C_KR + 1], 1.0 / 32, cm[0:32, M_ONES, 0:32],
                                 krope[0:32, tok:tok + n], "krope", rope=(rope32 if tok < NX else None), tok=tok)
                    wt, wk = wr.next()
                    for kk in range(2):
                        P.dma("pool", wt[:, kk, 0:512].rearrange("p (h e) -> p h e", e=64),
                              c_w_ukv[kk * 128:(kk + 1) * 128, :].rearrange("p (h e) -> p h e", e=128)[:, :, 64:128], (), [wk], "w%d" % wk[1])
                    for tt in range(18):
                        pb, pk = pr.next()
                        for c in range(2):
                            MM(pb[:, :], ckvn[:, c, tt * 128:(tt + 1) * 128], wt[:, c, 0:512], c == 0, c == 1, [wk, "ckvn"], [pk])
                        ACT(vC[:, tt, :], pb[:, :], AF.Copy, [pk], ["vC"])
                    orot = PsRot([2, 4])
                    drot = PsRot([3, 5])
                    pr2 = PsRot([6, 7])
                    allk = [16, 17] + list(range(16))
                    sc = 96.0 ** -0.5
                    for h in range(0 if (KSUB & 2) else 8):
                        c_, p0 = h // 2, (h % 2) * 64
                        wq, wqk = wqr.next()
                        P.dma("pool", wq[:], c_w_uq[:, h * 96:(h + 1) * 96].rearrange("(k p) f -> p k f", p=128), (), [wqk], "wq%d" % wqk[1])
                        wkk, wkkk = wkr.next()
                        P.dma("pool", wkk[:], c_w_ukv[:, h * 128:h * 128 + 64].rearrange("(k p) f -> p k f", p=128), (), [wkkk], "wk%d" % wkkk[1])
                        for (tok, n) in SUBS5[:4]:
                            pb, pk = pr2.next()
                            for c in range(6):
                                MM(pb[0:96, :n], wq[:, c, :], cqg[:, c, tok:tok + n], c == 0, c == 5, [wqk, "cqg"], [pk])
                            qs, qsk = R["tmp"].next()
                            TT("dve", qs[0:96, :n], pb[0:96, :n], rsa[0:96, tok:tok + n], ALU.mult, [pk, "rsa"], [qsk])
                            headnorm(R, pr2, qs[0:96, :n], qsk, 96, n, cols[0:96, C_Q96:C_Q96 + 1], cols[0:96, C_INV96:C_INV96 + 1],
                                     cm[0:96, M_BD96, 0:96], qcat[0:96, tok:tok + n], "qcat",
                                     rope=(cm[0:96, M_PERM96, 0:96], rC, rS, 64, 96), tok=tok)
                        for (tok, n) in SUBS5:
                            pb, pk = pr2.next()
                            for c in range(2):
                                MM(pb[0:64, :n], wkk[:, c, :], ckvn[:, c, tok:tok + n], c == 0, c == 1, [wkkk, "ckvn"], [pk])
                            headnorm(R, pr2, pb[0:64, :n], pk, 64, n, cols[0:64, C_KN:C_KN + 1], 1.0 / 64, cm[0:64, M_ONES, 0:64],
                                     kcat[0:64, tok:tok + n], "kcat")
                        P.dma("sp", kcat[64:96, :], krope[0:32, :], ["krope"], ["kcat"], "kcr")
                        for qg in range(4):
                            qoff = qg * 512
                            Ob, Ok = orot.next()
                            Db, Dk = drot.next()
                            def s_fn(i):
                                kt = allk[i]
                                Sb, Sk = srot.next()
                                MM(Sb[:, :], kcat[0:96, kt * 128:(kt + 1) * 128], qcat[0:96, qoff:qoff + 512], True, True, ["kcat", "qcat"], [Sk])
                                return Sb, Sk

                            def rest_fn(i, sctx):
                                Sb, Sk = sctx
                                kt = allk[i]
                                Pt, Pk = prot.next()
                                ACT(Pt[:, :], Sb[:, :], AF.Exp, [Sk], [Pk], scale=sc)
                                MM(Ob[0:64, :], vC[:, kt, h * 64:(h + 1) * 64], Pt[:, :], i == 0, i == 17, ["vC", Pk], [Ok])
                                MM(Db[0:64, :], cm[:, M_ONES, 0:64], Pt[:, :], i == 0, i == 17, ["cm", Pk], [Dk])
                            pipelined(18, s_fn, rest_fn)
                            rd, rdk = rdrot.next()
                            RECIP(rd[0:64, :], Db[0:64, :], [Dk], [rdk])
                            TT("dve", hy[p0:p0 + 64, c_, qoff:qoff + 512], Ob[0:64, :], rd[0:64, :], ALU.mult, [Ok, rdk], ["hy"])
                    P.barrier()
                out_proj(l, bi, cd_w_out, (lambda fc: hy[:, fc, :] if fc < 4 else yD[:, fc - 4, :]), False)

        for bi in range(2):
            ffn(0, 0, bi, "in", "xs" if stage > 1 else "out", True)
            if stage <= 1:
                continue
            mixer_ab(bi)
            if stage <= 2:
                _dump(P, nc, top, xs, outT, bi, final_ops)
                continue
            ffn(0, 1, bi, "xs", "xs" if stage > 3 else "out", True)
            if stage <= 3:
                continue
            ffn(1, 0, bi, "xs", "xs" if stage > 4 else "out", True)
            if stage <= 4:
                continue
            mixer_cd(bi)
            if stage <= 5:
                _dump(P, nc, top, xs, outT, bi, final_ops)
                continue
            ffn(1, 1, bi, "xs", "out", False)
        P.emit(final_ops)
    return nc


def _dump(P, nc, top, xs, outT, bi, final_ops):
    P.barrier()
    op = P.dma("sp", outT[bi], xs[:, :, 0:NX], [("xs", 0), ("xs", 1), ("xs", 2)], [("out", bi, "d")], "dump")
    final_ops.append(op)
    P.barrier()


def _rope_tables(rot_dim):
    nf = rot_dim // 4
    inv = (10000.0 ** (-np.arange(nf, dtype=np.float64) / nf))
    t = np.arange(NX)
    row = (t // 64).astype(np.float64)
    col = (t % 64).astype(np.float64)
    C = np.zeros((rot_dim, NX), np.float64)
    S = np.zeros((rot_dim, NX), np.float64)
    for d in range(rot_dim):
        axis = d // (2 * nf)
        half = (d % (2 * nf)) // nf
        f = d % nf
        ang = (row if axis == 0 else col) * inv[f]
        C[d] = np.cos(ang)
        S[d] = np.sin(ang) * (-1.0 if half == 0 else 1.0)
    return C.astype(np.float32), S.astype(np.float32)


def _perm(rot_dim):
    nf = rot_dim // 4
    Pm = np.zeros((rot_dim, rot_dim), np.float32)
    for d in range(rot_dim):
        half = (d % (2 * nf)) // nf
        partner = d + nf if half == 0 else d - nf
        Pm[partner, d] = 1.0
    return Pm


def _constants():
    cmat = np.zeros((128, 6, 128), np.float32)
    cmat[:, M_ONES, :] = 1.0
    cmat[0:64, M_BD64, 0:64] = 1.0
    cmat[64:128, M_BD64, 64:128] = 1.0
    p64 = _perm(64)
    cmat[0:64, M_PERM64, 0:64] = p64
    cmat[64:128, M_PERM64, 64:128] = p64
    cmat[0:64, M_BD96, 0:64] = 1.0
    cmat[64:96, M_BD96, 64:96] = 1.0
    p32 = _perm(32)
    cmat[0:64, M_PERM96, 0:64] = np.eye(64, dtype=np.float32)
    cmat[64:96, M_PERM96, 64:96] = p32
    cmat[0:32, M_PERM32, 0:32] = p32
    C64, S64 = _rope_tables(64)
    rope64 = np.stack([np.concatenate([C64, C64]), np.concatenate([S64, S64])]).astype(np.float32)
    C32, S32 = _rope_tables(32)
    ropeM = np.zeros((2, 128, NX), np.float32)
    ropeM[0] = 1.0
    ropeM[0, 0:32] = C32
    ropeM[1, 0:32] = S32
    ropeM[0, 64:96] = C32
    ropeM[1, 64:96] = S32
    maskA = np.zeros((128, 6, 512), np.float32)
    jj = np.arange(128)[:, None]
    qq = np.arange(512)[None, :]
    b = qq // 128
    ii = qq % 128
    for mi in range(6):
        d = mi - 1 - b
        maskA[:, mi, :] = (np.abs(ii - jj - 128 * d) <= 128).astype(np.float32)
    return cmat, rope64, ropeM, maskA


def _rpb_gather(rpb):
    kc = np.arange(64)[:, None]
    qc = np.arange(64)[None, :]
    dc = np.clip(kc - qc, -15, 15) + 15
    cs = np.clip(qc - 8, 0, 48)
    valid = (kc >= cs) & (kc < cs + 16)
    G = np.empty((2, 64, 14, 8, 64), np.float32)
    for rr in range(2):
        for i in range(14):
            g = rpb[:, i + rr, :][:, dc]
            g = np.where(valid[None], g, np.float32(-100.0))
            G[rr, :, i, :, :] = np.transpose(g, (1, 0, 2))
    return np.ascontiguousarray(G.reshape(128, 14, 512))


def _prep(inp):
    f = lambda a: np.ascontiguousarray(np.asarray(a, dtype=np.float32))
    cmat, rope64, ropeM, maskA = _constants()
    shared = {
        "w_mod": f(inp["w_mod"]),
        "b_modT": f(np.transpose(np.asarray(inp["b_mod"]).reshape(2, 72, 128), (2, 0, 1))),
        "f1g": f(inp["ffn1_w_gate"]), "f1u": f(inp["ffn1_w_up"]), "f1d": f(inp["ffn1_w_down"]),
        "f2g": f(inp["ffn2_w_gate"]), "f2u": f(inp["ffn2_w_up"]), "f2d": f(inp["ffn2_w_down"]),
        "ab_w_in": f(inp["ab_w_in"][0]), "ab_w_out": f(inp["ab_w_out"][0]),
        "cd_w_in": f(inp["cd_w_in"][0]), "cd_w_out": f(inp["cd_w_out"][0]),
        "c_w_uq": f(inp["c_w_uq"][0]), "c_w_ukv": f(inp["c_w_ukv"][0]),
        "cmat": cmat, "rope64": rope64, "ropeM": ropeM, "maskA": maskA,
        "rpbG": _rpb_gather(np.asarray(inp["d_rpb"][0], dtype=np.float32)),
    }
    cols = np.zeros((128, NCOL), np.float32)
    g = lambda k: np.asarray(inp[k][0], dtype=np.float32)
    cols[:, C_AQ] = np.tile(g("a_q_norm"), 2)
    cols[:, C_AK] = np.tile(g("a_k_norm"), 2)
    cols[:, C_BQ] = np.tile(g("b_q_norm"), 2)
    cols[:, C_BK] = np.tile(g("b_k_norm"), 2)
    cols[:, C_BSUB] = g("b_sub_norm")
    cols[:, C_SINK:C_SINK + 8] = g("a_sink")[None, :]
    cols[0:64, C_LQ1] = g("b_lambda_q1")
    cols[0:64, C_LK1] = g("b_lambda_k1")
    cols[0:64, C_LQ2] = g("b_lambda_q2")
    cols[0:64, C_LK2] = g("b_lambda_k2")
    cols[:, C_QA:C_QA + 6] = g("c_q_a_norm").reshape(6, 128).T
    cols[:, C_KVA:C_KVA + 2] = g("c_kv_a_norm").reshape(2, 128).T
    cols[:, C_Q96] = 1.0
    cols[0:64, C_Q96] = g("c_q_nope_norm")
    cols[64:96, C_Q96] = g("c_q_rope_norm")
    cols[:, C_INV96] = 1.0 / 64
    cols[64:96, C_INV96] = 1.0 / 32
    cols[:, C_KN] = np.tile(g("c_k_nope_norm"), 2)
    cols[:, C_KR] = np.tile(g("c_k_rope_norm"), 4)
    cols[:, C_DQ] = np.tile(g("d_q_norm"), 2)
    cols[:, C_DK] = np.tile(g("d_k_norm"), 2)
    shared["cols"] = cols
    x = np.asarray(inp["x"], dtype=np.float32)
    ctx = np.asarray(inp["ctx"], dtype=np.float32)
    c = np.asarray(inp["c"], dtype=np.float32)
    c_ctx = np.asarray(inp["c_ctx"], dtype=np.float32)
    in_maps = []
    for core in range(8):
        b0 = 2 * core
        m = dict(shared)
        m["xT"] = np.ascontiguousarray(np.transpose(x[b0:b0 + 2], (0, 2, 1)).reshape(2, 8, 128, NX))
        m["ctxT"] = np.ascontiguousarray(np.transpose(ctx[b0:b0 + 2], (0, 2, 1)).reshape(2, 8, 128, NCTX))
        cc = np.stack([c[b0], c[b0 + 1], c_ctx], axis=-1)
        m["cT"] = np.ascontiguousarray(np.transpose(cc.reshape(8, 128, 3), (1, 0, 2)))
        in_maps.append(m)
    return in_maps


def kernel(**inputs):
    stage = int(os.environ.get("KSTAGE", "99"))
    ncores = int(os.environ.get("KCORES", "8"))
    in_maps = _prep(inputs)
    nc = build(stage)
    res = run_bass_kernel_spmd(nc, in_maps[:ncores], core_ids=list(range(ncores)))
    out = np.zeros((16, NX, 1024), np.float32)
    for core in range(ncores):
        o = np.asarray(res.results[core]["outT"]).reshape(2, 1024, NX)
        out[2 * core:2 * core + 2] = np.transpose(o, (0, 2, 1))
    return out
```
